# Optimizing a Trainium2 kernel written in Bass

```python
import math
import jax
import jax.numpy as jnp
from jax import lax
import numpy as np

D_MODEL = 1024
BATCH = 8
SEQ = 4096
DEPTH = 4

HEAD_DIM = 64
MIX_WIDTH = D_MODEL
N_HEADS = MIX_WIDTH // HEAD_DIM
N_DIL_HEADS = N_HEADS // 4
N_DIFF_HEADS = (N_HEADS - N_DIL_HEADS) // 2
N_FOX_HEADS = N_HEADS - N_DIL_HEADS - N_DIFF_HEADS
DIFF_QK_DIM = HEAD_DIM // 2
D_FF = ((8 * D_MODEL // 3 + 255) // 256) * 256
DILATED_PATTERNS = ((128, 1), (512, 4), (2048, 16))
Q_BLOCK = 128
ROPE_THETA = 10000.0
NORM_EPS = 1e-6
SUBLN_EPS = 1e-5
MACARON_SCALE = 0.5
NEG_INF = -1e30

SPLIT_SIZES = (
    N_DIFF_HEADS * 2 * DIFF_QK_DIM,
    N_DIFF_HEADS * 2 * DIFF_QK_DIM,
    N_DIFF_HEADS * HEAD_DIM,
    N_FOX_HEADS * HEAD_DIM,
    N_FOX_HEADS * HEAD_DIM,
    N_FOX_HEADS * HEAD_DIM,
    N_FOX_HEADS,
    N_DIL_HEADS * HEAD_DIM,
    N_DIL_HEADS * HEAD_DIM,
    N_DIL_HEADS * HEAD_DIM,
)
IN_WIDTH = sum(SPLIT_SIZES)
SPLIT_POINTS = tuple(int(v) for v in np.cumsum(SPLIT_SIZES)[:-1])

kernel_name = 'hybrid_diff_fox_dilated_macaron'


def rms_norm(x, g, eps=NORM_EPS):
    xf = x.astype(jnp.float32)
    y = xf * lax.rsqrt(jnp.mean(xf * xf, axis=-1, keepdims=True) + eps)
    return (y * g.astype(jnp.float32)).astype(x.dtype)


def rope(x, pos):
    half = x.shape[-1] // 2
    inv_freq = ROPE_THETA ** (-jnp.arange(half, dtype=jnp.float32) / half)
    ang = pos.astype(jnp.float32)[:, None] * inv_freq[None, :]
    cos = jnp.cos(ang)[None, :, None, :]
    sin = jnp.sin(ang)[None, :, None, :]
    xf = x.astype(jnp.float32)
    x1, x2 = xf[..., :half], xf[..., half:]
    return jnp.concatenate([x1 * cos - x2 * sin, x2 * cos + x1 * sin], axis=-1).astype(x.dtype)


def to_blocks(x):
    b, h, s = x.shape[:3]
    x = x.reshape((b, h, s // Q_BLOCK, Q_BLOCK) + x.shape[3:])
    return jnp.moveaxis(x, 2, 0)


def from_blocks(x):
    nb, b, h, blk, d = x.shape
    return jnp.moveaxis(x, 0, 2).reshape(b, h, nb * blk, d).transpose(0, 2, 1, 3)


def causal_mask(qpos, s):
    return qpos[:, None] >= jnp.arange(s)[None, :]


def differential_attention(q, k, v, lam_q1, lam_k1, lam_q2, lam_k2, g_sub, lam_init, pos):
    b, s, h, _ = v.shape
    q = rope(q.reshape(b, s, 2 * h, DIFF_QK_DIM), pos).reshape(b, s, h, 2, DIFF_QK_DIM)
    k = rope(k.reshape(b, s, 2 * h, DIFF_QK_DIM), pos).reshape(b, s, h, 2, DIFF_QK_DIM)
    q = q.transpose(3, 0, 2, 1, 4)
    k = k.transpose(3, 0, 2, 1, 4)
    vt = v.transpose(0, 2, 1, 3)
    f32 = jnp.float32
    lam = (jnp.exp(jnp.sum(lam_q1.astype(f32) * lam_k1.astype(f32)))
           - jnp.exp(jnp.sum(lam_q2.astype(f32) * lam_k2.astype(f32))) + lam_init)
    scale = DIFF_QK_DIM ** -0.5
    k1, k2 = k[0], k[1]

    def block(args):
        q1b, q2b, qpos = args
        mask = causal_mask(qpos, s)
        s1 = jnp.einsum('bhqe,bhke->bhqk', q1b, k1).astype(f32) * scale
        s2 = jnp.einsum('bhqe,bhke->bhqk', q2b, k2).astype(f32) * scale
        p1 = jax.nn.softmax(jnp.where(mask, s1, NEG_INF), axis=-1)
        p2 = jax.nn.softmax(jnp.where(mask, s2, NEG_INF), axis=-1)
        return jnp.einsum('bhqk,bhkd->bhqd', (p1 - lam * p2).astype(vt.dtype), vt)

    o = lax.map(block, (to_blocks(q[0]), to_blocks(q[1]), pos.reshape(-1, Q_BLOCK)))
    o = from_blocks(o)
    return rms_norm(o, g_sub, SUBLN_EPS) * (1.0 - lam_init)


def forgetting_attention(q, k, v, f_logit, b_f, pos):
    b, s, h, d = v.shape
    f32 = jnp.float32
    log_f = jax.nn.log_sigmoid(f_logit.astype(f32) + b_f.astype(f32))
    c = jnp.cumsum(log_f, axis=1).transpose(0, 2, 1)
    qt, kt, vt = (t.transpose(0, 2, 1, 3) for t in (q, k, v))
    scale = d ** -0.5

    def block(args):
        qb, cb, qpos = args
        mask = causal_mask(qpos, s)
        sc = (jnp.einsum('bhqe,bhke->bhqk', qb, kt).astype(f32) * scale
              + (cb[..., :, None] - c[..., None, :]))
        p = jax.nn.softmax(jnp.where(mask, sc, NEG_INF), axis=-1)
        return jnp.einsum('bhqk,bhkd->bhqd', p.astype(vt.dtype), vt)

    o = lax.map(block, (to_blocks(qt), to_blocks(c), pos.reshape(-1, Q_BLOCK)))
    return from_blocks(o)


def dilated_branch(q, k, v, window, dilation):
    b, s, h, d = q.shape
    n = window // dilation
    L = s // dilation
    nb = -(-L // Q_BLOCK)
    lp = nb * Q_BLOCK
    f32 = jnp.float32

    def to_sub(x):
        x = x.reshape(b, L, dilation, h, d).transpose(0, 2, 3, 1, 4)
        x = jnp.pad(x, ((0, 0), (0, 0), (0, 0), (0, lp - L), (0, 0)))
        return x.reshape(b, dilation, h, nb, Q_BLOCK, d)

    def with_prev(x):
        prev = jnp.pad(x, ((0, 0), (0, 0), (0, 0), (1, 0), (0, 0), (0, 0)))[:, :, :, :nb]
        return jnp.concatenate([prev, x], axis=4)

    qs = to_sub(q)
    kw, vw = with_prev(to_sub(k)), with_prev(to_sub(v))
    qi = Q_BLOCK + jnp.arange(Q_BLOCK)
    kj = jnp.arange(2 * Q_BLOCK)
    dist = qi[:, None] - kj[None, :]
    band = (dist >= 0) & (dist <= n)
    first = (jnp.arange(nb) == 0)[:, None, None] & (kj < Q_BLOCK)[None, None, :]
    mask = band[None] & ~first
    sc = jnp.einsum('brhnqe,brhnke->brhnqk', qs, kw).astype(f32) * (d ** -0.5)
    sc = jnp.where(mask, sc, NEG_INF)
    m = jnp.max(sc, axis=-1, keepdims=True)
    p = jnp.exp(sc - m)
    l = jnp.sum(p, axis=-1)
    o = jnp.einsum('brhnqk,brhnkd->brhnqd', p.astype(vw.dtype), vw).astype(f32) / l[..., None]
    lse = m[..., 0] + jnp.log(l)

    def from_sub(x):
        x = x.reshape((b, dilation, h, lp) + x.shape[5:])[:, :, :, :L]
        x = jnp.moveaxis(x, 3, 1)
        return x.reshape((b, s, h) + x.shape[4:])

    return from_sub(o), from_sub(lse)


def dilated_attention(q, k, v, pos):
    q = rope(q, pos)
    k = rope(k, pos)
    outs, lses = zip(*[dilated_branch(q, k, v, w, r) for (w, r) in DILATED_PATTERNS])
    wts = jax.nn.softmax(jnp.stack(lses, axis=0), axis=0)
    o = jnp.sum(wts[..., None] * jnp.stack(outs, axis=0), axis=0)
    return o.astype(v.dtype)


def swiglu(x, w_gate, w_up, w_down):
    return (jax.nn.silu(x @ w_gate) * (x @ w_up)) @ w_down


def hybrid_mixer(xn, w_in, b_f, lam_q1, lam_k1, lam_q2, lam_k2, g_sub, w_o, lam_init):
    b, s, _ = xn.shape
    pos = jnp.arange(s)
    proj = xn @ w_in
    aq, ak, av, fq, fk, fv, fg, cq, ck, cv = jnp.split(proj, SPLIT_POINTS, axis=-1)
    o_a = differential_attention(
        aq.reshape(b, s, N_DIFF_HEADS, 2 * DIFF_QK_DIM), ak.reshape(b, s, N_DIFF_HEADS, 2 * DIFF_QK_DIM),
        av.reshape(b, s, N_DIFF_HEADS, HEAD_DIM), lam_q1, lam_k1, lam_q2, lam_k2, g_sub, lam_init, pos)
    o_b = forgetting_attention(
        fq.reshape(b, s, N_FOX_HEADS, HEAD_DIM), fk.reshape(b, s, N_FOX_HEADS, HEAD_DIM),
        fv.reshape(b, s, N_FOX_HEADS, HEAD_DIM), fg, b_f, pos)
    o_c = dilated_attention(
        cq.reshape(b, s, N_DIL_HEADS, HEAD_DIM), ck.reshape(b, s, N_DIL_HEADS, HEAD_DIM),
        cv.reshape(b, s, N_DIL_HEADS, HEAD_DIM), pos)
    o = jnp.concatenate([o_a.reshape(b, s, -1), o_b.reshape(b, s, -1), o_c.reshape(b, s, -1)], axis=-1)
    return o @ w_o


def setup_inputs(seed: int = 0) -> dict:
    key = jax.random.key(seed)
    ks = jax.random.split(key, 20)
    nrm = jax.random.normal
    f32 = jnp.float32
    return {
        'x': nrm(ks[0], (BATCH, SEQ, D_MODEL), f32),
        'w_in': nrm(ks[1], (DEPTH, D_MODEL, IN_WIDTH), f32) * D_MODEL ** -0.5,
        'b_f': 3.0 + 0.5 * nrm(ks[2], (DEPTH, N_FOX_HEADS), f32),
        'lam_q1': 0.1 * nrm(ks[3], (DEPTH, DIFF_QK_DIM), f32),
        'lam_k1': 0.1 * nrm(ks[4], (DEPTH, DIFF_QK_DIM), f32),
        'lam_q2': 0.1 * nrm(ks[5], (DEPTH, DIFF_QK_DIM), f32),
        'lam_k2': 0.1 * nrm(ks[6], (DEPTH, DIFF_QK_DIM), f32),
        'g_sub': 1.0 + 0.1 * nrm(ks[7], (DEPTH, HEAD_DIM), f32),
        'w_o': nrm(ks[8], (DEPTH, MIX_WIDTH, D_MODEL), f32) * MIX_WIDTH ** -0.5,
        'g_ffn1': 1.0 + 0.1 * nrm(ks[9], (DEPTH, D_MODEL), f32),
        'w1_gate': nrm(ks[10], (DEPTH, D_MODEL, D_FF), f32) * D_MODEL ** -0.5,
        'w1_up': nrm(ks[11], (DEPTH, D_MODEL, D_FF), f32) * D_MODEL ** -0.5,
        'w1_down': nrm(ks[12], (DEPTH, D_FF, D_MODEL), f32) * D_FF ** -0.5,
        'g_mix': 1.0 + 0.1 * nrm(ks[13], (DEPTH, D_MODEL), f32),
        'g_ffn2': 1.0 + 0.1 * nrm(ks[14], (DEPTH, D_MODEL), f32),
        'w2_gate': nrm(ks[15], (DEPTH, D_MODEL, D_FF), f32) * D_MODEL ** -0.5,
        'w2_up': nrm(ks[16], (DEPTH, D_MODEL, D_FF), f32) * D_MODEL ** -0.5,
        'w2_down': nrm(ks[17], (DEPTH, D_FF, D_MODEL), f32) * D_FF ** -0.5,
        'g_final': 1.0 + 0.1 * nrm(ks[18], (D_MODEL,), f32),
    }


def reference(x, w_in, b_f, lam_q1, lam_k1, lam_q2, lam_k2, g_sub, w_o, g_ffn1, w1_gate, w1_up, w1_down,
              g_mix, g_ffn2, w2_gate, w2_up, w2_down, g_final):
    for l in range(DEPTH):
        lam_init = 0.8 - 0.6 * math.exp(-0.3 * l)
        x = x + MACARON_SCALE * swiglu(rms_norm(x, g_ffn1[l]), w1_gate[l], w1_up[l], w1_down[l])
        x = x + hybrid_mixer(rms_norm(x, g_mix[l]), w_in[l], b_f[l], lam_q1[l], lam_k1[l], lam_q2[l],
                             lam_k2[l], g_sub[l], w_o[l], lam_init)
        x = x + MACARON_SCALE * swiglu(rms_norm(x, g_ffn2[l]), w2_gate[l], w2_up[l], w2_down[l])
    return rms_norm(x, g_final)
```

```python
import math
from contextlib import ExitStack

import numpy as np
import ml_dtypes

import concourse.bass as bass
import concourse.mybir as mybir
from concourse.bass_utils import run_bass_kernel_spmd

F32 = mybir.dt.float32
BF16 = mybir.dt.bfloat16
AF = mybir.ActivationFunctionType
ALU = mybir.AluOpType

D = 1024
DFF = 2816
NFC = DFF // 128
DEPTH = 4
SEQ = 4096
NCORES = 8
INW = 3078
NORM_EPS = 1e-6
SUBLN_EPS = 1e-5
O_AQ, O_AK, O_AV, O_FQ, O_FK, O_FV, O_FG, O_CQ, O_CK, O_CV = 0, 384, 768, 1152, 1536, 1920, 2304, 2310, 2566, 2822


class Buf:
    def __init__(self, name, excl=False):
        self.name = name
        self.excl = excl
        self.w = {}
        self.r = {}


def _merge(d, tok):
    if tok is None:
        return
    s, v = tok
    if d.get(s, 0) < v:
        d[s] = v


class Ctx:
    def __init__(self, nc):
        self.nc = nc
        self.eng = {"pe": nc.tensor, "act": nc.scalar, "dve": nc.vector, "pool": nc.gpsimd, "sp": nc.sync}
        self.sems = {}
        self.cnt = {}
        self.seen = {}
        self.ninst = 0

    def sem(self, name):
        if name not in self.sems:
            self.sems[name] = self.nc.alloc_semaphore(name)
            self.cnt[name] = 0
        return self.sems[name]

    def wait(self, e, tok):
        if tok is None:
            return
        s, v = tok
        if v <= 0:
            return
        key = (e, s)
        if self.seen.get(key, 0) >= v:
            return
        self.eng[e].wait_ge(self.sems[s], v)
        self.seen[key] = v
        self.ninst += 1

    def _deps(self, e, reads, writes):
        reads = [b for b in reads if b is not None]
        writes = [b for b in writes if b is not None]
        for b in reads:
            for s, v in b.w.items():
                self.wait(e, (s, v))
            if b.excl:
                for s, v in b.r.items():
                    self.wait(e, (s, v))
        for b in writes:
            for s, v in b.w.items():
                self.wait(e, (s, v))
            for s, v in b.r.items():
                self.wait(e, (s, v))

    def _commit(self, tok, reads, writes):
        reads = [b for b in reads if b is not None]
        writes = [b for b in writes if b is not None]
        for b in reads:
            if b.excl:
                b.w = {tok[0]: tok[1]}
                b.r = {}
            else:
                _merge(b.r, tok)
        for b in writes:
            b.w = {tok[0]: tok[1]}
            b.r = {}

    def op(self, e, fn, reads=(), writes=()):
        self._deps(e, reads, writes)
        ins = fn(self.eng[e])
        s = "c_" + e
        self.sem(s)
        self.cnt[s] += 1
        ins.then_inc(self.sems[s], 1)
        self.ninst += 1
        tok = (s, self.cnt[s])
        self._commit(tok, reads, writes)
        return tok

    def group(self, e, fns, reads=(), writes=()):
        self._deps(e, reads, writes)
        ins = None
        for fn in fns:
            ins = fn(self.eng[e])
            self.ninst += 1
        s = "c_" + e
        self.sem(s)
        self.cnt[s] += 1
        ins.then_inc(self.sems[s], 1)
        tok = (s, self.cnt[s])
        self._commit(tok, reads, writes)
        return tok

    def dma(self, e, semname, out, in_, reads=(), writes=(), **kw):
        self.sem(semname)
        self._deps(e, reads, writes)
        self.wait(e, (semname, self.cnt[semname]))
        ins = self.eng[e].dma_start(out=out, in_=in_, **kw)
        self.cnt[semname] += 16
        ins.then_inc(self.sems[semname], 16)
        self.ninst += 1
        tok = (semname, self.cnt[semname])
        self._commit(tok, reads, writes)
        return tok

    def barrier(self, engines=("pe", "act", "dve", "pool", "sp")):
        for e in engines:
            for s, v in self.cnt.items():
                self.wait(e, (s, v))


class G:
    pass


def rmsnorm_rstd(c, g, xt_ap, XT, junk, JUNK, ss, SS, rs, RS):
    c.op("act", lambda e: e.activation(out=junk, in_=xt_ap, func=AF.Square, accum_out=ss),
         reads=[XT], writes=[JUNK, SS])
    c.op("act", lambda e: e.activation(out=rs, in_=ss, func=AF.Sqrt, scale=1.0 / D, bias=g.eps_norm[:, 0:1]),
         reads=[SS], writes=[RS])
    c.op("dve", lambda e: e.reciprocal(out=rs, in_=rs), reads=[], writes=[RS])


def norm_tile(c, g, i, tt, x_src, XSRC, gbc, GBC, t):
    n = i * 4 + tt
    s = n % 2
    r0 = n * 128
    c.dma("sp", f"ld_xt{s}", t.xt[s][:], x_src[r0:r0 + 128, :], reads=[XSRC], writes=[t.XT[s]])
    rmsnorm_rstd(c, g, t.xt[s][:], t.XT[s], t.junk[:], t.JUNK, t.ss[s][:], t.SS[s], t.rs[s][:], t.RS[s])
    c.op("dve", lambda e: e.scalar_tensor_tensor(out=t.xn[tt][:], in0=t.xt[s][:], scalar=t.rs[s][:, 0:1],
                                                 in1=gbc[:], op0=ALU.mult, op1=ALU.mult),
         reads=[t.XT[s], t.RS[s], GBC], writes=[t.XN[tt]])


def transpose_tile(c, g, tt, t, xnT, XNT):
    if tt % 2 == 0:
        pst, PSTB = g.psT[:], g.PST
    else:
        pst, PSTB = g.ps[6][:].bitcast(BF16), g.PS[6]
    c.group("pe", [
        (lambda e, kc=kc: e.transpose(out=pst[:, kc * 128:(kc + 1) * 128],
                                      in_=t.xn[tt][:, kc * 128:(kc + 1) * 128], identity=g.ident[:]))
        for kc in range(8)], reads=[t.XN[tt], g.IDENT], writes=[PSTB])
    src = pst.rearrange("p (k t) -> p k t", k=8)
    dst = xnT[:, :, tt * 128:(tt + 1) * 128]
    if tt % 2 == 0:
        c.op("act", lambda e: e.copy(out=dst, in_=src), reads=[PSTB], writes=[XNT])
    else:
        c.op("dve", lambda e: e.tensor_copy(out=dst, in_=src), reads=[PSTB], writes=[XNT])


def norm_to_xnT(c, g, i, x_src, XSRC, gbc, GBC, t, xnT, XNT, S):
    for tt in range(4):
        norm_tile(c, g, i, tt, x_src, XSRC, gbc, GBC, t)
        transpose_tile(c, g, tt, t, xnT, XNT)


class T:
    pass


def alloc_norm_tiles(nc, st, pfx):
    t = T()
    A = lambda name, shape, dt: st.enter_context(nc.sbuf_tensor(pfx + name, shape, dt))
    t.xt = [A(f"xt{i}", [128, D], F32) for i in range(2)]
    t.XT = [Buf(f"xt{i}") for i in range(2)]
    t.xn = [A(f"xn{i}", [128, D], BF16) for i in range(4)]
    t.XN = [Buf(f"xn{i}") for i in range(4)]
    t.junk = A("junk", [128, D], BF16)
    t.JUNK = Buf("junk")
    t.ss = [A(f"ss{i}", [128, 1], F32) for i in range(2)]
    t.SS = [Buf(f"ss{i}") for i in range(2)]
    t.rs = [A(f"rs{i}", [128, 1], F32) for i in range(2)]
    t.RS = [Buf(f"rs{i}") for i in range(2)]
    return t


def phase_ffn(c, g, wg_d, wu_d, wd_d, gvec_d, x_src, XSRC, x_dst, XDST, S, pfx):
    nc = c.nc
    NB = S // 512
    with ExitStack() as st:
        A = lambda name, shape, dt: st.enter_context(nc.sbuf_tensor(pfx + name, shape, dt))
        wg = A("wg", [128, 8, DFF], BF16)
        wu = A("wu", [128, 8, DFF], BF16)
        wd = A("wd", [128, NFC, D], BF16)
        gbc = A("gbc", [128, D], F32)
        GBC = Buf("gbc")
        t = alloc_norm_tiles(nc, st, pfx)
        xnT = [A(f"xnT{i}", [128, 8, 512], BF16) for i in range(2)]
        XNT = [Buf(f"xnT{i}") for i in range(2)]
        hT = A("hT", [128, NFC, 512], BF16)
        HT = Buf("hT")
        sg = [A(f"sg{i}", [128, 512], F32) for i in range(2)]
        SG = [Buf(f"sg{i}") for i in range(2)]
        xr = [A(f"xr{i}", [128, 512], F32) for i in range(3)]
        XR = [Buf(f"xr{i}") for i in range(3)]

        bounds = [0, 128, 384, 768, 1280, 1920, 2368, DFF]
        NG = len(bounds) - 1
        WG = [Buf(f"wg{j}") for j in range(NG)]
        WU = [Buf(f"wu{j}") for j in range(NG)]
        WD = [Buf(f"wd{j}") for j in range(2)]
        c.dma("sp", "ld_g", gbc[:], gvec_d.partition_broadcast(128), writes=[GBC])
        wg_v = wg_d.rearrange("(kc p) f -> p kc f", p=128)
        wu_v = wu_d.rearrange("(kc p) f -> p kc f", p=128)
        wd_v = wd_d.rearrange("(fc p) d -> p fc d", p=128)
        k = 0
        for j in range(NG):
            lo, hi = bounds[j], bounds[j + 1]
            c.dma("pool", f"ld_w{k % 4}", wg[:, :, lo:hi], wg_v[:, :, lo:hi], writes=[WG[j]])
            k += 1
            c.dma("pool", f"ld_w{k % 4}", wu[:, :, lo:hi], wu_v[:, :, lo:hi], writes=[WU[j]])
            k += 1
            if j == 3:
                for jj in range(2):
                    c.dma("pool", f"ld_w{k % 4}", wd[:, jj * 11:(jj + 1) * 11, :], wd_v[:, jj * 11:(jj + 1) * 11, :],
                          writes=[WD[jj]])
                    k += 1

        def grp(col):
            for j in range(NG):
                if bounds[j] <= col < bounds[j + 1]:
                    return j

        norm_to_xnT(c, g, 0, x_src, XSRC, gbc, GBC, t, xnT[0], XNT[0], S)
        for i in range(NB):
            b = i % 2
            for fc in range(NFC):
                pg, PG = g.ps[fc % 2], g.PS[fc % 2]
                pu, PU = g.ps[2 + fc % 2], g.PS[2 + fc % 2]
                j = grp(fc * 128)
                j2 = grp(fc * 128 + 127)
                wbufs_g = [WG[j]] + ([WG[j2]] if j2 != j else [])
                wbufs_u = [WU[j]] + ([WU[j2]] if j2 != j else [])
                c.group("pe", [
                    (lambda e, kc=kc: e.matmul(pg[:], wg[:, kc, fc * 128:(fc + 1) * 128], xnT[b][:, kc, :],
                                               start=(kc == 0), stop=(kc == 7)))
                    for kc in range(8)], reads=[XNT[b]] + wbufs_g, writes=[PG])
                c.group("pe", [
                    (lambda e, kc=kc: e.matmul(pu[:], wu[:, kc, fc * 128:(fc + 1) * 128], xnT[b][:, kc, :],
                                               start=(kc == 0), stop=(kc == 7)))
                    for kc in range(8)], reads=[XNT[b]] + wbufs_u, writes=[PU])
                c.op("act", lambda e: e.activation(out=sg[fc % 2][:], in_=pg[:], func=AF.Silu),
                     reads=[PG], writes=[SG[fc % 2]])
                c.op("dve", lambda e: e.tensor_tensor(out=hT[:, fc, :], in0=sg[fc % 2][:], in1=pu[:], op=ALU.mult),
                     reads=[SG[fc % 2], PU], writes=[HT])
                if i + 1 < NB and fc in (3, 8, 13, 18):
                    norm_tile(c, g, i + 1, (fc - 3) // 5, x_src, XSRC, gbc, GBC, t)
            if i + 1 < NB:
                for tt in range(4):
                    transpose_tile(c, g, tt, t, xnT[1 - b], XNT[1 - b])
            n = 0
            for tt in range(4):
                for dh in range(2):
                    py, PY = g.ps[4 + n % 2], g.PS[4 + n % 2]
                    r0 = (i * 4 + tt) * 128
                    xs = (i * 8 + n) % 3
                    c.dma("sp", f"ld_xr{xs}", xr[xs][:], x_src[r0:r0 + 128, dh * 512:(dh + 1) * 512],
                          reads=[XSRC], writes=[XR[xs]])
                    c.group("pe", [
                        (lambda e, fc=fc: e.matmul(py[:], hT[:, fc, tt * 128:(tt + 1) * 128],
                                                   wd[:, fc, dh * 512:(dh + 1) * 512],
                                                   start=(fc == 0), stop=(fc == NFC - 1)))
                        for fc in range(NFC)], reads=[HT, WD[0], WD[1]], writes=[PY])
                    c.op("dve", lambda e: e.scalar_tensor_tensor(out=xr[xs][:], in0=py[:], scalar=0.5, in1=xr[xs][:],
                                                                 op0=ALU.mult, op1=ALU.add),
                         reads=[PY], writes=[XR[xs]])
                    c.dma("sp", f"st_xr{xs}", x_dst[r0:r0 + 128, dh * 512:(dh + 1) * 512], xr[xs][:],
                          reads=[XR[xs]], writes=[XDST])
                    n += 1
        c.barrier()


def phase_final(c, g, gvec_d, x_src, XSRC, x_dst, XDST, S, pfx):
    nc = c.nc
    with ExitStack() as st:
        A = lambda name, shape, dt: st.enter_context(nc.sbuf_tensor(pfx + name, shape, dt))
        gbc = A("gbc", [128, D], F32)
        GBC = Buf("gbc")
        t = alloc_norm_tiles(nc, st, pfx)
        yo = [A(f"yo{i}", [128, D], F32) for i in range(2)]
        YO = [Buf(f"yo{i}") for i in range(2)]
        c.dma("sp", "ld_g", gbc[:], gvec_d.partition_broadcast(128), writes=[GBC])
        for n in range(S // 128):
            s = n % 2
            r0 = n * 128
            c.dma("sp", f"ld_xt{s}", t.xt[s][:], x_src[r0:r0 + 128, :], reads=[XSRC], writes=[t.XT[s]])
            rmsnorm_rstd(c, g, t.xt[s][:], t.XT[s], t.junk[:], t.JUNK, t.ss[s][:], t.SS[s], t.rs[s][:], t.RS[s])
            c.op("dve", lambda e: e.scalar_tensor_tensor(out=yo[s][:], in0=t.xt[s][:], scalar=t.rs[s][:, 0:1],
                                                         in1=gbc[:], op0=ALU.mult, op1=ALU.mult),
                 reads=[t.XT[s], t.RS[s], GBC], writes=[YO[s]])
            c.dma("sp", f"st_yo{s}", x_dst[r0:r0 + 128, :], yo[s][:], reads=[YO[s]], writes=[XDST])
        c.barrier()


def alloc_scratch(nc, S):
    sc = T()
    d = lambda name, shape, dtype: nc.dram_tensor(name, shape, dtype).ap()
    sc.qTd = d("s_qTd", [384, S], BF16)
    sc.kTd = d("s_kTd", [384, S], BF16)
    sc.qTf = d("s_qTf", [6, 65, S], BF16)
    sc.kTf = d("s_kTf", [6, 64, S], BF16)
    sc.qTc = d("s_qTc", [256, S], BF16)
    sc.kTc = d("s_kTc", [256, S], BF16)
    sc.v = d("s_v", [S, 1024], BF16)
    sc.ckm = d("s_ckm", [128, (S // 128) * 6], F32)
    sc.cend = d("s_cend", [6, S // 512], F32)
    return sc


WQ_AQ, WQ_AK, WQ_FQ, WQ_FK, WQ_CQ, WQ_CK = 0, 384, 768, 1152, 1536, 1792
WQ_RAQ, WQ_RAK, WQ_RCQ, WQ_RCK = 2048, 2432, 2816, 3072
WQ_FG = 3328
WQ_COLS = 3334


def phase_proj(c, g, sc, w_in_d, gvec_d, bf_d, consts, x_src, S, pfx):
    nc = c.nc
    NB = S // 512
    NT = S // 128
    with ExitStack() as st:
        A = lambda name, shape, dt: st.enter_context(nc.sbuf_tensor(pfx + name, shape, dt))
        wq = A("wq", [128, 8, WQ_COLS], BF16)
        wv = A("wv", [128, 8, 1024], BF16)
        gbc = A("gbc", [128, D], F32)
        GBC = Buf("gbc")
        t = alloc_norm_tiles(nc, st, pfx)
        xnT = [A(f"xnT{i}", [128, 8, 512], BF16) for i in range(2)]
        XNT = [Buf(f"xnT{i}") for i in range(2)]
        rt = [[A(f"rt{i}_{j}", [128, 512], F32) for j in range(4)] for i in range(2)]
        RT = [[Buf(f"rt{i}_{j}") for j in range(4)] for i in range(2)]
        tm1 = [A(f"tm1_{i}", [128, 512], F32) for i in range(2)]
        TM1 = [Buf(f"tm1_{i}") for i in range(2)]
        tm2 = [A(f"tm2_{i}", [128, 512], F32) for i in range(2)]
        TM2 = [Buf(f"tm2_{i}") for i in range(2)]
        ob = [A(f"ob{i}", [128, 512], BF16) for i in range(3)]
        OB = [Buf(f"ob{i}") for i in range(3)]
        vb = [A(f"vb{i}", [128, 1024], BF16) for i in range(2)]
        VB = [Buf(f"vb{i}") for i in range(2)]
        nbf = A("nbf", [6, 1], F32)
        NBF = Buf("nbf")
        uu = A("uu", [6, 512], F32)
        UU = Buf("uu")
        lf = A("lf", [6, 512], F32)
        LF = Buf("lf")
        cb = [A(f"cb{i}", [6, 512], F32) for i in range(2)]
        CB = [Buf(f"cb{i}") for i in range(2)]
        augb = A("augb", [6, 512], BF16)
        AUGB = Buf("augb")
        ones6 = A("ones6", [6, 512], F32)
        ONES6 = Buf("ones6")
        ckm = A("ckm", [128, NT, 6], F32)
        CKM = Buf("ckm")

        WQP = [Buf(f"wqp{j}") for j in range(4)]
        WQR = {0: Buf("wqr0"), 2: Buf("wqr2")}
        WV = [Buf(f"wv{j}") for j in range(3)]
        wv_ = w_in_d.rearrange("(kc p) f -> p kc f", p=128)
        c.dma("sp", "ld_g", gbc[:], gvec_d.partition_broadcast(128), writes=[GBC])
        c.dma("sp", "ld_c1", nbf[:], bf_d.rearrange("(p o) -> p o", o=1), writes=[NBF])
        c.op("dve", lambda e: e.tensor_scalar(out=nbf[:], in0=nbf[:], scalar1=-1.0, scalar2=None, op0=ALU.mult),
             reads=[], writes=[NBF])
        c.op("dve", lambda e: e.memset(ones6[:], 1.0), writes=[ONES6])
        k = 0
        for (pj, dst, srcc, n) in [(3, WQ_FG, O_FG, 6), (0, WQ_AQ, O_AQ, 768), (1, WQ_FQ, O_FQ, 768),
                                   (2, WQ_CQ, O_CQ, 512)]:
            c.dma("pool", f"ld_w{k % 4}", wq[:, :, dst:dst + n], wv_[:, :, srcc:srcc + n], writes=[WQP[pj]])
            k += 1
        for j, (dst, srcc, n) in enumerate([(0, O_AV, 384), (384, O_FV, 384), (768, O_CV, 256)]):
            c.dma("pool", f"ld_w{k % 4}", wv[:, :, dst:dst + n], wv_[:, :, srcc:srcc + n], writes=[WV[j]])
            k += 1
        for (pj, dst, srcc, n, half) in [(0, WQ_RAQ, WQ_AQ, 768, 16), (2, WQ_RCQ, WQ_CQ, 512, 32)]:
            for kc in range(8):
                sv = wq[:, kc, srcc:srcc + n].rearrange("p (u e) -> p u e", e=2 * half)
                dv = wq[:, kc, dst:dst + n].rearrange("p (u e) -> p u e", e=2 * half)
                eng = "dve" if kc % 2 == 0 else "pool"
                c.op(eng, lambda e: e.tensor_scalar(out=dv[:, :, 0:half], in0=sv[:, :, half:2 * half],
                                                    scalar1=-1.0, scalar2=None, op0=ALU.mult),
                     reads=[WQP[pj]], writes=[WQR[pj]])
                c.op(eng, lambda e: e.tensor_copy(out=dv[:, :, half:2 * half], in_=sv[:, :, 0:half]),
                     reads=[WQP[pj]], writes=[WQR[pj]])

        chunks = []
        for ci in range(16):
            col = ci * 128
            if ci < 6:
                chunks.append(("diff", col, WQ_RAQ + col, 0, 0))
            elif ci < 9:
                chunks.append(("fq", col, None, None, 1))
            elif ci < 12:
                chunks.append(("fk", col, None, None, 1))
            else:
                chunks.append(("dil", col, WQ_RCQ + (col - WQ_CQ), 2, 2))

        nob = 0
        norm_to_xnT(c, g, 0, x_src, None, gbc, GBC, t, xnT[0], XNT[0], S)
        for i in range(NB):
            b = i % 2
            t0 = i * 512
            for j, nm in enumerate(["c_cosA", "c_sinA", "c_cosC", "c_sinC"]):
                c.dma("sp", f"ld_rt{j}", rt[b][j][:], consts[nm][:, t0:t0 + 512], writes=[RT[b][j]])
            pf, PF = g.ps[6], g.PS[6]
            c.group("pe", [
                (lambda e, kc=kc: e.matmul(pf[0:6, :], wq[:, kc, WQ_FG:WQ_FG + 6], xnT[b][:, kc, :],
                                           start=(kc == 0), stop=(kc == 7)))
                for kc in range(8)], reads=[XNT[b], WQP[3]], writes=[PF])
            c.op("act", lambda e: e.activation(out=uu[:], in_=pf[0:6, :], func=AF.Exp, scale=-1.0, bias=nbf[:, 0:1]),
                 reads=[PF, NBF], writes=[UU])
            c.op("act", lambda e: e.activation(out=lf[:], in_=uu[:], func=AF.Ln, scale=1.0, bias=g.one_c[0:6, 0:1]),
                 reads=[UU], writes=[LF])
            init = 0.0 if i == 0 else cb[1 - b][:, 511:512]
            c.op("dve", lambda e: e.tensor_tensor_scan(out=cb[b][:], data0=ones6[:], data1=lf[:], initial=init,
                                                       op0=ALU.mult, op1=ALU.add),
                 reads=[LF, ONES6] + ([CB[1 - b]] if i > 0 else []), writes=[CB[b]])
            c.op("dve", lambda e: e.tensor_scalar(out=augb[:], in0=cb[b][:], scalar1=cb[b][:, 511:512], scalar2=-1.0,
                                                  op0=ALU.subtract, op1=ALU.mult),
                 reads=[CB[b]], writes=[AUGB])
            c.dma("sp", "st_aug", sc.qTf[:, 64, t0:t0 + 512], augb[:], reads=[AUGB])
            c.dma("sp", "st_cend", sc.cend[:, i:i + 1], cb[b][:, 511:512], reads=[CB[b]], allow_slow_non_contiguous=True)

            def fg_transposes():
                c.group("pe", [
                    (lambda e, tt=tt: e.transpose(out=pf[:, tt * 6:(tt + 1) * 6],
                                                  in_=cb[b][0:6, tt * 128:(tt + 1) * 128],
                                                  identity=g.identf[0:6, 0:6]))
                    for tt in range(4)], reads=[CB[b], g.IDENTF], writes=[PF])
                c.op("dve", lambda e: e.tensor_copy(out=ckm[:, i * 4:(i + 1) * 4, :],
                                                    in_=pf[:, 0:24].rearrange("p (t h) -> p t h", h=6)),
                     reads=[PF], writes=[CKM])
            for ci, (kind, col, rcol, rti, pj) in enumerate(chunks):
                if ci == 4:
                    fg_transposes()
                if i + 1 < NB and ci in (2, 6, 10, 14):
                    norm_tile(c, g, i + 1, (ci - 2) // 4, x_src, None, gbc, GBC, t)
                pa, PA = g.ps[ci % 2], g.PS[ci % 2]
                c.group("pe", [
                    (lambda e, kc=kc: e.matmul(pa[:], wq[:, kc, col:col + 128], xnT[b][:, kc, :],
                                               start=(kc == 0), stop=(kc == 7)))
                    for kc in range(8)], reads=[XNT[b], WQP[pj]], writes=[PA])
                o = nob % 3
                nob += 1
                if rcol is not None:
                    pb_, PB_ = g.ps[2 + ci % 2], g.PS[2 + ci % 2]
                    c.group("pe", [
                        (lambda e, kc=kc: e.matmul(pb_[:], wq[:, kc, rcol:rcol + 128], xnT[b][:, kc, :],
                                                   start=(kc == 0), stop=(kc == 7)))
                        for kc in range(8)], reads=[XNT[b], WQR[pj]], writes=[PB_])
                    s2 = ci % 2
                    c.op("dve", lambda e: e.tensor_tensor(out=tm1[s2][:], in0=pa[:], in1=rt[b][rti][:], op=ALU.mult),
                         reads=[PA, RT[b][rti]], writes=[TM1[s2]])
                    c.op("dve", lambda e: e.tensor_tensor(out=tm2[s2][:], in0=pb_[:], in1=rt[b][rti + 1][:], op=ALU.mult),
                         reads=[PB_, RT[b][rti + 1]], writes=[TM2[s2]])
                    c.op("pool", lambda e: e.tensor_tensor(out=ob[o][:], in0=tm1[s2][:], in1=tm2[s2][:], op=ALU.add),
                         reads=[TM1[s2], TM2[s2]], writes=[OB[o]])
                elif kind == "fq":
                    c.op("act", lambda e: e.activation(out=ob[o][:], in_=pa[:], func=AF.Copy, scale=0.125),
                         reads=[PA], writes=[OB[o]])
                else:
                    c.op("act", lambda e: e.copy(out=ob[o][:], in_=pa[:]), reads=[PA], writes=[OB[o]])
                if kind == "diff":
                    dstt = sc.qTd if ci < 3 else sc.kTd
                    r = (ci % 3) * 128
                    c.dma("sp", f"st_ob{o}", dstt[r:r + 128, t0:t0 + 512], ob[o][:], reads=[OB[o]])
                elif kind == "dil":
                    dstt = sc.qTc if ci < 14 else sc.kTc
                    r = (ci % 2) * 128
                    c.dma("sp", f"st_ob{o}", dstt[r:r + 128, t0:t0 + 512], ob[o][:], reads=[OB[o]])
                else:
                    dstt = sc.qTf if kind == "fq" else sc.kTf
                    h0 = 2 * ((ci - 6) % 3)
                    c.dma("sp", f"st_ob{o}", dstt[h0, 0:64, t0:t0 + 512], ob[o][0:64, :], reads=[OB[o]])
                    c.dma("sp", f"st_ob{o}b", dstt[h0 + 1, 0:64, t0:t0 + 512], ob[o][64:128, :], reads=[OB[o]])
            for tt in range(4):
                n = i * 4 + tt
                s = n % 2
                for hf in range(2):
                    pv, PV = g.ps[4 + hf], g.PS[4 + hf]
                    c.group("pe", [
                        (lambda e, kc=kc: e.matmul(pv[:], xnT[b][:, kc, tt * 128:(tt + 1) * 128],
                                                   wv[:, kc, hf * 512:(hf + 1) * 512],
                                                   start=(kc == 0), stop=(kc == 7)))
                        for kc in range(8)], reads=[XNT[b]] + WV, writes=[PV])
                    if hf == 0:
                        c.op("act", lambda e: e.copy(out=vb[s][:, 0:512], in_=pv[:]), reads=[PV], writes=[VB[s]])
                    else:
                        c.op("dve", lambda e: e.tensor_copy(out=vb[s][:, 512:1024], in_=pv[:]), reads=[PV], writes=[VB[s]])
                c.dma("sp", f"st_vb{s}", sc.v[n * 128:(n + 1) * 128, :], vb[s][:], reads=[VB[s]])
            if i + 1 < NB:
                for tt in range(4):
                    transpose_tile(c, g, tt, t, xnT[1 - b], XNT[1 - b])
        c.dma("sp", "st_ckm", sc.ckm, ckm[:].rearrange("p t h -> p (t h)"), reads=[CKM])
        c.barrier()


def phase_attn(c, g, sc, wo_d, lamq1_d, lamk1_d, lamq2_d, lamk2_d, gsub_d, lam_init, consts, x_src, x_dst, S, pfx,
               dbg=None):
    nc = c.nc
    NB = S // 512
    NT = S // 128
    with ExitStack() as st:
        A = lambda name, shape, dt: st.enter_context(nc.sbuf_tensor(pfx + name, shape, dt))
        oT = A("oT", [128, 8, S], BF16)
        OT = Buf("oT")
        wo = A("wo", [128, 8, D], BF16)
        WO = Buf("wo")
        qs = [A(f"qs{i}", [128, S], BF16) for i in range(2)]
        QS = [Buf(f"qs{i}") for i in range(2)]
        qz = [A(f"qz{i}", [128, S], BF16) for i in range(4)]
        QZ = [Buf(f"qz{i}") for i in range(4)]
        ks = [A(f"ks{i}", [128, S], BF16) for i in range(2)]
        KS = [Buf(f"ks{i}") for i in range(2)]
        va = [A(f"va{i}", [128, NT, 128], BF16) for i in range(2)]
        VA = [Buf(f"va{i}") for i in range(2)]
        NPT = 5
        pt = [A(f"pt{i}", [128, 512], BF16) for i in range(NPT)]
        PT = [Buf(f"pt{i}") for i in range(NPT)]
        tril = A("tril", [128, 128], BF16)
        TRIL = Buf("tril")
        wt = A("wt", [128, 2688], BF16)
        WT = Buf("wt")
        ckm = A("ckm", [128, NT, 6], F32)
        CKM = Buf("ckm")
        cend = A("cend", [128, 6 * NB], F32)
        CEND = Buf("cend")
        bq = [A(f"bq{i}", [128, NT], F32) for i in range(2)]
        BQ = [Buf(f"bq{i}") for i in range(2)]
        rec = [A(f"rec{i}", [64, 512], F32) for i in range(2)]
        REC = [Buf(f"rec{i}") for i in range(2)]
        ta = A("ta", [64, 512], F32)
        TA = Buf("ta")
        tb = A("tb", [64, 512], F32)
        TB = Buf("tb")
        td2 = [A(f"td{i}", [64, 512], F32) for i in range(2)]
        TD2 = [Buf(f"td{i}") for i in range(2)]
        tsq2 = [A(f"tsq{i}", [64, 512], BF16) for i in range(2)]
        TSQ2 = [Buf(f"tsq{i}") for i in range(2)]
        tr2 = [A(f"tr{i}", [64, 512], F32) for i in range(2)]
        TR2 = [Buf(f"tr{i}") for i in range(2)]
        lv = [A(f"lv{i}", [64, 32], F32) for i in range(4)]
        LV = [Buf(f"lv{i}") for i in range(4)]
        lj = A("lj", [64, 32], F32)
        LJ = Buf("lj")
        sm = A("sm", [64, 8], F32)
        SM = Buf("sm")
        xr = [A(f"xr{i}", [128, 512], F32) for i in range(3)]
        XR = [Buf(f"xr{i}") for i in range(3)]

        c.dma("sp", "ld_c0", tril[:], consts["c_tril"], writes=[TRIL])
        c.dma("sp", "ld_c1", wt[:], consts["c_wt"], writes=[WT])
        c.dma("sp", "ld_c2", ckm[:].rearrange("p t h -> p (t h)"), sc.ckm, writes=[CKM])
        c.dma("sp", "ld_c3", cend[:], sc.cend.rearrange("h q -> (h q)").partition_broadcast(128), writes=[CEND])
        wo_v = wo_d.rearrange("(cc p) d -> p cc d", p=128)
        c.dma("pool", "ld_w0", wo[:], wo_v, writes=[WO])
        for j, dd in enumerate([lamq1_d, lamk1_d, lamq2_d, lamk2_d]):
            c.dma("sp", f"ld_lv{j}", lv[j][:], dd.partition_broadcast(64), writes=[LV[j]])
        c.dma("sp", "ld_c4", sm[:, 5:6], gsub_d.rearrange("(p o) -> p o", o=1), writes=[SM])
        for j in range(2):
            c.op("dve", lambda e: e.scalar_tensor_tensor(out=lj[:], in0=lv[2 * j][:], scalar=1.0, in1=lv[2 * j + 1][:],
                                                         op0=ALU.mult, op1=ALU.mult, accum_out=sm[:, j:j + 1]),
                 reads=[LV[2 * j], LV[2 * j + 1]], writes=[LJ, SM])
        c.op("act", lambda e: e.activation(out=sm[:, 2:4], in_=sm[:, 0:2], func=AF.Exp), reads=[], writes=[SM])
        c.op("dve", lambda e: e.tensor_tensor(out=sm[:, 4:5], in0=sm[:, 2:3], in1=sm[:, 3:4], op=ALU.subtract),
             reads=[], writes=[SM])
        c.op("dve", lambda e: e.tensor_scalar(out=sm[:, 4:5], in0=sm[:, 4:5], scalar1=float(lam_init), scalar2=-1.0,
                                              op0=ALU.add, op1=ALU.mult), reads=[], writes=[SM])
        c.op("dve", lambda e: e.tensor_scalar(out=sm[:, 5:6], in0=sm[:, 5:6], scalar1=float(1.0 - lam_init),
                                              scalar2=None, op0=ALU.mult), reads=[], writes=[SM])
        for i in range(2):
            c.op("pool", lambda e: e.memset(va[i][:, :, 64:128], 1.0), writes=[VA[i]])
        for i in range(4):
            c.op("pool" if i % 2 == 0 else "dve", lambda e: e.memset(qz[i][:], 0.0), writes=[QZ[i]])
        pm = g.psT[:].bitcast(F32)
        PM = g.PST

        vsrc = sc.v.rearrange("(kb p) c -> p kb c", p=128)
        state = {"sb": 0, "pt": 0, "ob": 0, "rec": 0, "bq": 0}
        SCALE_D = 32 ** -0.5
        nhalf = A("nhalf", [64, 512], F32)
        NHALF = Buf("nhalf")
        c.op("pool", lambda e: e.memset(nhalf[:], -0.5), writes=[NHALF])

        heads = []
        for h in range(6):
            cd, hl = h // 2, h % 2
            ksl = cd % 2

            def loads(h=h, cd=cd, hl=hl, ksl=ksl):
                if hl == 0:
                    c.dma("sp", f"ld_ks{ksl}", ks[ksl][:], sc.kTd[cd * 128:(cd + 1) * 128, :], writes=[KS[ksl]])
                for pz in (2 * hl, 2 * hl + 1):
                    r = cd * 128 + pz * 32
                    c.dma("sp", f"ld_qz{pz}", qz[pz][pz * 32:(pz + 1) * 32, :], sc.qTd[r:r + 32, :], writes=[QZ[pz]])
                c.dma("sp", f"ld_va{h % 2}", va[h % 2][:, :, 0:64], vsrc[:, :, h * 64:(h + 1) * 64], writes=[VA[h % 2]])
            heads.append(dict(kind="diff", hh=h, vs=h % 2, loads=loads, K=128, scale=SCALE_D, mask="causal",
                              units=[(qz[2 * hl], QZ[2 * hl], ks[ksl], KS[ksl]),
                                     (qz[2 * hl + 1], QZ[2 * hl + 1], ks[ksl], KS[ksl])]))
        for h in range(6):
            sl = (6 + h) % 2
            ksl = (3 + h) % 2

            def loads(h=h, sl=sl, ksl=ksl):
                c.dma("sp", f"ld_qs{sl}", qs[sl][0:65, :], sc.qTf[h], writes=[QS[sl]])
                c.dma("sp", f"ld_ks{ksl}", ks[ksl][0:64, :], sc.kTf[h], writes=[KS[ksl]])
                c.op("pool", lambda e: e.memset(ks[ksl][64:65, :], 1.0), writes=[KS[ksl]])
                c.dma("sp", f"ld_va{sl}", va[sl][:, :, 0:64], vsrc[:, :, (6 + h) * 64:(7 + h) * 64], writes=[VA[sl]])
            heads.append(dict(kind="fox", hh=6 + h, fh=h, vs=sl, loads=loads, K=65, scale=1.0, mask="causal",
                              units=[(qs[sl], QS[sl], ks[ksl], KS[ksl])]))
        for h in range(4):
            cc, hl = h // 2, h % 2
            ksl = (9 + cc) % 2
            zi = 0 if hl == 0 else 3
            sl = (12 + h) % 2

            def loads(h=h, cc=cc, hl=hl, ksl=ksl, zi=zi, sl=sl):
                if hl == 0:
                    c.dma("sp", f"ld_ks{ksl}", ks[ksl][:], sc.kTc[cc * 128:(cc + 1) * 128, :], writes=[KS[ksl]])
                r = cc * 128 + hl * 64
                c.dma("sp", f"ld_qz{zi}", qz[zi][hl * 64:(hl + 1) * 64, :], sc.qTc[r:r + 64, :], writes=[QZ[zi]])
                c.dma("sp", f"ld_va{sl}", va[sl][:, :, 0:64], vsrc[:, :, (12 + h) * 64:(13 + h) * 64], writes=[VA[sl]])
            heads.append(dict(kind="dil", hh=12 + h, vs=sl, loads=loads, K=128, scale=0.125, mask="dil",
                              units=[(qz[zi], QZ[zi], ks[ksl], KS[ksl])]))

        items = []
        for hi, hd in enumerate(heads):
            for qb in range(NB):
                for ui, un in enumerate(hd["units"]):
                    kb_lo = max(0, 4 * qb - 16) if hd["kind"] == "dil" else 0
                    kbs = list(range(kb_lo, 4 * qb + 4))
                    for n_, kb in enumerate(kbs):
                        items.append(dict(hi=hi, hd=hd, qb=qb, ui=ui, un=un, kb=kb, first=(n_ == 0),
                                          last=(n_ == len(kbs) - 1), head_start=(qb == 0 and ui == 0 and n_ == 0)))

        deferred = []

        def emit_S(it):
            hd = it["hd"]
            qt, QTB, kt, KTB = it["un"]
            qb, kb = it["qb"], it["kb"]
            q0 = qb * 512
            if it["first"] and hd["kind"] == "fox":
                bi = state["bq"] % 2
                state["bq"] += 1
                nk = 4 * qb + 4
                fh = hd["fh"]
                c.op("dve", lambda e: e.tensor_scalar(out=bq[bi][:, 0:nk], in0=ckm[:, 0:nk, fh],
                                                      scalar1=cend[:, fh * NB + qb:fh * NB + qb + 1], scalar2=None,
                                                      op0=ALU.subtract), reads=[CKM, CEND], writes=[BQ[bi]])
                hd["bi"] = bi
            if hd["kind"] == "fox":
                it["bi"] = hd["bi"]
            j = kb - 4 * qb
            col0 = max(0, j) * 128
            si = state["sb"] % 3
            state["sb"] += 1
            sbk, SBK = g.ps[si], g.PS[si]
            K = hd["K"]
            if hd["mask"] == "causal" and j >= 0:
                c.group("pe", [
                    lambda e: e.matmul(sbk[:, col0:512], kt[0:K, kb * 128:(kb + 1) * 128],
                                       qt[0:K, q0 + col0:q0 + 512], start=True, stop=False),
                    lambda e: e.matmul(sbk[:, col0:col0 + 128], g.ident[:], tril[:], start=False, stop=True),
                ], reads=[QTB, KTB, TRIL, g.IDENT], writes=[SBK])
            else:
                c.op("pe", lambda e: e.matmul(sbk[:, col0:512], kt[0:K, kb * 128:(kb + 1) * 128],
                                              qt[0:K, q0 + col0:q0 + 512], start=True, stop=True),
                     reads=[QTB, KTB], writes=[SBK])
            it["S"] = (j, col0, sbk, SBK)

        def plain_epilogue(obk, OBK, hh, qb, use_act=False):
            q0 = qb * 512
            ri = state["rec"] % 2
            state["rec"] += 1
            pb = (hh % 2) * 64
            if use_act:
                c.op("act", lambda e: e.activation(out=rec[ri][:], in_=obk[64:128, :], func=AF.Ln),
                     reads=[OBK], writes=[REC[ri]])
                c.op("act", lambda e: e.activation(out=rec[ri][:], in_=rec[ri][:], func=AF.Exp, scale=-1.0),
                     reads=[], writes=[REC[ri]])
            else:
                c.op("dve", lambda e: e.reciprocal(out=rec[ri][:], in_=obk[64:128, :]), reads=[OBK],
                     writes=[REC[ri]])
            c.op("dve", lambda e: e.tensor_tensor(out=oT[pb:pb + 64, hh // 2, q0:q0 + 512], in0=obk[0:64, :],
                                                  in1=rec[ri][:], op=ALU.mult), reads=[OBK, REC[ri]], writes=[OT])

        def diff_stage1(oa, OA, o2, O2, par):
            td, TD, tsq, TSQ = td2[par], TD2[par], tsq2[par], TSQ2[par]
            c.op("dve", lambda e: e.reciprocal(out=rec[0][:], in_=oa[64:128, :]), reads=[OA], writes=[REC[0]])
            c.op("dve", lambda e: e.tensor_tensor(out=ta[:], in0=oa[0:64, :], in1=rec[0][:], op=ALU.mult),
                 reads=[OA, REC[0]], writes=[TA])
            c.op("dve", lambda e: e.reciprocal(out=rec[1][:], in_=o2[64:128, :]), reads=[O2], writes=[REC[1]])
            c.op("dve", lambda e: e.tensor_tensor(out=tb[:], in0=o2[0:64, :], in1=rec[1][:], op=ALU.mult),
                 reads=[O2, REC[1]], writes=[TB])
            c.op("dve", lambda e: e.scalar_tensor_tensor(out=td[:], in0=tb[:], scalar=sm[:, 4:5], in1=ta[:],
                                                         op0=ALU.mult, op1=ALU.add), reads=[TA, TB, SM], writes=[TD])
            c.op("pool", lambda e: e.tensor_tensor(out=tsq[:], in0=td[:], in1=td[:], op=ALU.mult),
                 reads=[TD], writes=[TSQ])

        def diff_stage2(h, qb, par):
            td, TD, tsq, TSQ, tr, TR = td2[par], TD2[par], tsq2[par], TSQ2[par], tr2[par], TR2[par]
            q0 = qb * 512
            pb = (h % 2) * 64
            c.op("pe", lambda e: e.matmul(pm[0:64, :], g.ones64b[0:64, 0:64], tsq[:], start=True, stop=True),
                 reads=[TSQ, g.ONES64], writes=[PM])
            c.op("act", lambda e: e.activation(out=tr[:], in_=pm[0:64, :], func=AF.Ln, scale=1.0,
                                               bias=g.eps_sub[0:64, 0:1]), reads=[PM], writes=[TR])
            c.op("act", lambda e: e.activation(out=tr[:], in_=tr[:], func=AF.Exp, scale=-0.5), reads=[], writes=[TR])
            c.op("dve", lambda e: e.scalar_tensor_tensor(out=oT[pb:pb + 64, h // 2, q0:q0 + 512], in0=td[:],
                                                         scalar=sm[:, 5:6], in1=tr[:], op0=ALU.mult, op1=ALU.mult),
                 reads=[TD, TR, SM], writes=[OT])

        def emit_PV(i):
            it = items[i]
            hd = it["hd"]
            qb, kb = it["qb"], it["kb"]
            p, P, col0 = it["P"]
            if it["first"]:
                obi = 3 + state["ob"] % 4
                state["ob"] += 1
                hd["cur_o"] = (g.ps[obi], g.PS[obi])
            obk, OBK = hd["cur_o"]
            vs = hd["vs"]
            c.op("pe", lambda e: e.matmul(obk[:, col0:512], va[vs][:, kb, :], p[:, col0:512],
                                          start=it["first"], stop=it["last"]), reads=[P, VA[vs]], writes=[OBK])
            if it["last"]:
                if hd["kind"] == "diff":
                    if it["ui"] == 0:
                        pend_diff[(it["hi"], qb)] = (obk, OBK)
                    else:
                        oa, OA = pend_diff.pop((it["hi"], qb))
                        par = npair[0] % 2
                        npair[0] += 1
                        while len(deferred) > 1:
                            deferred.pop(0)[1]()
                        diff_stage1(oa, OA, obk, OBK, par)
                        deferred.append((i + DEFER, (lambda h=hd["hh"], qb=qb, par=par: diff_stage2(h, qb, par))))
                else:
                    plain_epilogue(obk, OBK, hd["hh"], qb, use_act=(hd["kind"] == "dil"))

        heads[0]["loads"]()
        heads[1]["loads"]()
        LAG = 1
        PRE = 2
        DEFER = 22
        npair = [0]
        for i in range(min(PRE, len(items))):
            emit_S(items[i])
        pend_diff = {}
        for i, it in enumerate(items):
            hd = it["hd"]
            if i >= LAG and items[i - LAG]["head_start"]:
                nh = items[i - LAG]["hi"] + 1
                if nh >= 2 and nh < len(heads):
                    heads[nh]["loads"]()
            j, col0, sbk, SBK = it["S"]
            qb, kb = it["qb"], it["kb"]
            q0 = qb * 512
            pi = state["pt"] % NPT
            state["pt"] += 1
            p, P = pt[pi], PT[pi]
            if hd["kind"] == "fox":
                bi = it["bi"]
                c.op("act", lambda e: e.activation(out=p[:, col0:512], in_=sbk[:, col0:512], func=AF.Exp,
                                                   scale=hd["scale"], bias=bq[bi][:, kb:kb + 1]),
                     reads=[SBK, BQ[bi]], writes=[P])
            else:
                c.op("act", lambda e: e.activation(out=p[:, col0:512], in_=sbk[:, col0:512], func=AF.Exp,
                                                   scale=hd["scale"]), reads=[SBK], writes=[P])
            if hd["mask"] != "causal":
                off = q0 - kb * 128 + col0
                c.op("pool" if i % 3 == 2 else "dve",
                     lambda e: e.tensor_tensor(out=p[:, col0:512], in0=p[:, col0:512],
                                               in1=wt[:, off:off + 512 - col0], op=ALU.mult),
                     reads=[WT], writes=[P])
            if i + PRE < len(items):
                emit_S(items[i + PRE])
            it["P"] = (p, P, col0)
            if i >= LAG:
                emit_PV(i - LAG)
            while deferred and deferred[0][0] <= i:
                deferred.pop(0)[1]()
        for i2 in range(max(0, len(items) - LAG), len(items)):
            emit_PV(i2)
        while deferred:
            deferred.pop(0)[1]()

        if dbg is not None:
            for cc in range(8):
                c.dma("sp", "st_dbg", dbg[cc * 128:(cc + 1) * 128, :], oT[:, cc, :], reads=[OT])

        groups = [(i, tt, dh) for i in range(NB) for tt in range(4) for dh in range(2)]
        NXR = 4
        xr = xr + [A("xr3", [128, 512], F32)]
        XR = XR + [Buf("xr3")]

        def ld_x(n):
            i, tt, dh = groups[n]
            r0 = (i * 4 + tt) * 128
            c.dma("sp", f"ld_xr{n % NXR}", xr[n % NXR][:], x_src[r0:r0 + 128, dh * 512:(dh + 1) * 512],
                  writes=[XR[n % NXR]])

        ld_x(0)
        ld_x(1)
        for n, (i, tt, dh) in enumerate(groups):
            if n + 2 < len(groups):
                ld_x(n + 2)
            py, PY = g.ps[n % 2], g.PS[n % 2]
            r0 = (i * 4 + tt) * 128
            xs = n % NXR
            c.group("pe", [
                (lambda e, cc=cc: e.matmul(py[:], oT[:, cc, r0:r0 + 128], wo[:, cc, dh * 512:(dh + 1) * 512],
                                           start=(cc == 0), stop=(cc == 7)))
                for cc in range(8)], reads=[OT, WO], writes=[PY])
            c.op("dve", lambda e: e.tensor_tensor(out=xr[xs][:], in0=py[:], in1=xr[xs][:], op=ALU.add),
                 reads=[PY], writes=[XR[xs]])
            c.dma("sp", f"st_xr{xs}", x_dst[r0:r0 + 128, dh * 512:(dh + 1) * 512], xr[xs][:], reads=[XR[xs]])
        c.barrier()


def setup_globals(c, nc, consts):
    g = G()
    g.ps = [nc.alloc_psum_tensor(f"ps{i}", [128, 512], F32) for i in range(7)]
    g.PS = [Buf(f"ps{i}", excl=True) for i in range(7)]
    g.psT = nc.alloc_psum_tensor("psT", [128, 1024], BF16)
    g.PST = Buf("psT", excl=True)
    g.ident = nc.alloc_sbuf_tensor("ident", [128, 128], BF16)
    g.IDENT = Buf("ident")
    g.identf = nc.alloc_sbuf_tensor("identf", [128, 128], F32)
    g.IDENTF = Buf("identf")
    g.ones64 = nc.alloc_sbuf_tensor("ones64", [64, 64], F32)
    g.ones64b = nc.alloc_sbuf_tensor("ones64b", [64, 64], BF16)
    g.ONES64 = Buf("ones64")
    g.eps_norm = nc.alloc_sbuf_tensor("eps_norm", [128, 1], F32)
    g.eps_sub = nc.alloc_sbuf_tensor("eps_sub", [128, 1], F32)
    g.one_c = nc.alloc_sbuf_tensor("one_c", [128, 1], F32)
    g.EPS = Buf("eps")
    c.dma("sp", "ld_c0", g.ident[:], consts["c_ident"], writes=[g.IDENT])
    c.dma("sp", "ld_c1", g.identf[:], consts["c_identf"], writes=[g.IDENTF])
    c.op("dve", lambda e: e.memset(g.eps_norm[:], NORM_EPS), writes=[g.EPS])
    c.op("dve", lambda e: e.memset(g.eps_sub[:], SUBLN_EPS), writes=[g.EPS])
    c.op("dve", lambda e: e.memset(g.one_c[:], 1.0), writes=[g.EPS])
    c.op("dve", lambda e: e.memset(g.ones64[:], 1.0 / 64.0), writes=[g.ONES64])
    c.op("dve", lambda e: e.memset(g.ones64b[:], 1.0 / 64.0), writes=[g.ONES64])
    return g


W_NAMES = ["w_in", "b_f", "lam_q1", "lam_k1", "lam_q2", "lam_k2", "g_sub", "w_o", "g_ffn1", "w1_gate", "w1_up",
           "w1_down", "g_mix", "g_ffn2", "w2_gate", "w2_up", "w2_down", "g_final"]
W_SHAPES = {
    "w_in": [DEPTH, D, INW], "b_f": [DEPTH, 6], "lam_q1": [DEPTH, 32], "lam_k1": [DEPTH, 32], "lam_q2": [DEPTH, 32],
    "lam_k2": [DEPTH, 32], "g_sub": [DEPTH, 64], "w_o": [DEPTH, D, D], "g_ffn1": [DEPTH, D],
    "w1_gate": [DEPTH, D, DFF], "w1_up": [DEPTH, D, DFF], "w1_down": [DEPTH, DFF, D], "g_mix": [DEPTH, D],
    "g_ffn2": [DEPTH, D], "w2_gate": [DEPTH, D, DFF], "w2_up": [DEPTH, D, DFF], "w2_down": [DEPTH, DFF, D],
    "g_final": [D],
}


def const_shapes(S):
    return {"c_ident": ([128, 128], BF16), "c_identf": ([128, 128], F32), "c_tril": ([128, 128], BF16),
            "c_wt": ([128, 2688], BF16), "c_cosA": ([128, S], F32), "c_sinA": ([128, S], F32),
            "c_cosC": ([128, S], F32), "c_sinC": ([128, S], F32)}


def make_consts(S):
    bf = ml_dtypes.bfloat16
    cst = {}
    cst["c_ident"] = np.eye(128, dtype=np.float32).astype(bf)
    cst["c_identf"] = np.eye(128, dtype=np.float32)
    kl = np.arange(128)[:, None]
    ql = np.arange(128)[None, :]
    cst["c_tril"] = np.where(ql >= kl, 0.0, -30000.0).astype(np.float32).astype(bf)
    xx = np.arange(2688)[None, :]
    dl = xx - kl
    wmask = ((dl >= 0) & (dl <= 128)).astype(np.float32) + ((dl >= 0) & (dl <= 512) & (dl % 4 == 0)) \
        + ((dl >= 0) & (dl <= 2048) & (dl % 16 == 0))
    cst["c_wt"] = wmask.astype(np.float32).astype(bf)
    pos = np.arange(S, dtype=np.float32)
    for nm, half in (("A", 16), ("C", 32)):
        inv = (np.float32(10000.0) ** (-(np.arange(half, dtype=np.float32) / np.float32(half)))).astype(np.float32)
        ang = (pos[None, :] * inv[:, None]).astype(np.float32)
        rows = np.arange(128) % half
        a = ang[rows].astype(np.float64)
        cst["c_cos" + nm] = np.cos(a).astype(np.float32)
        cst["c_sin" + nm] = np.sin(a).astype(np.float32)
    return cst


def build_nc(S, layers, final, phases=("ffn1", "proj", "attn", "ffn2"), dbg=False):
    nc = bass.Bass("TRN2", target_bir_lowering=False)
    c = Ctx(nc)
    dt = lambda name, shape, dtype=F32: nc.dram_tensor(name, shape, dtype, kind="ExternalInput").ap()
    x_in = dt("x", [S, D])
    w = {nm: dt(nm, W_SHAPES[nm]) for nm in W_NAMES}
    consts = {nm: dt(nm, shp, dty) for nm, (shp, dty) in const_shapes(S).items()}
    out = nc.dram_tensor("out", [S, D], F32, kind="ExternalOutput").ap()
    dbg_ap = nc.dram_tensor("dbg", [1024, S], BF16, kind="ExternalOutput").ap() if dbg else None
    sc = alloc_scratch(nc, S)
    g = setup_globals(c, nc, consts)
    c.barrier()
    first = True

    def srcs():
        return x_in if first else out

    for l in layers:
        lam_init = 0.8 - 0.6 * math.exp(-0.3 * l)
        if "ffn1" in phases:
            phase_ffn(c, g, w["w1_gate"][l], w["w1_up"][l], w["w1_down"][l], w["g_ffn1"][l], srcs(), None, out, None, S,
                      f"L{l}a_")
            first = False
        if "proj" in phases:
            phase_proj(c, g, sc, w["w_in"][l], w["g_mix"][l], w["b_f"][l], consts, srcs(), S, f"L{l}p_")
        if "attn" in phases:
            phase_attn(c, g, sc, w["w_o"][l], w["lam_q1"][l], w["lam_k1"][l], w["lam_q2"][l], w["lam_k2"][l],
                       w["g_sub"][l], lam_init, consts, srcs(), out, S, f"L{l}m_", dbg=dbg_ap)
            first = False
        if "ffn2" in phases:
            phase_ffn(c, g, w["w2_gate"][l], w["w2_up"][l], w["w2_down"][l], w["g_ffn2"][l], srcs(), None, out, None, S,
                      f"L{l}c_")
            first = False
    if final:
        phase_final(c, g, w["g_final"], srcs(), None, out, None, S, "fin_")
    c.barrier()
    nc._ctx_ninst = c.ninst
    return nc


def kernel(**inputs):
    x = np.ascontiguousarray(inputs["x"], dtype=np.float32)
    B = x.shape[0]
    nc = build_nc(SEQ, list(range(DEPTH)), True)
    shared = {k: np.ascontiguousarray(inputs[k], dtype=np.float32) for k in W_NAMES}
    shared.update(make_consts(SEQ))
    in_maps = [dict(shared, x=x[b]) for b in range(B)]
    res = run_bass_kernel_spmd(nc, in_maps, core_ids=list(range(B)))
    return np.stack([r["out"] for r in res.results], axis=0)
```

```python
import math
from contextlib import ExitStack

import numpy as np
import ml_dtypes

import concourse.bass as bass
import concourse.mybir as mybir
from concourse.bass_utils import run_bass_kernel_spmd

F32 = mybir.dt.float32
BF16 = mybir.dt.bfloat16
AF = mybir.ActivationFunctionType
ALU = mybir.AluOpType

D = 1024
DFF = 2816
NFC = DFF // 128
DEPTH = 4
SEQ = 4096
NCORES = 8
INW = 3078
NORM_EPS = 1e-6
SUBLN_EPS = 1e-5
O_AQ, O_AK, O_AV, O_FQ, O_FK, O_FV, O_FG, O_CQ, O_CK, O_CV = 0, 384, 768, 1152, 1536, 1920, 2304, 2310, 2566, 2822


class Buf:
    def __init__(self, name, excl=False):
        self.name = name
        self.excl = excl
        self.w = {}
        self.r = {}


def _merge(d, tok):
    if tok is None:
        return
    s, v = tok
    if d.get(s, 0) < v:
        d[s] = v


class Ctx:
    def __init__(self, nc):
        self.nc = nc
        self.eng = {"pe": nc.tensor, "act": nc.scalar, "dve": nc.vector, "pool": nc.gpsimd, "sp": nc.sync}
        self.sems = {}
        self.cnt = {}
        self.seen = {}
        self.ninst = 0

    def sem(self, name):
        if name not in self.sems:
            self.sems[name] = self.nc.alloc_semaphore(name)
            self.cnt[name] = 0
        return self.sems[name]

    def wait(self, e, tok):
        if tok is None:
            return
        s, v = tok
        if v <= 0:
            return
        key = (e, s)
        if self.seen.get(key, 0) >= v:
            return
        self.eng[e].wait_ge(self.sems[s], v)
        self.seen[key] = v
        self.ninst += 1

    def _deps(self, e, reads, writes):
        reads = [b for b in reads if b is not None]
        writes = [b for b in writes if b is not None]
        for b in reads:
            for s, v in b.w.items():
                self.wait(e, (s, v))
            if b.excl:
                for s, v in b.r.items():
                    self.wait(e, (s, v))
        for b in writes:
            for s, v in b.w.items():
                self.wait(e, (s, v))
            for s, v in b.r.items():
                self.wait(e, (s, v))

    def _commit(self, tok, reads, writes):
        reads = [b for b in reads if b is not None]
        writes = [b for b in writes if b is not None]
        for b in reads:
            if b.excl:
                b.w = {tok[0]: tok[1]}
                b.r = {}
            else:
                _merge(b.r, tok)
        for b in writes:
            b.w = {tok[0]: tok[1]}
            b.r = {}

    def op(self, e, fn, reads=(), writes=()):
        self._deps(e, reads, writes)
        ins = fn(self.eng[e])
        s = "c_" + e
        self.sem(s)
        self.cnt[s] += 1
        ins.then_inc(self.sems[s], 1)
        self.ninst += 1
        tok = (s, self.cnt[s])
        self._commit(tok, reads, writes)
        return tok

    def group(self, e, fns, reads=(), writes=()):
        self._deps(e, reads, writes)
        ins = None
        for fn in fns:
            ins = fn(self.eng[e])
            self.ninst += 1
        s = "c_" + e
        self.sem(s)
        self.cnt[s] += 1
        ins.then_inc(self.sems[s], 1)
        tok = (s, self.cnt[s])
        self._commit(tok, reads, writes)
        return tok

    def dma(self, e, semname, out, in_, reads=(), writes=(), **kw):
        self.sem(semname)
        self._deps(e, reads, writes)
        self.wait(e, (semname, self.cnt[semname]))
        ins = self.eng[e].dma_start(out=out, in_=in_, **kw)
        self.cnt[semname] += 16
        ins.then_inc(self.sems[semname], 16)
        self.ninst += 1
        tok = (semname, self.cnt[semname])
        self._commit(tok, reads, writes)
        return tok

    def barrier(self, engines=("pe", "act", "dve", "pool", "sp")):
        for e in engines:
            for s, v in self.cnt.items():
                self.wait(e, (s, v))


class G:
    pass


def rmsnorm_rstd(c, g, xt_ap, XT, junk, JUNK, ss, SS, rs, RS):
    c.op("act", lambda e: e.activation(out=junk, in_=xt_ap, func=AF.Square, accum_out=ss),
         reads=[XT], writes=[JUNK, SS])
    c.op("act", lambda e: e.activation(out=rs, in_=ss, func=AF.Sqrt, scale=1.0 / D, bias=g.eps_norm[:, 0:1]),
         reads=[SS], writes=[RS])
    c.op("dve", lambda e: e.reciprocal(out=rs, in_=rs), reads=[], writes=[RS])


def norm_tile(c, g, i, tt, x_src, XSRC, gbc, GBC, t, stages="ABC"):
    n = i * 4 + tt
    s = n % 2
    r0 = n * 128
    if "A" in stages:
        c.dma("sp", f"ld_xt{s}", t.xt[s][:], x_src[r0:r0 + 128, :], reads=[XSRC], writes=[t.XT[s]])
    if "B" in stages:
        c.op("act", lambda e: e.activation(out=t.junk[:], in_=t.xt[s][:], func=AF.Square, accum_out=t.ss[s][:]),
             reads=[t.XT[s]], writes=[t.JUNK, t.SS[s]])
        c.op("act", lambda e: e.activation(out=t.rs[s][:], in_=t.ss[s][:], func=AF.Sqrt, scale=1.0 / D,
                                           bias=g.eps_norm[:, 0:1]), reads=[t.SS[s]], writes=[t.RS[s]])
    if "C" in stages:
        c.op("dve", lambda e: e.reciprocal(out=t.rs[s][:], in_=t.rs[s][:]), reads=[], writes=[t.RS[s]])
        c.op("dve", lambda e: e.scalar_tensor_tensor(out=t.xn[tt][:], in0=t.xt[s][:], scalar=t.rs[s][:, 0:1],
                                                     in1=gbc[:], op0=ALU.mult, op1=ALU.mult),
             reads=[t.XT[s], t.RS[s], GBC], writes=[t.XN[tt]])


def transpose_tile(c, g, tt, t, xnT, XNT):
    if tt % 2 == 0:
        pst, PSTB = g.psT[:], g.PST
    else:
        pst, PSTB = g.ps[6][:].bitcast(BF16), g.PS[6]
    c.group("pe", [
        (lambda e, kc=kc: e.transpose(out=pst[:, kc * 128:(kc + 1) * 128],
                                      in_=t.xn[tt][:, kc * 128:(kc + 1) * 128], identity=g.ident[:]))
        for kc in range(8)], reads=[t.XN[tt], g.IDENT], writes=[PSTB])
    src = pst.rearrange("p (k t) -> p k t", k=8)
    dst = xnT[:, :, tt * 128:(tt + 1) * 128]
    if tt % 2 == 0:
        c.op("act", lambda e: e.copy(out=dst, in_=src), reads=[PSTB], writes=[XNT])
    else:
        c.op("dve", lambda e: e.tensor_copy(out=dst, in_=src), reads=[PSTB], writes=[XNT])


def norm_to_xnT(c, g, i, x_src, XSRC, gbc, GBC, t, xnT, XNT, S):
    for tt in range(4):
        norm_tile(c, g, i, tt, x_src, XSRC, gbc, GBC, t)
        transpose_tile(c, g, tt, t, xnT, XNT)


class T:
    pass


def alloc_norm_tiles(nc, st, pfx):
    t = T()
    A = lambda name, shape, dt: st.enter_context(nc.sbuf_tensor(pfx + name, shape, dt))
    t.xt = [A(f"xt{i}", [128, D], F32) for i in range(2)]
    t.XT = [Buf(f"xt{i}") for i in range(2)]
    t.xn = [A(f"xn{i}", [128, D], BF16) for i in range(4)]
    t.XN = [Buf(f"xn{i}") for i in range(4)]
    t.junk = A("junk", [128, D], BF16)
    t.JUNK = Buf("junk")
    t.ss = [A(f"ss{i}", [128, 1], F32) for i in range(2)]
    t.SS = [Buf(f"ss{i}") for i in range(2)]
    t.rs = [A(f"rs{i}", [128, 1], F32) for i in range(2)]
    t.RS = [Buf(f"rs{i}") for i in range(2)]
    return t


def phase_ffn(c, g, wg_d, wu_d, wd_d, gvec_d, x_src, XSRC, x_dst, XDST, S, pfx):
    nc = c.nc
    NB = S // 512
    with ExitStack() as st:
        A = lambda name, shape, dt: st.enter_context(nc.sbuf_tensor(pfx + name, shape, dt))
        wg = A("wg", [128, 8, DFF], BF16)
        wu = A("wu", [128, 8, DFF], BF16)
        wd = A("wd", [128, NFC, D], BF16)
        gbc = A("gbc", [128, D], F32)
        GBC = Buf("gbc")
        t = alloc_norm_tiles(nc, st, pfx)
        xnT = [A(f"xnT{i}", [128, 8, 512], BF16) for i in range(2)]
        XNT = [Buf(f"xnT{i}") for i in range(2)]
        hT = A("hT", [128, NFC, 512], BF16)
        HT = Buf("hT")
        sg = [A(f"sg{i}", [128, 512], F32) for i in range(2)]
        SG = [Buf(f"sg{i}") for i in range(2)]
        xr = [A(f"xr{i}", [128, 512], F32) for i in range(3)]
        XR = [Buf(f"xr{i}") for i in range(3)]

        bounds = [0, 128, 384, 768, 1280, 1920, 2368, DFF]
        NG = len(bounds) - 1
        WG = [Buf(f"wg{j}") for j in range(NG)]
        WU = [Buf(f"wu{j}") for j in range(NG)]
        WD = [Buf(f"wd{j}") for j in range(2)]
        c.dma("sp", "ld_g", gbc[:], gvec_d.partition_broadcast(128), writes=[GBC])
        wg_v = wg_d.rearrange("(kc p) f -> p kc f", p=128)
        wu_v = wu_d.rearrange("(kc p) f -> p kc f", p=128)
        wd_v = wd_d.rearrange("(fc p) d -> p fc d", p=128)
        k = 0
        for j in range(NG):
            lo, hi = bounds[j], bounds[j + 1]
            c.dma("pool", f"ld_w{k % 4}", wg[:, :, lo:hi], wg_v[:, :, lo:hi], writes=[WG[j]])
            k += 1
            c.dma("pool", f"ld_w{k % 4}", wu[:, :, lo:hi], wu_v[:, :, lo:hi], writes=[WU[j]])
            k += 1
            if j == 3:
                for jj in range(2):
                    c.dma("pool", f"ld_w{k % 4}", wd[:, jj * 11:(jj + 1) * 11, :], wd_v[:, jj * 11:(jj + 1) * 11, :],
                          writes=[WD[jj]])
                    k += 1

        def grp(col):
            for j in range(NG):
                if bounds[j] <= col < bounds[j + 1]:
                    return j

        norm_to_xnT(c, g, 0, x_src, XSRC, gbc, GBC, t, xnT[0], XNT[0], S)
        for i in range(NB):
            b = i % 2
            for fc in range(NFC):
                pg, PG = g.ps[fc % 2], g.PS[fc % 2]
                pu, PU = g.ps[2 + fc % 2], g.PS[2 + fc % 2]
                j = grp(fc * 128)
                j2 = grp(fc * 128 + 127)
                wbufs_g = [WG[j]] + ([WG[j2]] if j2 != j else [])
                wbufs_u = [WU[j]] + ([WU[j2]] if j2 != j else [])
                c.group("pe", [
                    (lambda e, kc=kc: e.matmul(pg[:], wg[:, kc, fc * 128:(fc + 1) * 128], xnT[b][:, kc, :],
                                               start=(kc == 0), stop=(kc == 7)))
                    for kc in range(8)], reads=[XNT[b]] + wbufs_g, writes=[PG])
                c.group("pe", [
                    (lambda e, kc=kc: e.matmul(pu[:], wu[:, kc, fc * 128:(fc + 1) * 128], xnT[b][:, kc, :],
                                               start=(kc == 0), stop=(kc == 7)))
                    for kc in range(8)], reads=[XNT[b]] + wbufs_u, writes=[PU])
                c.op("act", lambda e: e.activation(out=sg[fc % 2][:], in_=pg[:], func=AF.Silu),
                     reads=[PG], writes=[SG[fc % 2]])
                c.op("dve", lambda e: e.tensor_tensor(out=hT[:, fc, :], in0=sg[fc % 2][:], in1=pu[:], op=ALU.mult),
                     reads=[SG[fc % 2], PU], writes=[HT])
                if i + 1 < NB:
                    if fc in (0, 5, 10, 15):
                        norm_tile(c, g, i + 1, fc // 5, x_src, XSRC, gbc, GBC, t, stages="A")
                    if fc in (2, 7, 12, 17):
                        norm_tile(c, g, i + 1, (fc - 2) // 5, x_src, XSRC, gbc, GBC, t, stages="B")
                    if fc in (4, 9, 14, 19):
                        norm_tile(c, g, i + 1, (fc - 4) // 5, x_src, XSRC, gbc, GBC, t, stages="C")
            if i + 1 < NB:
                for tt in range(4):
                    transpose_tile(c, g, tt, t, xnT[1 - b], XNT[1 - b])
            n = 0
            for tt in range(4):
                for dh in range(2):
                    py, PY = g.ps[4 + n % 2], g.PS[4 + n % 2]
                    r0 = (i * 4 + tt) * 128
                    xs = (i * 8 + n) % 3
                    c.dma("sp", f"ld_xr{xs}", xr[xs][:], x_src[r0:r0 + 128, dh * 512:(dh + 1) * 512],
                          reads=[XSRC], writes=[XR[xs]])
                    c.group("pe", [
                        (lambda e, fc=fc: e.matmul(py[:], hT[:, fc, tt * 128:(tt + 1) * 128],
                                                   wd[:, fc, dh * 512:(dh + 1) * 512],
                                                   start=(fc == 0), stop=(fc == NFC - 1)))
                        for fc in range(NFC)], reads=[HT, WD[0], WD[1]], writes=[PY])
                    c.op("dve", lambda e: e.scalar_tensor_tensor(out=xr[xs][:], in0=py[:], scalar=0.5, in1=xr[xs][:],
                                                                 op0=ALU.mult, op1=ALU.add),
                         reads=[PY], writes=[XR[xs]])
                    c.dma("sp", f"st_xr{xs}", x_dst[r0:r0 + 128, dh * 512:(dh + 1) * 512], xr[xs][:],
                          reads=[XR[xs]], writes=[XDST])
                    n += 1
        c.barrier()


def phase_final(c, g, gvec_d, x_src, XSRC, x_dst, XDST, S, pfx):
    nc = c.nc
    with ExitStack() as st:
        A = lambda name, shape, dt: st.enter_context(nc.sbuf_tensor(pfx + name, shape, dt))
        gbc = A("gbc", [128, D], F32)
        GBC = Buf("gbc")
        t = alloc_norm_tiles(nc, st, pfx)
        yo = [A(f"yo{i}", [128, D], F32) for i in range(2)]
        YO = [Buf(f"yo{i}") for i in range(2)]
        c.dma("sp", "ld_g", gbc[:], gvec_d.partition_broadcast(128), writes=[GBC])
        for n in range(S // 128):
            s = n % 2
            r0 = n * 128
            c.dma("sp", f"ld_xt{s}", t.xt[s][:], x_src[r0:r0 + 128, :], reads=[XSRC], writes=[t.XT[s]])
            rmsnorm_rstd(c, g, t.xt[s][:], t.XT[s], t.junk[:], t.JUNK, t.ss[s][:], t.SS[s], t.rs[s][:], t.RS[s])
            c.op("dve", lambda e: e.scalar_tensor_tensor(out=yo[s][:], in0=t.xt[s][:], scalar=t.rs[s][:, 0:1],
                                                         in1=gbc[:], op0=ALU.mult, op1=ALU.mult),
                 reads=[t.XT[s], t.RS[s], GBC], writes=[YO[s]])
            c.dma("sp", f"st_yo{s}", x_dst[r0:r0 + 128, :], yo[s][:], reads=[YO[s]], writes=[XDST])
        c.barrier()


def alloc_scratch(nc, S):
    sc = T()
    d = lambda name, shape, dtype: nc.dram_tensor(name, shape, dtype).ap()
    sc.qTd = d("s_qTd", [384, S], BF16)
    sc.kTd = d("s_kTd", [384, S], BF16)
    sc.qTf = d("s_qTf", [6, 65, S], BF16)
    sc.kTf = d("s_kTf", [6, 64, S], BF16)
    sc.qTc = d("s_qTc", [256, S], BF16)
    sc.kTc = d("s_kTc", [256, S], BF16)
    sc.v = d("s_v", [S, 1024], BF16)
    sc.ckm = d("s_ckm", [128, (S // 128) * 6], F32)
    sc.cend = d("s_cend", [6, S // 512], F32)
    return sc


WQ_AQ, WQ_AK, WQ_FQ, WQ_FK, WQ_CQ, WQ_CK = 0, 384, 768, 1152, 1536, 1792
WQ_RAQ, WQ_RAK, WQ_RCQ, WQ_RCK = 2048, 2432, 2816, 3072
WQ_FG = 3328
WQ_COLS = 3334


def phase_proj(c, g, sc, w_in_d, gvec_d, bf_d, consts, x_src, S, pfx):
    nc = c.nc
    NB = S // 512
    NT = S // 128
    with ExitStack() as st:
        A = lambda name, shape, dt: st.enter_context(nc.sbuf_tensor(pfx + name, shape, dt))
        wq = A("wq", [128, 8, WQ_COLS], BF16)
        wv = A("wv", [128, 8, 1024], BF16)
        gbc = A("gbc", [128, D], F32)
        GBC = Buf("gbc")
        t = alloc_norm_tiles(nc, st, pfx)
        xnT = [A(f"xnT{i}", [128, 8, 512], BF16) for i in range(2)]
        XNT = [Buf(f"xnT{i}") for i in range(2)]
        rt = [[A(f"rt{i}_{j}", [128, 512], F32) for j in range(4)] for i in range(2)]
        RT = [[Buf(f"rt{i}_{j}") for j in range(4)] for i in range(2)]
        tm1 = [A(f"tm1_{i}", [128, 512], F32) for i in range(2)]
        TM1 = [Buf(f"tm1_{i}") for i in range(2)]
        tm2 = [A(f"tm2_{i}", [128, 512], F32) for i in range(2)]
        TM2 = [Buf(f"tm2_{i}") for i in range(2)]
        ob = [A(f"ob{i}", [128, 512], BF16) for i in range(3)]
        OB = [Buf(f"ob{i}") for i in range(3)]
        vb = [A(f"vb{i}", [128, 1024], BF16) for i in range(2)]
        VB = [Buf(f"vb{i}") for i in range(2)]
        nbf = A("nbf", [6, 1], F32)
        NBF = Buf("nbf")
        uu = A("uu", [6, 512], F32)
        UU = Buf("uu")
        lf = A("lf", [6, 512], F32)
        LF = Buf("lf")
        cb = [A(f"cb{i}", [6, 512], F32) for i in range(2)]
        CB = [Buf(f"cb{i}") for i in range(2)]
        augb = A("augb", [6, 512], BF16)
        AUGB = Buf("augb")
        ones6 = A("ones6", [6, 512], F32)
        ONES6 = Buf("ones6")
        ckm = A("ckm", [128, NT, 6], F32)
        CKM = Buf("ckm")

        WQP = [Buf(f"wqp{j}") for j in range(4)]
        WQR = {0: Buf("wqr0"), 2: Buf("wqr2")}
        WV = [Buf(f"wv{j}") for j in range(3)]
        wv_ = w_in_d.rearrange("(kc p) f -> p kc f", p=128)
        c.dma("sp", "ld_g", gbc[:], gvec_d.partition_broadcast(128), writes=[GBC])
        c.dma("sp", "ld_c1", nbf[:], bf_d.rearrange("(p o) -> p o", o=1), writes=[NBF])
        c.op("dve", lambda e: e.tensor_scalar(out=nbf[:], in0=nbf[:], scalar1=-1.0, scalar2=None, op0=ALU.mult),
             reads=[], writes=[NBF])
        c.op("dve", lambda e: e.memset(ones6[:], 1.0), writes=[ONES6])
        k = 0
        for (pj, dst, srcc, n) in [(3, WQ_FG, O_FG, 6), (0, WQ_AQ, O_AQ, 768), (1, WQ_FQ, O_FQ, 768),
                                   (2, WQ_CQ, O_CQ, 512)]:
            c.dma("pool", f"ld_w{k % 4}", wq[:, :, dst:dst + n], wv_[:, :, srcc:srcc + n], writes=[WQP[pj]])
            k += 1
        for j, (dst, srcc, n) in enumerate([(0, O_AV, 384), (384, O_FV, 384), (768, O_CV, 256)]):
            c.dma("pool", f"ld_w{k % 4}", wv[:, :, dst:dst + n], wv_[:, :, srcc:srcc + n], writes=[WV[j]])
            k += 1
        for (pj, dst, srcc, n, half) in [(0, WQ_RAQ, WQ_AQ, 768, 16), (2, WQ_RCQ, WQ_CQ, 512, 32)]:
            for kc in range(8):
                sv = wq[:, kc, srcc:srcc + n].rearrange("p (u e) -> p u e", e=2 * half)
                dv = wq[:, kc, dst:dst + n].rearrange("p (u e) -> p u e", e=2 * half)
                if kc % 2 == 0:
                    c.op("dve", lambda e: e.tensor_scalar(out=dv[:, :, 0:half], in0=sv[:, :, half:2 * half],
                                                          scalar1=-1.0, scalar2=None, op0=ALU.mult),
                         reads=[WQP[pj]], writes=[WQR[pj]])
                    c.op("dve", lambda e: e.tensor_copy(out=dv[:, :, half:2 * half], in_=sv[:, :, 0:half]),
                         reads=[WQP[pj]], writes=[WQR[pj]])
                else:
                    c.op("act", lambda e: e.mul(out=dv[:, :, 0:half], in_=sv[:, :, half:2 * half], mul=-1.0),
                         reads=[WQP[pj]], writes=[WQR[pj]])
                    c.op("act", lambda e: e.copy(out=dv[:, :, half:2 * half], in_=sv[:, :, 0:half]),
                         reads=[WQP[pj]], writes=[WQR[pj]])

        chunks = []
        for ci in range(16):
            col = ci * 128
            if ci < 6:
                chunks.append(("diff", col, WQ_RAQ + col, 0, 0))
            elif ci < 9:
                chunks.append(("fq", col, None, None, 1))
            elif ci < 12:
                chunks.append(("fk", col, None, None, 1))
            else:
                chunks.append(("dil", col, WQ_RCQ + (col - WQ_CQ), 2, 2))

        nob = 0

        def ld_rope(i):
            for j, nm in enumerate(["c_cosA", "c_sinA", "c_cosC", "c_sinC"]):
                c.dma("sp", f"ld_rt{j}", rt[i % 2][j][:], consts[nm][:, i * 512:(i + 1) * 512], writes=[RT[i % 2][j]])

        ld_rope(0)
        norm_to_xnT(c, g, 0, x_src, None, gbc, GBC, t, xnT[0], XNT[0], S)
        for i in range(NB):
            b = i % 2
            t0 = i * 512
            pf, PF = g.ps[6], g.PS[6]
            c.group("pe", [
                (lambda e, kc=kc: e.matmul(pf[0:6, :], wq[:, kc, WQ_FG:WQ_FG + 6], xnT[b][:, kc, :],
                                           start=(kc == 0), stop=(kc == 7)))
                for kc in range(8)], reads=[XNT[b], WQP[3]], writes=[PF])
            c.op("act", lambda e: e.activation(out=uu[:], in_=pf[0:6, :], func=AF.Exp, scale=-1.0, bias=nbf[:, 0:1]),
                 reads=[PF, NBF], writes=[UU])
            c.op("act", lambda e: e.activation(out=lf[:], in_=uu[:], func=AF.Ln, scale=1.0, bias=g.one_c[0:6, 0:1]),
                 reads=[UU], writes=[LF])
            init = 0.0 if i == 0 else cb[1 - b][:, 511:512]
            c.op("dve", lambda e: e.tensor_tensor_scan(out=cb[b][:], data0=ones6[:], data1=lf[:], initial=init,
                                                       op0=ALU.mult, op1=ALU.add),
                 reads=[LF, ONES6] + ([CB[1 - b]] if i > 0 else []), writes=[CB[b]])
            c.op("dve", lambda e: e.tensor_scalar(out=augb[:], in0=cb[b][:], scalar1=cb[b][:, 511:512], scalar2=-1.0,
                                                  op0=ALU.subtract, op1=ALU.mult),
                 reads=[CB[b]], writes=[AUGB])
            c.dma("sp", "st_aug", sc.qTf[:, 64, t0:t0 + 512], augb[:], reads=[AUGB])
            c.dma("sp", "st_cend", sc.cend[:, i:i + 1], cb[b][:, 511:512], reads=[CB[b]], allow_slow_non_contiguous=True)

            def fg_transposes():
                c.group("pe", [
                    (lambda e, tt=tt: e.transpose(out=pf[:, tt * 6:(tt + 1) * 6],
                                                  in_=cb[b][0:6, tt * 128:(tt + 1) * 128],
                                                  identity=g.identf[0:6, 0:6]))
                    for tt in range(4)], reads=[CB[b], g.IDENTF], writes=[PF])
                c.op("dve", lambda e: e.tensor_copy(out=ckm[:, i * 4:(i + 1) * 4, :],
                                                    in_=pf[:, 0:24].rearrange("p (t h) -> p t h", h=6)),
                     reads=[PF], writes=[CKM])
            for ci, (kind, col, rcol, rti, pj) in enumerate(chunks):
                if ci == 1 and i + 1 < NB:
                    ld_rope(i + 1)
                if ci == 4:
                    fg_transposes()
                if i + 1 < NB:
                    if ci % 4 == 0:
                        norm_tile(c, g, i + 1, ci // 4, x_src, None, gbc, GBC, t, stages="A")
                    if ci % 4 == 1:
                        norm_tile(c, g, i + 1, ci // 4, x_src, None, gbc, GBC, t, stages="B")
                    if ci % 4 == 3:
                        norm_tile(c, g, i + 1, ci // 4, x_src, None, gbc, GBC, t, stages="C")
                pa, PA = g.ps[ci % 2], g.PS[ci % 2]
                c.group("pe", [
                    (lambda e, kc=kc: e.matmul(pa[:], wq[:, kc, col:col + 128], xnT[b][:, kc, :],
                                               start=(kc == 0), stop=(kc == 7)))
                    for kc in range(8)], reads=[XNT[b], WQP[pj]], writes=[PA])
                o = nob % 3
                nob += 1
                if rcol is not None:
                    pb_, PB_ = g.ps[2 + ci % 2], g.PS[2 + ci % 2]
                    c.group("pe", [
                        (lambda e, kc=kc: e.matmul(pb_[:], wq[:, kc, rcol:rcol + 128], xnT[b][:, kc, :],
                                                   start=(kc == 0), stop=(kc == 7)))
                        for kc in range(8)], reads=[XNT[b], WQR[pj]], writes=[PB_])
                    s2 = ci % 2
                    c.op("dve", lambda e: e.tensor_tensor(out=tm1[s2][:], in0=pa[:], in1=rt[b][rti][:], op=ALU.mult),
                         reads=[PA, RT[b][rti]], writes=[TM1[s2]])
                    c.op("dve", lambda e: e.tensor_tensor(out=tm2[s2][:], in0=pb_[:], in1=rt[b][rti + 1][:], op=ALU.mult),
                         reads=[PB_, RT[b][rti + 1]], writes=[TM2[s2]])
                    c.op("pool", lambda e: e.tensor_tensor(out=ob[o][:], in0=tm1[s2][:], in1=tm2[s2][:], op=ALU.add),
                         reads=[TM1[s2], TM2[s2]], writes=[OB[o]])
                elif kind == "fq":
                    c.op("act", lambda e: e.activation(out=ob[o][:], in_=pa[:], func=AF.Copy, scale=0.125),
                         reads=[PA], writes=[OB[o]])
                else:
                    c.op("act", lambda e: e.copy(out=ob[o][:], in_=pa[:]), reads=[PA], writes=[OB[o]])
                if kind == "diff":
                    dstt = sc.qTd if ci < 3 else sc.kTd
                    r = (ci % 3) * 128
                    c.dma("sp", f"st_ob{o}", dstt[r:r + 128, t0:t0 + 512], ob[o][:], reads=[OB[o]])
                elif kind == "dil":
                    dstt = sc.qTc if ci < 14 else sc.kTc
                    r = (ci % 2) * 128
                    c.dma("sp", f"st_ob{o}", dstt[r:r + 128, t0:t0 + 512], ob[o][:], reads=[OB[o]])
                else:
                    dstt = sc.qTf if kind == "fq" else sc.kTf
                    h0 = 2 * ((ci - 6) % 3)
                    c.dma("sp", f"st_ob{o}", dstt[h0, 0:64, t0:t0 + 512], ob[o][0:64, :], reads=[OB[o]])
                    c.dma("sp", f"st_ob{o}b", dstt[h0 + 1, 0:64, t0:t0 + 512], ob[o][64:128, :], reads=[OB[o]])
            for tt in range(4):
                n = i * 4 + tt
                s = n % 2
                for hf in range(2):
                    pv, PV = g.ps[4 + hf], g.PS[4 + hf]
                    c.group("pe", [
                        (lambda e, kc=kc: e.matmul(pv[:], xnT[b][:, kc, tt * 128:(tt + 1) * 128],
                                                   wv[:, kc, hf * 512:(hf + 1) * 512],
                                                   start=(kc == 0), stop=(kc == 7)))
                        for kc in range(8)], reads=[XNT[b]] + WV, writes=[PV])
                    if hf == 0:
                        c.op("act", lambda e: e.copy(out=vb[s][:, 0:512], in_=pv[:]), reads=[PV], writes=[VB[s]])
                    else:
                        c.op("dve", lambda e: e.tensor_copy(out=vb[s][:, 512:1024], in_=pv[:]), reads=[PV], writes=[VB[s]])
                c.dma("sp", f"st_vb{s}", sc.v[n * 128:(n + 1) * 128, :], vb[s][:], reads=[VB[s]])
            if i + 1 < NB:
                for tt in range(4):
                    transpose_tile(c, g, tt, t, xnT[1 - b], XNT[1 - b])
        c.dma("sp", "st_ckm", sc.ckm, ckm[:].rearrange("p t h -> p (t h)"), reads=[CKM])
        c.barrier()


def phase_attn(c, g, sc, wo_d, lamq1_d, lamk1_d, lamq2_d, lamk2_d, gsub_d, lam_init, consts, x_src, x_dst, S, pfx,
               dbg=None):
    nc = c.nc
    NB = S // 512
    NT = S // 128
    with ExitStack() as st:
        A = lambda name, shape, dt: st.enter_context(nc.sbuf_tensor(pfx + name, shape, dt))
        oT = A("oT", [128, 8, S], BF16)
        OT = Buf("oT")
        wo = A("wo", [128, 8, D], BF16)
        WO = Buf("wo")
        qs = [A(f"qs{i}", [128, S], BF16) for i in range(2)]
        QS = [Buf(f"qs{i}") for i in range(2)]
        qz = [A(f"qz{i}", [128, S], BF16) for i in range(4)]
        QZ = [Buf(f"qz{i}") for i in range(4)]
        ks = [A(f"ks{i}", [128, S], BF16) for i in range(2)]
        KS = [Buf(f"ks{i}") for i in range(2)]
        va = [A(f"va{i}", [128, NT, 128], BF16) for i in range(2)]
        VA = [Buf(f"va{i}") for i in range(2)]
        NPT = 5
        pt = [A(f"pt{i}", [128, 512], BF16) for i in range(NPT)]
        PT = [Buf(f"pt{i}") for i in range(NPT)]
        tril = A("tril", [128, 128], BF16)
        TRIL = Buf("tril")
        wt = A("wt", [128, 2688], BF16)
        WT = Buf("wt")
        ckm = A("ckm", [128, NT, 6], F32)
        CKM = Buf("ckm")
        cend = A("cend", [128, 6 * NB], F32)
        CEND = Buf("cend")
        bq = [A(f"bq{i}", [128, NT], F32) for i in range(2)]
        BQ = [Buf(f"bq{i}") for i in range(2)]
        rec = [A(f"rec{i}", [64, 512], F32) for i in range(2)]
        REC = [Buf(f"rec{i}") for i in range(2)]
        ta = A("ta", [64, 512], F32)
        TA = Buf("ta")
        tb = A("tb", [64, 512], F32)
        TB = Buf("tb")
        td2 = [A(f"td{i}", [64, 512], F32) for i in range(2)]
        TD2 = [Buf(f"td{i}") for i in range(2)]
        tsq2 = [A(f"tsq{i}", [64, 512], BF16) for i in range(2)]
        TSQ2 = [Buf(f"tsq{i}") for i in range(2)]
        tr2 = [A(f"tr{i}", [64, 512], F32) for i in range(2)]
        TR2 = [Buf(f"tr{i}") for i in range(2)]
        lv = [A(f"lv{i}", [64, 32], F32) for i in range(4)]
        LV = [Buf(f"lv{i}") for i in range(4)]
        lj = A("lj", [64, 32], F32)
        LJ = Buf("lj")
        sm = A("sm", [64, 8], F32)
        SM = Buf("sm")
        xr = [A(f"xr{i}", [128, 512], F32) for i in range(3)]
        XR = [Buf(f"xr{i}") for i in range(3)]

        c.dma("sp", "ld_c0", tril[:], consts["c_tril"], writes=[TRIL])
        c.dma("sp", "ld_c1", wt[:], consts["c_wt"], writes=[WT])
        c.dma("sp", "ld_c2", ckm[:].rearrange("p t h -> p (t h)"), sc.ckm, writes=[CKM])
        c.dma("sp", "ld_c3", cend[:], sc.cend.rearrange("h q -> (h q)").partition_broadcast(128), writes=[CEND])
        wo_v = wo_d.rearrange("(cc p) d -> p cc d", p=128)
        c.dma("pool", "ld_w0", wo[:], wo_v, writes=[WO])
        for j, dd in enumerate([lamq1_d, lamk1_d, lamq2_d, lamk2_d]):
            c.dma("sp", f"ld_lv{j}", lv[j][:], dd.partition_broadcast(64), writes=[LV[j]])
        c.dma("sp", "ld_c4", sm[:, 5:6], gsub_d.rearrange("(p o) -> p o", o=1), writes=[SM])
        for j in range(2):
            c.op("dve", lambda e: e.scalar_tensor_tensor(out=lj[:], in0=lv[2 * j][:], scalar=1.0, in1=lv[2 * j + 1][:],
                                                         op0=ALU.mult, op1=ALU.mult, accum_out=sm[:, j:j + 1]),
                 reads=[LV[2 * j], LV[2 * j + 1]], writes=[LJ, SM])
        c.op("act", lambda e: e.activation(out=sm[:, 2:4], in_=sm[:, 0:2], func=AF.Exp), reads=[], writes=[SM])
        c.op("dve", lambda e: e.tensor_tensor(out=sm[:, 4:5], in0=sm[:, 2:3], in1=sm[:, 3:4], op=ALU.subtract),
             reads=[], writes=[SM])
        c.op("dve", lambda e: e.tensor_scalar(out=sm[:, 4:5], in0=sm[:, 4:5], scalar1=float(lam_init), scalar2=-1.0,
                                              op0=ALU.add, op1=ALU.mult), reads=[], writes=[SM])
        c.op("dve", lambda e: e.tensor_scalar(out=sm[:, 5:6], in0=sm[:, 5:6], scalar1=float(1.0 - lam_init),
                                              scalar2=None, op0=ALU.mult), reads=[], writes=[SM])
        for i in range(2):
            c.op("pool", lambda e: e.memset(va[i][:, :, 64:128], 1.0), writes=[VA[i]])
        for i in range(4):
            c.op("pool" if i % 2 == 0 else "dve", lambda e: e.memset(qz[i][:], 0.0), writes=[QZ[i]])
        pm = g.psT[:].bitcast(F32)
        PM = g.PST

        vsrc = sc.v.rearrange("(kb p) c -> p kb c", p=128)
        state = {"sb": 0, "pt": 0, "ob": 0, "rec": 0, "bq": 0}
        SCALE_D = 32 ** -0.5
        nhalf = A("nhalf", [64, 512], F32)
        NHALF = Buf("nhalf")
        c.op("pool", lambda e: e.memset(nhalf[:], -0.5), writes=[NHALF])

        heads = []
        for h in range(6):
            cd, hl = h // 2, h % 2
            ksl = cd % 2

            def loads(h=h, cd=cd, hl=hl, ksl=ksl):
                if hl == 0:
                    c.dma("sp", f"ld_ks{ksl}", ks[ksl][:], sc.kTd[cd * 128:(cd + 1) * 128, :], writes=[KS[ksl]])
                for pz in (2 * hl, 2 * hl + 1):
                    r = cd * 128 + pz * 32
                    c.dma("sp", f"ld_qz{pz}", qz[pz][pz * 32:(pz + 1) * 32, :], sc.qTd[r:r + 32, :], writes=[QZ[pz]])
                c.dma("sp", f"ld_va{h % 2}", va[h % 2][:, :, 0:64], vsrc[:, :, h * 64:(h + 1) * 64], writes=[VA[h % 2]])
            heads.append(dict(kind="diff", hh=h, vs=h % 2, loads=loads, K=128, scale=SCALE_D, mask="causal",
                              units=[(qz[2 * hl], QZ[2 * hl], ks[ksl], KS[ksl]),
                                     (qz[2 * hl + 1], QZ[2 * hl + 1], ks[ksl], KS[ksl])]))
        for h in range(6):
            sl = (6 + h) % 2
            ksl = (3 + h) % 2

            def loads(h=h, sl=sl, ksl=ksl):
                c.dma("sp", f"ld_qs{sl}", qs[sl][0:65, :], sc.qTf[h], writes=[QS[sl]])
                c.dma("sp", f"ld_ks{ksl}", ks[ksl][0:64, :], sc.kTf[h], writes=[KS[ksl]])
                c.op("pool", lambda e: e.memset(ks[ksl][64:65, :], 1.0), writes=[KS[ksl]])
                c.dma("sp", f"ld_va{sl}", va[sl][:, :, 0:64], vsrc[:, :, (6 + h) * 64:(7 + h) * 64], writes=[VA[sl]])
            heads.append(dict(kind="fox", hh=6 + h, fh=h, vs=sl, loads=loads, K=65, scale=1.0, mask="causal",
                              units=[(qs[sl], QS[sl], ks[ksl], KS[ksl])]))
        for h in range(4):
            cc, hl = h // 2, h % 2
            ksl = (9 + cc) % 2
            zi = 0 if hl == 0 else 3
            sl = (12 + h) % 2

            def loads(h=h, cc=cc, hl=hl, ksl=ksl, zi=zi, sl=sl):
                if hl == 0:
                    c.dma("sp", f"ld_ks{ksl}", ks[ksl][:], sc.kTc[cc * 128:(cc + 1) * 128, :], writes=[KS[ksl]])
                r = cc * 128 + hl * 64
                c.dma("sp", f"ld_qz{zi}", qz[zi][hl * 64:(hl + 1) * 64, :], sc.qTc[r:r + 64, :], writes=[QZ[zi]])
                c.dma("sp", f"ld_va{sl}", va[sl][:, :, 0:64], vsrc[:, :, (12 + h) * 64:(13 + h) * 64], writes=[VA[sl]])
            heads.append(dict(kind="dil", hh=12 + h, vs=sl, loads=loads, K=128, scale=0.125, mask="dil",
                              units=[(qz[zi], QZ[zi], ks[ksl], KS[ksl])]))

        items = []
        for hi, hd in enumerate(heads):
            for qb in range(NB):
                for ui, un in enumerate(hd["units"]):
                    kb_lo = max(0, 4 * qb - 16) if hd["kind"] == "dil" else 0
                    kbs = list(range(kb_lo, 4 * qb + 4))
                    for n_, kb in enumerate(kbs):
                        items.append(dict(hi=hi, hd=hd, qb=qb, ui=ui, un=un, kb=kb, first=(n_ == 0),
                                          last=(n_ == len(kbs) - 1), head_start=(qb == 0 and ui == 0 and n_ == 0)))

        deferred = []

        def emit_S(it):
            hd = it["hd"]
            qt, QTB, kt, KTB = it["un"]
            qb, kb = it["qb"], it["kb"]
            q0 = qb * 512
            if it["first"] and hd["kind"] == "fox":
                bi = state["bq"] % 2
                state["bq"] += 1
                nk = 4 * qb + 4
                fh = hd["fh"]
                c.op("dve", lambda e: e.tensor_scalar(out=bq[bi][:, 0:nk], in0=ckm[:, 0:nk, fh],
                                                      scalar1=cend[:, fh * NB + qb:fh * NB + qb + 1], scalar2=None,
                                                      op0=ALU.subtract), reads=[CKM, CEND], writes=[BQ[bi]])
                hd["bi"] = bi
            if hd["kind"] == "fox":
                it["bi"] = hd["bi"]
            j = kb - 4 * qb
            col0 = max(0, j) * 128
            si = state["sb"] % 3
            state["sb"] += 1
            sbk, SBK = g.ps[si], g.PS[si]
            K = hd["K"]
            if hd["mask"] == "causal" and j >= 0:
                c.group("pe", [
                    lambda e: e.matmul(sbk[:, col0:512], kt[0:K, kb * 128:(kb + 1) * 128],
                                       qt[0:K, q0 + col0:q0 + 512], start=True, stop=False),
                    lambda e: e.matmul(sbk[:, col0:col0 + 128], g.ident[:], tril[:], start=False, stop=True),
                ], reads=[QTB, KTB, TRIL, g.IDENT], writes=[SBK])
            else:
                c.op("pe", lambda e: e.matmul(sbk[:, col0:512], kt[0:K, kb * 128:(kb + 1) * 128],
                                              qt[0:K, q0 + col0:q0 + 512], start=True, stop=True),
                     reads=[QTB, KTB], writes=[SBK])
            it["S"] = (j, col0, sbk, SBK)

        def plain_epilogue(obk, OBK, hh, qb, use_act=False):
            q0 = qb * 512
            ri = state["rec"] % 2
            state["rec"] += 1
            pb = (hh % 2) * 64
            if use_act:
                c.op("act", lambda e: e.activation(out=rec[ri][:], in_=obk[64:128, :], func=AF.Ln),
                     reads=[OBK], writes=[REC[ri]])
                c.op("act", lambda e: e.activation(out=rec[ri][:], in_=rec[ri][:], func=AF.Exp, scale=-1.0),
                     reads=[], writes=[REC[ri]])
            else:
                c.op("dve", lambda e: e.reciprocal(out=rec[ri][:], in_=obk[64:128, :]), reads=[OBK],
                     writes=[REC[ri]])
            c.op("dve", lambda e: e.tensor_tensor(out=oT[pb:pb + 64, hh // 2, q0:q0 + 512], in0=obk[0:64, :],
                                                  in1=rec[ri][:], op=ALU.mult), reads=[OBK, REC[ri]], writes=[OT])

        def diff_stage1(oa, OA, o2, O2, par):
            td, TD, tsq, TSQ = td2[par], TD2[par], tsq2[par], TSQ2[par]
            c.op("dve", lambda e: e.reciprocal(out=rec[0][:], in_=oa[64:128, :]), reads=[OA], writes=[REC[0]])
            c.op("dve", lambda e: e.tensor_tensor(out=ta[:], in0=oa[0:64, :], in1=rec[0][:], op=ALU.mult),
                 reads=[OA, REC[0]], writes=[TA])
            c.op("dve", lambda e: e.reciprocal(out=rec[1][:], in_=o2[64:128, :]), reads=[O2], writes=[REC[1]])
            c.op("dve", lambda e: e.tensor_tensor(out=tb[:], in0=o2[0:64, :], in1=rec[1][:], op=ALU.mult),
                 reads=[O2, REC[1]], writes=[TB])
            c.op("dve", lambda e: e.scalar_tensor_tensor(out=td[:], in0=tb[:], scalar=sm[:, 4:5], in1=ta[:],
                                                         op0=ALU.mult, op1=ALU.add), reads=[TA, TB, SM], writes=[TD])
            c.op("pool", lambda e: e.tensor_tensor(out=tsq[:], in0=td[:], in1=td[:], op=ALU.mult),
                 reads=[TD], writes=[TSQ])

        def diff_stage2(h, qb, par):
            td, TD, tsq, TSQ, tr, TR = td2[par], TD2[par], tsq2[par], TSQ2[par], tr2[par], TR2[par]
            q0 = qb * 512
            pb = (h % 2) * 64
            c.op("pe", lambda e: e.matmul(pm[0:64, :], g.ones64b[0:64, 0:64], tsq[:], start=True, stop=True),
                 reads=[TSQ, g.ONES64], writes=[PM])
            c.op("act", lambda e: e.activation(out=tr[:], in_=pm[0:64, :], func=AF.Ln, scale=1.0,
                                               bias=g.eps_sub[0:64, 0:1]), reads=[PM], writes=[TR])
            c.op("act", lambda e: e.activation(out=tr[:], in_=tr[:], func=AF.Exp, scale=-0.5), reads=[], writes=[TR])
            c.op("dve", lambda e: e.scalar_tensor_tensor(out=oT[pb:pb + 64, h // 2, q0:q0 + 512], in0=td[:],
                                                         scalar=sm[:, 5:6], in1=tr[:], op0=ALU.mult, op1=ALU.mult),
                 reads=[TD, TR, SM], writes=[OT])

        def emit_PV(i):
            it = items[i]
            hd = it["hd"]
            qb, kb = it["qb"], it["kb"]
            p, P, col0 = it["P"]
            if it["first"]:
                obi = 3 + state["ob"] % 4
                state["ob"] += 1
                hd["cur_o"] = (g.ps[obi], g.PS[obi])
            obk, OBK = hd["cur_o"]
            vs = hd["vs"]
            c.op("pe", lambda e: e.matmul(obk[:, col0:512], va[vs][:, kb, :], p[:, col0:512],
                                          start=it["first"], stop=it["last"]), reads=[P, VA[vs]], writes=[OBK])
            if it["last"]:
                if hd["kind"] == "diff":
                    if it["ui"] == 0:
                        pend_diff[(it["hi"], qb)] = (obk, OBK)
                    else:
                        oa, OA = pend_diff.pop((it["hi"], qb))
                        par = npair[0] % 2
                        npair[0] += 1
                        while len(deferred) > 1:
                            deferred.pop(0)[1]()
                        diff_stage1(oa, OA, obk, OBK, par)
                        deferred.append((i + DEFER, (lambda h=hd["hh"], qb=qb, par=par: diff_stage2(h, qb, par))))
                else:
                    plain_epilogue(obk, OBK, hd["hh"], qb, use_act=(hd["kind"] == "dil"))

        heads[0]["loads"]()
        heads[1]["loads"]()
        LAG = 1
        PRE = 2
        DEFER = 22
        npair = [0]
        for i in range(min(PRE, len(items))):
            emit_S(items[i])
        pend_diff = {}
        for i, it in enumerate(items):
            hd = it["hd"]
            if i >= LAG and items[i - LAG]["head_start"]:
                nh = items[i - LAG]["hi"] + 1
                if nh >= 2 and nh < len(heads):
                    heads[nh]["loads"]()
            j, col0, sbk, SBK = it["S"]
            qb, kb = it["qb"], it["kb"]
            q0 = qb * 512
            pi = state["pt"] % NPT
            state["pt"] += 1
            p, P = pt[pi], PT[pi]
            if hd["kind"] == "fox":
                bi = it["bi"]
                c.op("act", lambda e: e.activation(out=p[:, col0:512], in_=sbk[:, col0:512], func=AF.Exp,
                                                   scale=hd["scale"], bias=bq[bi][:, kb:kb + 1]),
                     reads=[SBK, BQ[bi]], writes=[P])
            else:
                c.op("act", lambda e: e.activation(out=p[:, col0:512], in_=sbk[:, col0:512], func=AF.Exp,
                                                   scale=hd["scale"]), reads=[SBK], writes=[P])
            if hd["mask"] != "causal":
                off = q0 - kb * 128 + col0
                c.op("pool" if i % 3 == 2 else "dve",
                     lambda e: e.tensor_tensor(out=p[:, col0:512], in0=p[:, col0:512],
                                               in1=wt[:, off:off + 512 - col0], op=ALU.mult),
                     reads=[WT], writes=[P])
            if i + PRE < len(items):
                emit_S(items[i + PRE])
            it["P"] = (p, P, col0)
            if i >= LAG:
                emit_PV(i - LAG)
            while deferred and deferred[0][0] <= i:
                deferred.pop(0)[1]()
        for i2 in range(max(0, len(items) - LAG), len(items)):
            emit_PV(i2)
        while deferred:
            deferred.pop(0)[1]()

        if dbg is not None:
            for cc in range(8):
                c.dma("sp", "st_dbg", dbg[cc * 128:(cc + 1) * 128, :], oT[:, cc, :], reads=[OT])

        groups = [(i, tt, dh) for i in range(NB) for tt in range(4) for dh in range(2)]
        NXR = 4
        xr = xr + [A("xr3", [128, 512], F32)]
        XR = XR + [Buf("xr3")]

        def ld_x(n):
            i, tt, dh = groups[n]
            r0 = (i * 4 + tt) * 128
            c.dma("sp", f"ld_xr{n % NXR}", xr[n % NXR][:], x_src[r0:r0 + 128, dh * 512:(dh + 1) * 512],
                  writes=[XR[n % NXR]])

        ld_x(0)
        ld_x(1)
        for n, (i, tt, dh) in enumerate(groups):
            if n + 2 < len(groups):
                ld_x(n + 2)
            py, PY = g.ps[n % 2], g.PS[n % 2]
            r0 = (i * 4 + tt) * 128
            xs = n % NXR
            c.group("pe", [
                (lambda e, cc=cc: e.matmul(py[:], oT[:, cc, r0:r0 + 128], wo[:, cc, dh * 512:(dh + 1) * 512],
                                           start=(cc == 0), stop=(cc == 7)))
                for cc in range(8)], reads=[OT, WO], writes=[PY])
            c.op("dve", lambda e: e.tensor_tensor(out=xr[xs][:], in0=py[:], in1=xr[xs][:], op=ALU.add),
                 reads=[PY], writes=[XR[xs]])
            c.dma("sp", f"st_xr{xs}", x_dst[r0:r0 + 128, dh * 512:(dh + 1) * 512], xr[xs][:], reads=[XR[xs]])
        c.barrier()


def setup_globals(c, nc, consts):
    g = G()
    g.ps = [nc.alloc_psum_tensor(f"ps{i}", [128, 512], F32) for i in range(7)]
    g.PS = [Buf(f"ps{i}", excl=True) for i in range(7)]
    g.psT = nc.alloc_psum_tensor("psT", [128, 1024], BF16)
    g.PST = Buf("psT", excl=True)
    g.ident = nc.alloc_sbuf_tensor("ident", [128, 128], BF16)
    g.IDENT = Buf("ident")
    g.identf = nc.alloc_sbuf_tensor("identf", [128, 128], F32)
    g.IDENTF = Buf("identf")
    g.ones64 = nc.alloc_sbuf_tensor("ones64", [64, 64], F32)
    g.ones64b = nc.alloc_sbuf_tensor("ones64b", [64, 64], BF16)
    g.ONES64 = Buf("ones64")
    g.eps_norm = nc.alloc_sbuf_tensor("eps_norm", [128, 1], F32)
    g.eps_sub = nc.alloc_sbuf_tensor("eps_sub", [128, 1], F32)
    g.one_c = nc.alloc_sbuf_tensor("one_c", [128, 1], F32)
    g.EPS = Buf("eps")
    c.dma("sp", "ld_c0", g.ident[:], consts["c_ident"], writes=[g.IDENT])
    c.dma("sp", "ld_c1", g.identf[:], consts["c_identf"], writes=[g.IDENTF])
    c.op("dve", lambda e: e.memset(g.eps_norm[:], NORM_EPS), writes=[g.EPS])
    c.op("dve", lambda e: e.memset(g.eps_sub[:], SUBLN_EPS), writes=[g.EPS])
    c.op("dve", lambda e: e.memset(g.one_c[:], 1.0), writes=[g.EPS])
    c.op("dve", lambda e: e.memset(g.ones64[:], 1.0 / 64.0), writes=[g.ONES64])
    c.op("dve", lambda e: e.memset(g.ones64b[:], 1.0 / 64.0), writes=[g.ONES64])
    return g


W_NAMES = ["w_in", "b_f", "lam_q1", "lam_k1", "lam_q2", "lam_k2", "g_sub", "w_o", "g_ffn1", "w1_gate", "w1_up",
           "w1_down", "g_mix", "g_ffn2", "w2_gate", "w2_up", "w2_down", "g_final"]
W_SHAPES = {
    "w_in": [DEPTH, D, INW], "b_f": [DEPTH, 6], "lam_q1": [DEPTH, 32], "lam_k1": [DEPTH, 32], "lam_q2": [DEPTH, 32],
    "lam_k2": [DEPTH, 32], "g_sub": [DEPTH, 64], "w_o": [DEPTH, D, D], "g_ffn1": [DEPTH, D],
    "w1_gate": [DEPTH, D, DFF], "w1_up": [DEPTH, D, DFF], "w1_down": [DEPTH, DFF, D], "g_mix": [DEPTH, D],
    "g_ffn2": [DEPTH, D], "w2_gate": [DEPTH, D, DFF], "w2_up": [DEPTH, D, DFF], "w2_down": [DEPTH, DFF, D],
    "g_final": [D],
}


def const_shapes(S):
    return {"c_ident": ([128, 128], BF16), "c_identf": ([128, 128], F32), "c_tril": ([128, 128], BF16),
            "c_wt": ([128, 2688], BF16), "c_cosA": ([128, S], F32), "c_sinA": ([128, S], F32),
            "c_cosC": ([128, S], F32), "c_sinC": ([128, S], F32)}


def make_consts(S):
    bf = ml_dtypes.bfloat16
    cst = {}
    cst["c_ident"] = np.eye(128, dtype=np.float32).astype(bf)
    cst["c_identf"] = np.eye(128, dtype=np.float32)
    kl = np.arange(128)[:, None]
    ql = np.arange(128)[None, :]
    cst["c_tril"] = np.where(ql >= kl, 0.0, -30000.0).astype(np.float32).astype(bf)
    xx = np.arange(2688)[None, :]
    dl = xx - kl
    wmask = ((dl >= 0) & (dl <= 128)).astype(np.float32) + ((dl >= 0) & (dl <= 512) & (dl % 4 == 0)) \
        + ((dl >= 0) & (dl <= 2048) & (dl % 16 == 0))
    cst["c_wt"] = wmask.astype(np.float32).astype(bf)
    pos = np.arange(S, dtype=np.float32)
    for nm, half in (("A", 16), ("C", 32)):
        inv = (np.float32(10000.0) ** (-(np.arange(half, dtype=np.float32) / np.float32(half)))).astype(np.float32)
        ang = (pos[None, :] * inv[:, None]).astype(np.float32)
        rows = np.arange(128) % half
        a = ang[rows].astype(np.float64)
        cst["c_cos" + nm] = np.cos(a).astype(np.float32)
        cst["c_sin" + nm] = np.sin(a).astype(np.float32)
    return cst


def build_nc(S, layers, final, phases=("ffn1", "proj", "attn", "ffn2"), dbg=False):
    nc = bass.Bass("TRN2", target_bir_lowering=False)
    c = Ctx(nc)
    dt = lambda name, shape, dtype=F32: nc.dram_tensor(name, shape, dtype, kind="ExternalInput").ap()
    x_in = dt("x", [S, D])
    w = {nm: dt(nm, W_SHAPES[nm]) for nm in W_NAMES}
    consts = {nm: dt(nm, shp, dty) for nm, (shp, dty) in const_shapes(S).items()}
    out = nc.dram_tensor("out", [S, D], F32, kind="ExternalOutput").ap()
    dbg_ap = nc.dram_tensor("dbg", [1024, S], BF16, kind="ExternalOutput").ap() if dbg else None
    sc = alloc_scratch(nc, S)
    g = setup_globals(c, nc, consts)
    c.barrier()
    first = True

    def srcs():
        return x_in if first else out

    for l in layers:
        lam_init = 0.8 - 0.6 * math.exp(-0.3 * l)
        if "ffn1" in phases:
            phase_ffn(c, g, w["w1_gate"][l], w["w1_up"][l], w["w1_down"][l], w["g_ffn1"][l], srcs(), None, out, None, S,
                      f"L{l}a_")
            first = False
        if "proj" in phases:
            phase_proj(c, g, sc, w["w_in"][l], w["g_mix"][l], w["b_f"][l], consts, srcs(), S, f"L{l}p_")
        if "attn" in phases:
            phase_attn(c, g, sc, w["w_o"][l], w["lam_q1"][l], w["lam_k1"][l], w["lam_q2"][l], w["lam_k2"][l],
                       w["g_sub"][l], lam_init, consts, srcs(), out, S, f"L{l}m_", dbg=dbg_ap)
            first = False
        if "ffn2" in phases:
            phase_ffn(c, g, w["w2_gate"][l], w["w2_up"][l], w["w2_down"][l], w["g_ffn2"][l], srcs(), None, out, None, S,
                      f"L{l}c_")
            first = False
    if final:
        phase_final(c, g, w["g_final"], srcs(), None, out, None, S, "fin_")
    c.barrier()
    nc._ctx_ninst = c.ninst
    return nc


def kernel(**inputs):
    x = np.ascontiguousarray(inputs["x"], dtype=np.float32)
    B = x.shape[0]
    nc = build_nc(SEQ, list(range(DEPTH)), True)
    shared = {k: np.ascontiguousarray(inputs[k], dtype=np.float32) for k in W_NAMES}
    shared.update(make_consts(SEQ))
    in_maps = [dict(shared, x=x[b]) for b in range(B)]
    res = run_bass_kernel_spmd(nc, in_maps, core_ids=list(range(B)))
    return np.stack([r["out"] for r in res.results], axis=0)
```

```python
import math
from contextlib import ExitStack

import numpy as np
import ml_dtypes

import concourse.bass as bass
import concourse.mybir as mybir
from concourse.bass_utils import run_bass_kernel_spmd

F32 = mybir.dt.float32
BF16 = mybir.dt.bfloat16
AF = mybir.ActivationFunctionType
ALU = mybir.AluOpType

D = 1024
DFF = 2816
NFC = DFF // 128
DEPTH = 4
SEQ = 4096
NCORES = 8
INW = 3078
NORM_EPS = 1e-6
SUBLN_EPS = 1e-5
O_AQ, O_AK, O_AV, O_FQ, O_FK, O_FV, O_FG, O_CQ, O_CK, O_CV = 0, 384, 768, 1152, 1536, 1920, 2304, 2310, 2566, 2822


ATT_NSB = 4
ATT_PAIR = False
ATT_PRE = 2
ATT_LAG = 2
DIL_POOL_EVERY = 10 ** 9


class Buf:
    def __init__(self, name, excl=False):
        self.name = name
        self.excl = excl
        self.w = {}
        self.r = {}


def _merge(d, tok):
    if tok is None:
        return
    s, v = tok
    if d.get(s, 0) < v:
        d[s] = v


class Ctx:
    def __init__(self, nc):
        self.nc = nc
        self.eng = {"pe": nc.tensor, "act": nc.scalar, "dve": nc.vector, "pool": nc.gpsimd, "sp": nc.sync}
        self.sems = {}
        self.cnt = {}
        self.seen = {}
        self.ninst = 0

    def sem(self, name):
        if name not in self.sems:
            self.sems[name] = self.nc.alloc_semaphore(name)
            self.cnt[name] = 0
        return self.sems[name]

    def wait(self, e, tok):
        if tok is None:
            return
        s, v = tok
        if v <= 0:
            return
        key = (e, s)
        if self.seen.get(key, 0) >= v:
            return
        self.eng[e].wait_ge(self.sems[s], v)
        self.seen[key] = v
        self.ninst += 1

    def _deps(self, e, reads, writes):
        reads = [b for b in reads if b is not None]
        writes = [b for b in writes if b is not None]
        for b in reads:
            for s, v in b.w.items():
                self.wait(e, (s, v))
            if b.excl:
                for s, v in b.r.items():
                    self.wait(e, (s, v))
        for b in writes:
            for s, v in b.w.items():
                self.wait(e, (s, v))
            for s, v in b.r.items():
                self.wait(e, (s, v))

    def _commit(self, tok, reads, writes):
        reads = [b for b in reads if b is not None]
        writes = [b for b in writes if b is not None]
        for b in reads:
            if b.excl:
                b.w = {tok[0]: tok[1]}
                b.r = {}
            else:
                _merge(b.r, tok)
        for b in writes:
            b.w = {tok[0]: tok[1]}
            b.r = {}

    def op(self, e, fn, reads=(), writes=()):
        self._deps(e, reads, writes)
        ins = fn(self.eng[e])
        s = "c_" + e
        self.sem(s)
        self.cnt[s] += 1
        ins.then_inc(self.sems[s], 1)
        self.ninst += 1
        tok = (s, self.cnt[s])
        self._commit(tok, reads, writes)
        return tok

    def group(self, e, fns, reads=(), writes=()):
        self._deps(e, reads, writes)
        ins = None
        for fn in fns:
            ins = fn(self.eng[e])
            self.ninst += 1
        s = "c_" + e
        self.sem(s)
        self.cnt[s] += 1
        ins.then_inc(self.sems[s], 1)
        tok = (s, self.cnt[s])
        self._commit(tok, reads, writes)
        return tok

    def dma(self, e, semname, out, in_, reads=(), writes=(), **kw):
        self.sem(semname)
        self._deps(e, reads, writes)
        self.wait(e, (semname, self.cnt[semname]))
        ins = self.eng[e].dma_start(out=out, in_=in_, **kw)
        self.cnt[semname] += 16
        ins.then_inc(self.sems[semname], 16)
        self.ninst += 1
        tok = (semname, self.cnt[semname])
        self._commit(tok, reads, writes)
        return tok

    def barrier(self, engines=("pe", "act", "dve", "pool", "sp")):
        for e in engines:
            for s, v in self.cnt.items():
                self.wait(e, (s, v))


class G:
    pass


def rmsnorm_rstd(c, g, xt_ap, XT, junk, JUNK, ss, SS, rs, RS):
    c.op("act", lambda e: e.activation(out=junk, in_=xt_ap, func=AF.Square, accum_out=ss),
         reads=[XT], writes=[JUNK, SS])
    c.op("act", lambda e: e.activation(out=rs, in_=ss, func=AF.Sqrt, scale=1.0 / D, bias=g.eps_norm[:, 0:1]),
         reads=[SS], writes=[RS])
    c.op("dve", lambda e: e.reciprocal(out=rs, in_=rs), reads=[], writes=[RS])


def norm_tile(c, g, i, tt, x_src, XSRC, gbc, GBC, t, stages="ABC"):
    n = i * 4 + tt
    s = n % 2
    r0 = n * 128
    if "A" in stages:
        c.dma("sp", f"ld_xt{s}", t.xt[s][:], x_src[r0:r0 + 128, :], reads=[XSRC], writes=[t.XT[s]])
    if "B" in stages:
        c.op("act", lambda e: e.activation(out=t.junk[:], in_=t.xt[s][:], func=AF.Square, accum_out=t.ss[s][:]),
             reads=[t.XT[s]], writes=[t.JUNK, t.SS[s]])
        c.op("act", lambda e: e.activation(out=t.rs[s][:], in_=t.ss[s][:], func=AF.Sqrt, scale=1.0 / D,
                                           bias=g.eps_norm[:, 0:1]), reads=[t.SS[s]], writes=[t.RS[s]])
    if "C" in stages:
        c.op("dve", lambda e: e.reciprocal(out=t.rs[s][:], in_=t.rs[s][:]), reads=[], writes=[t.RS[s]])
        c.op("dve", lambda e: e.scalar_tensor_tensor(out=t.xn[tt][:], in0=t.xt[s][:], scalar=t.rs[s][:, 0:1],
                                                     in1=gbc[:], op0=ALU.mult, op1=ALU.mult),
             reads=[t.XT[s], t.RS[s], GBC], writes=[t.XN[tt]])


def transpose_tile(c, g, tt, t, xnT, XNT):
    if tt % 2 == 0:
        pst, PSTB = g.psT[:], g.PST
    else:
        pst, PSTB = g.ps[6][:].bitcast(BF16), g.PS[6]
    c.group("pe", [
        (lambda e, kc=kc: e.transpose(out=pst[:, kc * 128:(kc + 1) * 128],
                                      in_=t.xn[tt][:, kc * 128:(kc + 1) * 128], identity=g.ident[:]))
        for kc in range(8)], reads=[t.XN[tt], g.IDENT], writes=[PSTB])
    src = pst.rearrange("p (k t) -> p k t", k=8)
    dst = xnT[:, :, tt * 128:(tt + 1) * 128]
    if tt % 2 == 0:
        c.op("act", lambda e: e.copy(out=dst, in_=src), reads=[PSTB], writes=[XNT])
    else:
        c.op("dve", lambda e: e.tensor_copy(out=dst, in_=src), reads=[PSTB], writes=[XNT])


def norm_to_xnT(c, g, i, x_src, XSRC, gbc, GBC, t, xnT, XNT, S):
    for tt in range(4):
        norm_tile(c, g, i, tt, x_src, XSRC, gbc, GBC, t)
        transpose_tile(c, g, tt, t, xnT, XNT)


class T:
    pass


def alloc_norm_tiles(nc, st, pfx):
    t = T()
    A = lambda name, shape, dt: st.enter_context(nc.sbuf_tensor(pfx + name, shape, dt))
    t.xt = [A(f"xt{i}", [128, D], F32) for i in range(2)]
    t.XT = [Buf(f"xt{i}") for i in range(2)]
    t.xn = [A(f"xn{i}", [128, D], BF16) for i in range(4)]
    t.XN = [Buf(f"xn{i}") for i in range(4)]
    t.junk = A("junk", [128, D], BF16)
    t.JUNK = Buf("junk")
    t.ss = [A(f"ss{i}", [128, 1], F32) for i in range(2)]
    t.SS = [Buf(f"ss{i}") for i in range(2)]
    t.rs = [A(f"rs{i}", [128, 1], F32) for i in range(2)]
    t.RS = [Buf(f"rs{i}") for i in range(2)]
    return t


def phase_ffn(c, g, wg_d, wu_d, wd_d, gvec_d, x_src, XSRC, x_dst, XDST, S, pfx):
    nc = c.nc
    NB = S // 512
    with ExitStack() as st:
        A = lambda name, shape, dt: st.enter_context(nc.sbuf_tensor(pfx + name, shape, dt))
        wg = A("wg", [128, 8, DFF], BF16)
        wu = A("wu", [128, 8, DFF], BF16)
        wd = A("wd", [128, NFC, D], BF16)
        gbc = A("gbc", [128, D], F32)
        GBC = Buf("gbc")
        t = alloc_norm_tiles(nc, st, pfx)
        xnT = [A(f"xnT{i}", [128, 8, 512], BF16) for i in range(2)]
        XNT = [Buf(f"xnT{i}") for i in range(2)]
        hT = A("hT", [128, NFC, 512], BF16)
        HT = Buf("hT")
        sg = [A(f"sg{i}", [128, 512], F32) for i in range(2)]
        SG = [Buf(f"sg{i}") for i in range(2)]
        xr = [A(f"xr{i}", [128, 512], F32) for i in range(3)]
        XR = [Buf(f"xr{i}") for i in range(3)]

        bounds = [0, 128, 384, 768, 1280, 1920, 2368, DFF]
        NG = len(bounds) - 1
        WG = [Buf(f"wg{j}") for j in range(NG)]
        WU = [Buf(f"wu{j}") for j in range(NG)]
        WD = [Buf(f"wd{j}") for j in range(2)]
        c.dma("sp", "ld_g", gbc[:], gvec_d.partition_broadcast(128), writes=[GBC])
        wg_v = wg_d.rearrange("(kc p) f -> p kc f", p=128)
        wu_v = wu_d.rearrange("(kc p) f -> p kc f", p=128)
        wd_v = wd_d.rearrange("(fc p) d -> p fc d", p=128)
        k = 0
        for j in range(NG):
            lo, hi = bounds[j], bounds[j + 1]
            c.dma("pool", f"ld_w{k % 4}", wg[:, :, lo:hi], wg_v[:, :, lo:hi], writes=[WG[j]])
            k += 1
            c.dma("pool", f"ld_w{k % 4}", wu[:, :, lo:hi], wu_v[:, :, lo:hi], writes=[WU[j]])
            k += 1
            if j == 3:
                for jj in range(2):
                    c.dma("pool", f"ld_w{k % 4}", wd[:, jj * 11:(jj + 1) * 11, :], wd_v[:, jj * 11:(jj + 1) * 11, :],
                          writes=[WD[jj]])
                    k += 1

        def grp(col):
            for j in range(NG):
                if bounds[j] <= col < bounds[j + 1]:
                    return j

        norm_to_xnT(c, g, 0, x_src, XSRC, gbc, GBC, t, xnT[0], XNT[0], S)
        for i in range(NB):
            b = i % 2
            for fc in range(NFC):
                pg, PG = g.ps[fc % 2], g.PS[fc % 2]
                pu, PU = g.ps[2 + fc % 2], g.PS[2 + fc % 2]
                j = grp(fc * 128)
                j2 = grp(fc * 128 + 127)
                wbufs_g = [WG[j]] + ([WG[j2]] if j2 != j else [])
                wbufs_u = [WU[j]] + ([WU[j2]] if j2 != j else [])
                c.group("pe", [
                    (lambda e, kc=kc: e.matmul(pg[:], wg[:, kc, fc * 128:(fc + 1) * 128], xnT[b][:, kc, :],
                                               start=(kc == 0), stop=(kc == 7)))
                    for kc in range(8)], reads=[XNT[b]] + wbufs_g, writes=[PG])
                c.group("pe", [
                    (lambda e, kc=kc: e.matmul(pu[:], wu[:, kc, fc * 128:(fc + 1) * 128], xnT[b][:, kc, :],
                                               start=(kc == 0), stop=(kc == 7)))
                    for kc in range(8)], reads=[XNT[b]] + wbufs_u, writes=[PU])
                c.op("act", lambda e: e.activation(out=sg[fc % 2][:], in_=pg[:], func=AF.Silu),
                     reads=[PG], writes=[SG[fc % 2]])
                c.op("dve", lambda e: e.tensor_tensor(out=hT[:, fc, :], in0=sg[fc % 2][:], in1=pu[:], op=ALU.mult),
                     reads=[SG[fc % 2], PU], writes=[HT])
                if i + 1 < NB:
                    if fc in (0, 5, 10, 15):
                        norm_tile(c, g, i + 1, fc // 5, x_src, XSRC, gbc, GBC, t, stages="A")
                    if fc in (2, 7, 12, 17):
                        norm_tile(c, g, i + 1, (fc - 2) // 5, x_src, XSRC, gbc, GBC, t, stages="B")
                    if fc in (4, 9, 14, 19):
                        norm_tile(c, g, i + 1, (fc - 4) // 5, x_src, XSRC, gbc, GBC, t, stages="C")
            if i + 1 < NB:
                for tt in range(4):
                    transpose_tile(c, g, tt, t, xnT[1 - b], XNT[1 - b])
            n = 0
            for tt in range(4):
                for dh in range(2):
                    py, PY = g.ps[4 + n % 2], g.PS[4 + n % 2]
                    r0 = (i * 4 + tt) * 128
                    xs = (i * 8 + n) % 3
                    c.dma("sp", f"ld_xr{xs}", xr[xs][:], x_src[r0:r0 + 128, dh * 512:(dh + 1) * 512],
                          reads=[XSRC], writes=[XR[xs]])
                    c.group("pe", [
                        (lambda e, fc=fc: e.matmul(py[:], hT[:, fc, tt * 128:(tt + 1) * 128],
                                                   wd[:, fc, dh * 512:(dh + 1) * 512],
                                                   start=(fc == 0), stop=(fc == NFC - 1)))
                        for fc in range(NFC)], reads=[HT, WD[0], WD[1]], writes=[PY])
                    c.op("dve", lambda e: e.scalar_tensor_tensor(out=xr[xs][:], in0=py[:], scalar=0.5, in1=xr[xs][:],
                                                                 op0=ALU.mult, op1=ALU.add),
                         reads=[PY], writes=[XR[xs]])
                    c.dma("sp", f"st_xr{xs}", x_dst[r0:r0 + 128, dh * 512:(dh + 1) * 512], xr[xs][:],
                          reads=[XR[xs]], writes=[XDST])
                    n += 1
        c.barrier()


def phase_final(c, g, gvec_d, x_src, XSRC, x_dst, XDST, S, pfx):
    nc = c.nc
    with ExitStack() as st:
        A = lambda name, shape, dt: st.enter_context(nc.sbuf_tensor(pfx + name, shape, dt))
        gbc = A("gbc", [128, D], F32)
        GBC = Buf("gbc")
        t = alloc_norm_tiles(nc, st, pfx)
        yo = [A(f"yo{i}", [128, D], F32) for i in range(2)]
        YO = [Buf(f"yo{i}") for i in range(2)]
        c.dma("sp", "ld_g", gbc[:], gvec_d.partition_broadcast(128), writes=[GBC])
        for n in range(S // 128):
            s = n % 2
            r0 = n * 128
            c.dma("sp", f"ld_xt{s}", t.xt[s][:], x_src[r0:r0 + 128, :], reads=[XSRC], writes=[t.XT[s]])
            rmsnorm_rstd(c, g, t.xt[s][:], t.XT[s], t.junk[:], t.JUNK, t.ss[s][:], t.SS[s], t.rs[s][:], t.RS[s])
            c.op("dve", lambda e: e.scalar_tensor_tensor(out=yo[s][:], in0=t.xt[s][:], scalar=t.rs[s][:, 0:1],
                                                         in1=gbc[:], op0=ALU.mult, op1=ALU.mult),
                 reads=[t.XT[s], t.RS[s], GBC], writes=[YO[s]])
            c.dma("sp", f"st_yo{s}", x_dst[r0:r0 + 128, :], yo[s][:], reads=[YO[s]], writes=[XDST])
        c.barrier()


def alloc_scratch(nc, S):
    sc = T()
    d = lambda name, shape, dtype: nc.dram_tensor(name, shape, dtype).ap()
    sc.qTd = d("s_qTd", [384, S], BF16)
    sc.kTd = d("s_kTd", [384, S], BF16)
    sc.qTf = d("s_qTf", [6, 65, S], BF16)
    sc.kTf = d("s_kTf", [6, 64, S], BF16)
    sc.qTc = d("s_qTc", [256, S], BF16)
    sc.kTc = d("s_kTc", [256, S], BF16)
    sc.v = d("s_v", [S, 1024], BF16)
    sc.ckm = d("s_ckm", [128, (S // 128) * 6], F32)
    sc.cend = d("s_cend", [6, S // 512], F32)
    return sc


WQ_AQ, WQ_AK, WQ_FQ, WQ_FK, WQ_CQ, WQ_CK = 0, 384, 768, 1152, 1536, 1792
WQ_RAQ, WQ_RAK, WQ_RCQ, WQ_RCK = 2048, 2432, 2816, 3072
WQ_FG = 3328
WQ_COLS = 3334


def phase_proj(c, g, sc, w_in_d, gvec_d, bf_d, consts, x_src, S, pfx):
    nc = c.nc
    NB = S // 512
    NT = S // 128
    with ExitStack() as st:
        A = lambda name, shape, dt: st.enter_context(nc.sbuf_tensor(pfx + name, shape, dt))
        wq = A("wq", [128, 8, WQ_COLS], BF16)
        wv = A("wv", [128, 8, 1024], BF16)
        gbc = A("gbc", [128, D], F32)
        GBC = Buf("gbc")
        t = alloc_norm_tiles(nc, st, pfx)
        xnT = [A(f"xnT{i}", [128, 8, 512], BF16) for i in range(2)]
        XNT = [Buf(f"xnT{i}") for i in range(2)]
        rt = [[A(f"rt{i}_{j}", [128, 512], F32) for j in range(4)] for i in range(2)]
        RT = [[Buf(f"rt{i}_{j}") for j in range(4)] for i in range(2)]
        tm1 = [A(f"tm1_{i}", [128, 512], F32) for i in range(2)]
        TM1 = [Buf(f"tm1_{i}") for i in range(2)]
        tm2 = [A(f"tm2_{i}", [128, 512], F32) for i in range(2)]
        TM2 = [Buf(f"tm2_{i}") for i in range(2)]
        ob = [A(f"ob{i}", [128, 512], BF16) for i in range(3)]
        OB = [Buf(f"ob{i}") for i in range(3)]
        vb = [A(f"vb{i}", [128, 1024], BF16) for i in range(2)]
        VB = [Buf(f"vb{i}") for i in range(2)]
        nbf = A("nbf", [6, 1], F32)
        NBF = Buf("nbf")
        uu = A("uu", [6, 512], F32)
        UU = Buf("uu")
        lf = A("lf", [6, 512], F32)
        LF = Buf("lf")
        cb = [A(f"cb{i}", [6, 512], F32) for i in range(2)]
        CB = [Buf(f"cb{i}") for i in range(2)]
        augb = A("augb", [6, 512], BF16)
        AUGB = Buf("augb")
        ones6 = A("ones6", [6, 512], F32)
        ONES6 = Buf("ones6")
        ckm = A("ckm", [128, NT, 6], F32)
        CKM = Buf("ckm")

        WQP = [Buf(f"wqp{j}") for j in range(4)]
        WQR = {0: Buf("wqr0"), 2: Buf("wqr2")}
        WV = [Buf(f"wv{j}") for j in range(3)]
        wv_ = w_in_d.rearrange("(kc p) f -> p kc f", p=128)
        c.dma("sp", "ld_g", gbc[:], gvec_d.partition_broadcast(128), writes=[GBC])
        c.dma("sp", "ld_c1", nbf[:], bf_d.rearrange("(p o) -> p o", o=1), writes=[NBF])
        c.op("dve", lambda e: e.tensor_scalar(out=nbf[:], in0=nbf[:], scalar1=-1.0, scalar2=None, op0=ALU.mult),
             reads=[], writes=[NBF])
        c.op("dve", lambda e: e.memset(ones6[:], 1.0), writes=[ONES6])
        k = 0
        for (pj, dst, srcc, n) in [(3, WQ_FG, O_FG, 6), (0, WQ_AQ, O_AQ, 768), (1, WQ_FQ, O_FQ, 768),
                                   (2, WQ_CQ, O_CQ, 512)]:
            c.dma("pool", f"ld_w{k % 4}", wq[:, :, dst:dst + n], wv_[:, :, srcc:srcc + n], writes=[WQP[pj]])
            k += 1
        for j, (dst, srcc, n) in enumerate([(0, O_AV, 384), (384, O_FV, 384), (768, O_CV, 256)]):
            c.dma("pool", f"ld_w{k % 4}", wv[:, :, dst:dst + n], wv_[:, :, srcc:srcc + n], writes=[WV[j]])
            k += 1
        for (pj, dst, srcc, n, half) in [(0, WQ_RAQ, WQ_AQ, 768, 16), (2, WQ_RCQ, WQ_CQ, 512, 32)]:
            for kc in range(8):
                sv = wq[:, kc, srcc:srcc + n].rearrange("p (u e) -> p u e", e=2 * half)
                dv = wq[:, kc, dst:dst + n].rearrange("p (u e) -> p u e", e=2 * half)
                if kc % 2 == 0:
                    c.op("dve", lambda e: e.tensor_scalar(out=dv[:, :, 0:half], in0=sv[:, :, half:2 * half],
                                                          scalar1=-1.0, scalar2=None, op0=ALU.mult),
                         reads=[WQP[pj]], writes=[WQR[pj]])
                    c.op("dve", lambda e: e.tensor_copy(out=dv[:, :, half:2 * half], in_=sv[:, :, 0:half]),
                         reads=[WQP[pj]], writes=[WQR[pj]])
                else:
                    c.op("act", lambda e: e.mul(out=dv[:, :, 0:half], in_=sv[:, :, half:2 * half], mul=-1.0),
                         reads=[WQP[pj]], writes=[WQR[pj]])
                    c.op("act", lambda e: e.copy(out=dv[:, :, half:2 * half], in_=sv[:, :, 0:half]),
                         reads=[WQP[pj]], writes=[WQR[pj]])

        chunks = []
        for ci in range(16):
            col = ci * 128
            if ci < 6:
                chunks.append(("diff", col, WQ_RAQ + col, 0, 0))
            elif ci < 9:
                chunks.append(("fq", col, None, None, 1))
            elif ci < 12:
                chunks.append(("fk", col, None, None, 1))
            else:
                chunks.append(("dil", col, WQ_RCQ + (col - WQ_CQ), 2, 2))

        nob = 0

        def ld_rope(i):
            for j, nm in enumerate(["c_cosA", "c_sinA", "c_cosC", "c_sinC"]):
                c.dma("sp", f"ld_rt{j}", rt[i % 2][j][:], consts[nm][:, i * 512:(i + 1) * 512], writes=[RT[i % 2][j]])

        ld_rope(0)
        norm_to_xnT(c, g, 0, x_src, None, gbc, GBC, t, xnT[0], XNT[0], S)
        for i in range(NB):
            b = i % 2
            t0 = i * 512
            pf, PF = g.ps[6], g.PS[6]
            c.group("pe", [
                (lambda e, kc=kc: e.matmul(pf[0:6, :], wq[:, kc, WQ_FG:WQ_FG + 6], xnT[b][:, kc, :],
                                           start=(kc == 0), stop=(kc == 7)))
                for kc in range(8)], reads=[XNT[b], WQP[3]], writes=[PF])
            c.op("act", lambda e: e.activation(out=uu[:], in_=pf[0:6, :], func=AF.Exp, scale=-1.0, bias=nbf[:, 0:1]),
                 reads=[PF, NBF], writes=[UU])
            c.op("act", lambda e: e.activation(out=lf[:], in_=uu[:], func=AF.Ln, scale=1.0, bias=g.one_c[0:6, 0:1]),
                 reads=[UU], writes=[LF])
            init = 0.0 if i == 0 else cb[1 - b][:, 511:512]
            c.op("dve", lambda e: e.tensor_tensor_scan(out=cb[b][:], data0=ones6[:], data1=lf[:], initial=init,
                                                       op0=ALU.mult, op1=ALU.add),
                 reads=[LF, ONES6] + ([CB[1 - b]] if i > 0 else []), writes=[CB[b]])
            c.op("dve", lambda e: e.tensor_scalar(out=augb[:], in0=cb[b][:], scalar1=cb[b][:, 511:512], scalar2=-1.0,
                                                  op0=ALU.subtract, op1=ALU.mult),
                 reads=[CB[b]], writes=[AUGB])
            c.dma("sp", "st_aug", sc.qTf[:, 64, t0:t0 + 512], augb[:], reads=[AUGB])
            c.dma("sp", "st_cend", sc.cend[:, i:i + 1], cb[b][:, 511:512], reads=[CB[b]], allow_slow_non_contiguous=True)

            def fg_transposes():
                c.group("pe", [
                    (lambda e, tt=tt: e.transpose(out=pf[:, tt * 6:(tt + 1) * 6],
                                                  in_=cb[b][0:6, tt * 128:(tt + 1) * 128],
                                                  identity=g.identf[0:6, 0:6]))
                    for tt in range(4)], reads=[CB[b], g.IDENTF], writes=[PF])
                c.op("dve", lambda e: e.tensor_copy(out=ckm[:, i * 4:(i + 1) * 4, :],
                                                    in_=pf[:, 0:24].rearrange("p (t h) -> p t h", h=6)),
                     reads=[PF], writes=[CKM])
            for ci, (kind, col, rcol, rti, pj) in enumerate(chunks):
                if ci == 1 and i + 1 < NB:
                    ld_rope(i + 1)
                if ci == 4:
                    fg_transposes()
                if i + 1 < NB:
                    if ci % 4 == 0:
                        norm_tile(c, g, i + 1, ci // 4, x_src, None, gbc, GBC, t, stages="A")
                    if ci % 4 == 1:
                        norm_tile(c, g, i + 1, ci // 4, x_src, None, gbc, GBC, t, stages="B")
                    if ci % 4 == 3:
                        norm_tile(c, g, i + 1, ci // 4, x_src, None, gbc, GBC, t, stages="C")
                pa, PA = g.ps[ci % 2], g.PS[ci % 2]
                c.group("pe", [
                    (lambda e, kc=kc: e.matmul(pa[:], wq[:, kc, col:col + 128], xnT[b][:, kc, :],
                                               start=(kc == 0), stop=(kc == 7)))
                    for kc in range(8)], reads=[XNT[b], WQP[pj]], writes=[PA])
                o = nob % 3
                nob += 1
                if rcol is not None:
                    pb_, PB_ = g.ps[2 + ci % 2], g.PS[2 + ci % 2]
                    c.group("pe", [
                        (lambda e, kc=kc: e.matmul(pb_[:], wq[:, kc, rcol:rcol + 128], xnT[b][:, kc, :],
                                                   start=(kc == 0), stop=(kc == 7)))
                        for kc in range(8)], reads=[XNT[b], WQR[pj]], writes=[PB_])
                    s2 = ci % 2
                    c.op("dve", lambda e: e.tensor_tensor(out=tm1[s2][:], in0=pa[:], in1=rt[b][rti][:], op=ALU.mult),
                         reads=[PA, RT[b][rti]], writes=[TM1[s2]])
                    c.op("dve", lambda e: e.tensor_tensor(out=tm2[s2][:], in0=pb_[:], in1=rt[b][rti + 1][:], op=ALU.mult),
                         reads=[PB_, RT[b][rti + 1]], writes=[TM2[s2]])
                    c.op("pool", lambda e: e.tensor_tensor(out=ob[o][:], in0=tm1[s2][:], in1=tm2[s2][:], op=ALU.add),
                         reads=[TM1[s2], TM2[s2]], writes=[OB[o]])
                elif kind == "fq":
                    c.op("act", lambda e: e.activation(out=ob[o][:], in_=pa[:], func=AF.Copy, scale=0.125),
                         reads=[PA], writes=[OB[o]])
                else:
                    c.op("act", lambda e: e.copy(out=ob[o][:], in_=pa[:]), reads=[PA], writes=[OB[o]])
                if kind == "diff":
                    dstt = sc.qTd if ci < 3 else sc.kTd
                    r = (ci % 3) * 128
                    c.dma("sp", f"st_ob{o}", dstt[r:r + 128, t0:t0 + 512], ob[o][:], reads=[OB[o]])
                elif kind == "dil":
                    dstt = sc.qTc if ci < 14 else sc.kTc
                    r = (ci % 2) * 128
                    c.dma("sp", f"st_ob{o}", dstt[r:r + 128, t0:t0 + 512], ob[o][:], reads=[OB[o]])
                else:
                    dstt = sc.qTf if kind == "fq" else sc.kTf
                    h0 = 2 * ((ci - 6) % 3)
                    c.dma("sp", f"st_ob{o}", dstt[h0, 0:64, t0:t0 + 512], ob[o][0:64, :], reads=[OB[o]])
                    c.dma("sp", f"st_ob{o}b", dstt[h0 + 1, 0:64, t0:t0 + 512], ob[o][64:128, :], reads=[OB[o]])
            for tt in range(4):
                n = i * 4 + tt
                s = n % 2
                for hf in range(2):
                    pv, PV = g.ps[4 + hf], g.PS[4 + hf]
                    c.group("pe", [
                        (lambda e, kc=kc: e.matmul(pv[:], xnT[b][:, kc, tt * 128:(tt + 1) * 128],
                                                   wv[:, kc, hf * 512:(hf + 1) * 512],
                                                   start=(kc == 0), stop=(kc == 7)))
                        for kc in range(8)], reads=[XNT[b]] + WV, writes=[PV])
                    if hf == 0:
                        c.op("act", lambda e: e.copy(out=vb[s][:, 0:512], in_=pv[:]), reads=[PV], writes=[VB[s]])
                    else:
                        c.op("dve", lambda e: e.tensor_copy(out=vb[s][:, 512:1024], in_=pv[:]), reads=[PV], writes=[VB[s]])
                c.dma("sp", f"st_vb{s}", sc.v[n * 128:(n + 1) * 128, :], vb[s][:], reads=[VB[s]])
            if i + 1 < NB:
                for tt in range(4):
                    transpose_tile(c, g, tt, t, xnT[1 - b], XNT[1 - b])
        c.dma("sp", "st_ckm", sc.ckm, ckm[:].rearrange("p t h -> p (t h)"), reads=[CKM])
        c.barrier()


def phase_attn(c, g, sc, wo_d, lamq1_d, lamk1_d, lamq2_d, lamk2_d, gsub_d, lam_init, consts, x_src, x_dst, S, pfx,
               dbg=None):
    nc = c.nc
    NB = S // 512
    NT = S // 128
    with ExitStack() as st:
        A = lambda name, shape, dt: st.enter_context(nc.sbuf_tensor(pfx + name, shape, dt))
        oT = A("oT", [128, 8, S], BF16)
        OT = Buf("oT")
        wo = A("wo", [128, 8, D], BF16)
        WO = Buf("wo")
        qs = [A(f"qs{i}", [128, S], BF16) for i in range(2)]
        QS = [Buf(f"qs{i}") for i in range(2)]
        qz = [A(f"qz{i}", [128, S], BF16) for i in range(4)]
        QZ = [Buf(f"qz{i}") for i in range(4)]
        ks = [A(f"ks{i}", [128, S], BF16) for i in range(2)]
        KS = [Buf(f"ks{i}") for i in range(2)]
        va = [A(f"va{i}", [128, NT, 128], BF16) for i in range(2)]
        VA = [Buf(f"va{i}") for i in range(2)]
        NPT = ATT_LAG + 3
        pt = [A(f"pt{i}", [128, 1024 if ATT_PAIR else 512], BF16) for i in range(NPT)]
        PT = [Buf(f"pt{i}") for i in range(NPT)]
        tril = A("tril", [128, 128], BF16)
        TRIL = Buf("tril")
        wt = A("wt", [128, 2688], BF16)
        WT = Buf("wt")
        ckm = A("ckm", [128, NT, 6], F32)
        CKM = Buf("ckm")
        cend = A("cend", [128, 6 * NB], F32)
        CEND = Buf("cend")
        bq = [A(f"bq{i}", [128, NT], F32) for i in range(2)]
        BQ = [Buf(f"bq{i}") for i in range(2)]
        rec = [A(f"rec{i}", [64, 512], F32) for i in range(2)]
        REC = [Buf(f"rec{i}") for i in range(2)]
        ta = A("ta", [64, 512], F32)
        TA = Buf("ta")
        tb = A("tb", [64, 512], F32)
        TB = Buf("tb")
        td2 = [A(f"td{i}", [64, 512], F32) for i in range(2)]
        TD2 = [Buf(f"td{i}") for i in range(2)]
        tsq2 = [A(f"tsq{i}", [64, 512], BF16) for i in range(2)]
        TSQ2 = [Buf(f"tsq{i}") for i in range(2)]
        tr2 = [A(f"tr{i}", [64, 512], F32) for i in range(2)]
        TR2 = [Buf(f"tr{i}") for i in range(2)]
        lv = [A(f"lv{i}", [64, 32], F32) for i in range(4)]
        LV = [Buf(f"lv{i}") for i in range(4)]
        lj = A("lj", [64, 32], F32)
        LJ = Buf("lj")
        sm = A("sm", [64, 8], F32)
        SM = Buf("sm")
        xr = [A(f"xr{i}", [128, 512], F32) for i in range(3)]
        XR = [Buf(f"xr{i}") for i in range(3)]

        c.dma("sp", "ld_c0", tril[:], consts["c_tril"], writes=[TRIL])
        c.dma("sp", "ld_c1", wt[:], consts["c_wt"], writes=[WT])
        c.dma("sp", "ld_c2", ckm[:].rearrange("p t h -> p (t h)"), sc.ckm, writes=[CKM])
        c.dma("sp", "ld_c3", cend[:], sc.cend.rearrange("h q -> (h q)").partition_broadcast(128), writes=[CEND])
        wo_v = wo_d.rearrange("(cc p) d -> p cc d", p=128)
        c.dma("pool", "ld_w0", wo[:], wo_v, writes=[WO])
        for j, dd in enumerate([lamq1_d, lamk1_d, lamq2_d, lamk2_d]):
            c.dma("sp", f"ld_lv{j}", lv[j][:], dd.partition_broadcast(64), writes=[LV[j]])
        c.dma("sp", "ld_c4", sm[:, 5:6], gsub_d.rearrange("(p o) -> p o", o=1), writes=[SM])
        for j in range(2):
            c.op("dve", lambda e: e.scalar_tensor_tensor(out=lj[:], in0=lv[2 * j][:], scalar=1.0, in1=lv[2 * j + 1][:],
                                                         op0=ALU.mult, op1=ALU.mult, accum_out=sm[:, j:j + 1]),
                 reads=[LV[2 * j], LV[2 * j + 1]], writes=[LJ, SM])
        c.op("act", lambda e: e.activation(out=sm[:, 2:4], in_=sm[:, 0:2], func=AF.Exp), reads=[], writes=[SM])
        c.op("dve", lambda e: e.tensor_tensor(out=sm[:, 4:5], in0=sm[:, 2:3], in1=sm[:, 3:4], op=ALU.subtract),
             reads=[], writes=[SM])
        c.op("dve", lambda e: e.tensor_scalar(out=sm[:, 4:5], in0=sm[:, 4:5], scalar1=float(lam_init), scalar2=-1.0,
                                              op0=ALU.add, op1=ALU.mult), reads=[], writes=[SM])
        c.op("dve", lambda e: e.tensor_scalar(out=sm[:, 5:6], in0=sm[:, 5:6], scalar1=float(1.0 - lam_init),
                                              scalar2=None, op0=ALU.mult), reads=[], writes=[SM])
        for i in range(2):
            c.op("pool", lambda e: e.memset(va[i][:, :, 64:128], 1.0), writes=[VA[i]])
        for i in range(4):
            c.op("pool" if i % 2 == 0 else "dve", lambda e: e.memset(qz[i][:], 0.0), writes=[QZ[i]])
        pm = g.psT[:].bitcast(F32)
        PM = g.PST

        vsrc = sc.v.rearrange("(kb p) c -> p kb c", p=128)
        state = {"sb": 0, "pt": 0, "ob": 0, "rec": 0, "bq": 0}
        SCALE_D = 32 ** -0.5

        heads = []
        for h in range(6):
            cd, hl = h // 2, h % 2
            ksl = cd % 2

            def loads(h=h, cd=cd, hl=hl, ksl=ksl):
                if hl == 0:
                    c.dma("sp", f"ld_ks{ksl}", ks[ksl][:], sc.kTd[cd * 128:(cd + 1) * 128, :], writes=[KS[ksl]])
                for pz in (2 * hl, 2 * hl + 1):
                    r = cd * 128 + pz * 32
                    c.dma("sp", f"ld_qz{pz}", qz[pz][pz * 32:(pz + 1) * 32, :], sc.qTd[r:r + 32, :], writes=[QZ[pz]])
                c.dma("sp", f"ld_va{h % 2}", va[h % 2][:, :, 0:64], vsrc[:, :, h * 64:(h + 1) * 64], writes=[VA[h % 2]])
            heads.append(dict(kind="diff", hh=h, vs=h % 2, loads=loads, K=128, scale=SCALE_D, mask="causal",
                              units=[(qz[2 * hl], QZ[2 * hl], ks[ksl], KS[ksl]),
                                     (qz[2 * hl + 1], QZ[2 * hl + 1], ks[ksl], KS[ksl])]))
        for h in range(6):
            sl = (6 + h) % 2
            ksl = (3 + h) % 2

            def loads(h=h, sl=sl, ksl=ksl):
                c.dma("sp", f"ld_qs{sl}", qs[sl][0:65, :], sc.qTf[h], writes=[QS[sl]])
                c.dma("sp", f"ld_ks{ksl}", ks[ksl][0:64, :], sc.kTf[h], writes=[KS[ksl]])
                c.op("pool", lambda e: e.memset(ks[ksl][64:65, :], 1.0), writes=[KS[ksl]])
                c.dma("sp", f"ld_va{sl}", va[sl][:, :, 0:64], vsrc[:, :, (6 + h) * 64:(7 + h) * 64], writes=[VA[sl]])
            heads.append(dict(kind="fox", hh=6 + h, fh=h, vs=sl, loads=loads, K=65, scale=1.0, mask="causal",
                              units=[(qs[sl], QS[sl], ks[ksl], KS[ksl])]))
        for h in range(4):
            cc, hl = h // 2, h % 2
            ksl = (9 + cc) % 2
            zi = 0 if hl == 0 else 3
            sl = (12 + h) % 2

            def loads(h=h, cc=cc, hl=hl, ksl=ksl, zi=zi, sl=sl):
                if hl == 0:
                    c.dma("sp", f"ld_ks{ksl}", ks[ksl][:], sc.kTc[cc * 128:(cc + 1) * 128, :], writes=[KS[ksl]])
                r = cc * 128 + hl * 64
                c.dma("sp", f"ld_qz{zi}", qz[zi][hl * 64:(hl + 1) * 64, :], sc.qTc[r:r + 64, :], writes=[QZ[zi]])
                c.dma("sp", f"ld_va{sl}", va[sl][:, :, 0:64], vsrc[:, :, (12 + h) * 64:(13 + h) * 64], writes=[VA[sl]])
            heads.append(dict(kind="dil", hh=12 + h, vs=sl, loads=loads, K=128, scale=0.125, mask="dil",
                              units=[(qz[zi], QZ[zi], ks[ksl], KS[ksl])]))

        items = []
        for hi, hd in enumerate(heads):
            for qb in range(NB):
                for ui, un in enumerate(hd["units"]):
                    kb_lo = max(0, 4 * qb - 16) if hd["kind"] == "dil" else 0
                    kbs = list(range(kb_lo, 4 * qb + 4))
                    grp_ = []
                    kk = 0
                    while kk < len(kbs):
                        if hd["kind"] != "fox" and ATT_PAIR and kbs[kk] + 1 < 4 * qb and kk + 1 < len(kbs):
                            grp_.append([kbs[kk], kbs[kk + 1]])
                            kk += 2
                        else:
                            grp_.append([kbs[kk]])
                            kk += 1
                    for n_, kbl in enumerate(grp_):
                        items.append(dict(hi=hi, hd=hd, qb=qb, ui=ui, un=un, kbl=kbl, kb=kbl[0], first=(n_ == 0),
                                          last=(n_ == len(grp_) - 1),
                                          head_start=(qb == 0 and ui == 0 and n_ == 0)))

        deferred = []

        def emit_S(it):
            hd = it["hd"]
            qt, QTB, kt, KTB = it["un"]
            qb, kb = it["qb"], it["kb"]
            q0 = qb * 512
            if it["first"] and hd["kind"] == "fox":
                bi = state["bq"] % 2
                state["bq"] += 1
                nk = 4 * qb + 4
                fh = hd["fh"]
                c.op("dve", lambda e: e.tensor_scalar(out=bq[bi][:, 0:nk], in0=ckm[:, 0:nk, fh],
                                                      scalar1=cend[:, fh * NB + qb:fh * NB + qb + 1], scalar2=None,
                                                      op0=ALU.subtract), reads=[CKM, CEND], writes=[BQ[bi]])
                hd["bi"] = bi
            if hd["kind"] == "fox":
                it["bi"] = hd["bi"]
            K = hd["K"]
            if len(it["kbl"]) == 2:
                if state["sb"] % 2 == 1:
                    state["sb"] += 1
                ti = (state["sb"] % 4) // 2
                state["sb"] += 2
                ptile = g.pp[ti]
                c.group("pe", [
                    (lambda e, m=m, kbm=kbm: e.matmul(ptile[:, m * 512:(m + 1) * 512],
                                                      kt[0:K, kbm * 128:(kbm + 1) * 128],
                                                      qt[0:K, q0:q0 + 512], start=True, stop=True))
                    for m, kbm in enumerate(it["kbl"])], reads=[QTB, KTB], writes=[g.PS[2 * ti], g.PS[2 * ti + 1]])
                it["S"] = (-1, 0, ptile, [g.PS[2 * ti], g.PS[2 * ti + 1]])
                return
            j = kb - 4 * qb
            col0 = max(0, j) * 128
            si = state["sb"] % 4
            state["sb"] += 1
            sbk, SBK = g.ps[si], g.PS[si]
            if hd["mask"] == "causal" and j >= 0:
                c.group("pe", [
                    lambda e: e.matmul(sbk[:, col0:512], kt[0:K, kb * 128:(kb + 1) * 128],
                                       qt[0:K, q0 + col0:q0 + 512], start=True, stop=False),
                    lambda e: e.matmul(sbk[:, col0:col0 + 128], g.ident[:], tril[:], start=False, stop=True),
                ], reads=[QTB, KTB, TRIL, g.IDENT], writes=[SBK])
            else:
                c.op("pe", lambda e: e.matmul(sbk[:, col0:512], kt[0:K, kb * 128:(kb + 1) * 128],
                                              qt[0:K, q0 + col0:q0 + 512], start=True, stop=True),
                     reads=[QTB, KTB], writes=[SBK])
            it["S"] = (j, col0, sbk, [SBK])

        def plain_epilogue(obk, OBK, hh, qb, use_act=False):
            q0 = qb * 512
            ri = state["rec"] % 2
            state["rec"] += 1
            pb = (hh % 2) * 64
            if use_act:
                c.op("act", lambda e: e.activation(out=rec[ri][:], in_=obk[64:128, :], func=AF.Ln),
                     reads=[OBK], writes=[REC[ri]])
                c.op("act", lambda e: e.activation(out=rec[ri][:], in_=rec[ri][:], func=AF.Exp, scale=-1.0),
                     reads=[], writes=[REC[ri]])
            else:
                c.op("dve", lambda e: e.reciprocal(out=rec[ri][:], in_=obk[64:128, :]), reads=[OBK],
                     writes=[REC[ri]])
            c.op("dve", lambda e: e.tensor_tensor(out=oT[pb:pb + 64, hh // 2, q0:q0 + 512], in0=obk[0:64, :],
                                                  in1=rec[ri][:], op=ALU.mult), reads=[OBK, REC[ri]], writes=[OT])

        def diff_stage1(oa, OA, o2, O2, par):
            td, TD, tsq, TSQ = td2[par], TD2[par], tsq2[par], TSQ2[par]
            c.op("dve", lambda e: e.reciprocal(out=rec[0][:], in_=oa[64:128, :]), reads=[OA], writes=[REC[0]])
            c.op("dve", lambda e: e.tensor_tensor(out=ta[:], in0=oa[0:64, :], in1=rec[0][:], op=ALU.mult),
                 reads=[OA, REC[0]], writes=[TA])
            c.op("dve", lambda e: e.reciprocal(out=rec[1][:], in_=o2[64:128, :]), reads=[O2], writes=[REC[1]])
            c.op("dve", lambda e: e.tensor_tensor(out=tb[:], in0=o2[0:64, :], in1=rec[1][:], op=ALU.mult),
                 reads=[O2, REC[1]], writes=[TB])
            c.op("dve", lambda e: e.scalar_tensor_tensor(out=td[:], in0=tb[:], scalar=sm[:, 4:5], in1=ta[:],
                                                         op0=ALU.mult, op1=ALU.add), reads=[TA, TB, SM], writes=[TD])
            c.op("pool", lambda e: e.tensor_tensor(out=tsq[:], in0=td[:], in1=td[:], op=ALU.mult),
                 reads=[TD], writes=[TSQ])

        def diff_stage2(h, qb, par):
            td, TD, tsq, TSQ, tr, TR = td2[par], TD2[par], tsq2[par], TSQ2[par], tr2[par], TR2[par]
            q0 = qb * 512
            pb = (h % 2) * 64
            c.op("pe", lambda e: e.matmul(pm[0:64, :], g.ones64b[0:64, 0:64], tsq[:], start=True, stop=True),
                 reads=[TSQ, g.ONES64], writes=[PM])
            c.op("act", lambda e: e.activation(out=tr[:], in_=pm[0:64, :], func=AF.Ln, scale=1.0,
                                               bias=g.eps_sub[0:64, 0:1]), reads=[PM], writes=[TR])
            c.op("act", lambda e: e.activation(out=tr[:], in_=tr[:], func=AF.Exp, scale=-0.5), reads=[], writes=[TR])
            c.op("dve", lambda e: e.scalar_tensor_tensor(out=oT[pb:pb + 64, h // 2, q0:q0 + 512], in0=td[:],
                                                         scalar=sm[:, 5:6], in1=tr[:], op0=ALU.mult, op1=ALU.mult),
                 reads=[TD, TR, SM], writes=[OT])

        def emit_PV(i):
            it = items[i]
            hd = it["hd"]
            qb = it["qb"]
            p, P, col0 = it["P"]
            if it["first"]:
                obi = 4 + state["ob"] % 3
                state["ob"] += 1
                hd["cur_o"] = (g.ps[obi], g.PS[obi])
            obk, OBK = hd["cur_o"]
            vs = hd["vs"]
            nk = len(it["kbl"])
            if nk == 2:
                c.group("pe", [
                    (lambda e, m=m, kbm=kbm: e.matmul(obk[:, 0:512], va[vs][:, kbm, :], p[:, m * 512:(m + 1) * 512],
                                                      start=(it["first"] and m == 0), stop=(it["last"] and m == 1)))
                    for m, kbm in enumerate(it["kbl"])], reads=[P, VA[vs]], writes=[OBK])
            else:
                kb = it["kb"]
                c.op("pe", lambda e: e.matmul(obk[:, col0:512], va[vs][:, kb, :], p[:, col0:512],
                                              start=it["first"], stop=it["last"]), reads=[P, VA[vs]], writes=[OBK])
            if it["last"]:
                if hd["kind"] == "diff":
                    if it["ui"] == 0:
                        pend_diff[(it["hi"], qb)] = (obk, OBK)
                    else:
                        oa, OA = pend_diff.pop((it["hi"], qb))
                        par = npair[0] % 2
                        npair[0] += 1
                        while len(deferred) > 1:
                            deferred.pop(0)[1]()
                        diff_stage1(oa, OA, obk, OBK, par)
                        deferred.append((i + DEFER, (lambda h=hd["hh"], qb=qb, par=par: diff_stage2(h, qb, par))))
                else:
                    plain_epilogue(obk, OBK, hd["hh"], qb, use_act=(hd["kind"] == "dil"))

        heads[0]["loads"]()
        heads[1]["loads"]()
        LAG = ATT_LAG
        PRE = ATT_PRE
        DEFER = 22
        npair = [0]
        for i in range(min(PRE, len(items))):
            emit_S(items[i])
        pend_diff = {}
        for i, it in enumerate(items):
            hd = it["hd"]
            if i >= LAG and items[i - LAG]["head_start"]:
                nh = items[i - LAG]["hi"] + 1
                if nh >= 2 and nh < len(heads):
                    heads[nh]["loads"]()
            j, col0, sbk, SBKS = it["S"]
            qb, kb = it["qb"], it["kb"]
            q0 = qb * 512
            pi = state["pt"] % NPT
            state["pt"] += 1
            p, P = pt[pi], PT[pi]
            w1 = 1024 if len(it["kbl"]) == 2 else 512
            if hd["kind"] == "fox":
                bi = it["bi"]
                c.op("act", lambda e: e.activation(out=p[:, col0:512], in_=sbk[:, col0:512], func=AF.Exp,
                                                   scale=hd["scale"], bias=bq[bi][:, kb:kb + 1]),
                     reads=SBKS + [BQ[bi]], writes=[P])
            else:
                c.op("act", lambda e: e.activation(out=p[:, col0:w1], in_=sbk[:, col0:w1], func=AF.Exp,
                                                   scale=hd["scale"]), reads=SBKS, writes=[P])
            if hd["mask"] != "causal":
                for m, kbm in enumerate(it["kbl"]):
                    off = q0 - kbm * 128 + col0
                    c.op("pool" if i % DIL_POOL_EVERY == DIL_POOL_EVERY - 1 else "dve",
                         lambda e: e.tensor_tensor(out=p[:, m * 512 + col0:(m + 1) * 512],
                                                   in0=p[:, m * 512 + col0:(m + 1) * 512],
                                                   in1=wt[:, off:off + 512 - col0], op=ALU.mult),
                         reads=[WT], writes=[P])
            if i + PRE < len(items):
                emit_S(items[i + PRE])
            it["P"] = (p, P, col0)
            if i >= LAG:
                emit_PV(i - LAG)
            while deferred and deferred[0][0] <= i:
                deferred.pop(0)[1]()
        for i2 in range(max(0, len(items) - LAG), len(items)):
            emit_PV(i2)
        while deferred:
            deferred.pop(0)[1]()

        if dbg is not None:
            for cc in range(8):
                c.dma("sp", "st_dbg", dbg[cc * 128:(cc + 1) * 128, :], oT[:, cc, :], reads=[OT])

        groups = [(i, tt, dh) for i in range(NB) for tt in range(4) for dh in range(2)]
        NXR = 4
        xr = xr + [A("xr3", [128, 512], F32)]
        XR = XR + [Buf("xr3")]

        def ld_x(n):
            i, tt, dh = groups[n]
            r0 = (i * 4 + tt) * 128
            c.dma("sp", f"ld_xr{n % NXR}", xr[n % NXR][:], x_src[r0:r0 + 128, dh * 512:(dh + 1) * 512],
                  writes=[XR[n % NXR]])

        ld_x(0)
        ld_x(1)
        for n, (i, tt, dh) in enumerate(groups):
            if n + 2 < len(groups):
                ld_x(n + 2)
            py, PY = g.ps[n % 2], g.PS[n % 2]
            r0 = (i * 4 + tt) * 128
            xs = n % NXR
            c.group("pe", [
                (lambda e, cc=cc: e.matmul(py[:], oT[:, cc, r0:r0 + 128], wo[:, cc, dh * 512:(dh + 1) * 512],
                                           start=(cc == 0), stop=(cc == 7)))
                for cc in range(8)], reads=[OT, WO], writes=[PY])
            c.op("dve", lambda e: e.tensor_tensor(out=xr[xs][:], in0=py[:], in1=xr[xs][:], op=ALU.add),
                 reads=[PY], writes=[XR[xs]])
            c.dma("sp", f"st_xr{xs}", x_dst[r0:r0 + 128, dh * 512:(dh + 1) * 512], xr[xs][:], reads=[XR[xs]])
        c.barrier()


def setup_globals(c, nc, consts):
    g = G()
    g.pp = [nc.alloc_psum_tensor(f"pp{i}", [128, 1024], F32) for i in range(2)]
    g.ps = [g.pp[0][:, 0:512], g.pp[0][:, 512:1024], g.pp[1][:, 0:512], g.pp[1][:, 512:1024]]
    g.ps += [nc.alloc_psum_tensor(f"ps{i}", [128, 512], F32) for i in range(4, 7)]
    g.PS = [Buf(f"ps{i}", excl=True) for i in range(7)]
    g.psT = nc.alloc_psum_tensor("psT", [128, 1024], BF16)
    g.PST = Buf("psT", excl=True)
    g.ident = nc.alloc_sbuf_tensor("ident", [128, 128], BF16)
    g.IDENT = Buf("ident")
    g.identf = nc.alloc_sbuf_tensor("identf", [128, 128], F32)
    g.IDENTF = Buf("identf")
    g.ones64 = nc.alloc_sbuf_tensor("ones64", [64, 64], F32)
    g.ones64b = nc.alloc_sbuf_tensor("ones64b", [64, 64], BF16)
    g.ONES64 = Buf("ones64")
    g.eps_norm = nc.alloc_sbuf_tensor("eps_norm", [128, 1], F32)
    g.eps_sub = nc.alloc_sbuf_tensor("eps_sub", [128, 1], F32)
    g.one_c = nc.alloc_sbuf_tensor("one_c", [128, 1], F32)
    g.EPS = Buf("eps")
    c.dma("sp", "ld_c0", g.ident[:], consts["c_ident"], writes=[g.IDENT])
    c.dma("sp", "ld_c1", g.identf[:], consts["c_identf"], writes=[g.IDENTF])
    c.op("dve", lambda e: e.memset(g.eps_norm[:], NORM_EPS), writes=[g.EPS])
    c.op("dve", lambda e: e.memset(g.eps_sub[:], SUBLN_EPS), writes=[g.EPS])
    c.op("dve", lambda e: e.memset(g.one_c[:], 1.0), writes=[g.EPS])
    c.op("dve", lambda e: e.memset(g.ones64[:], 1.0 / 64.0), writes=[g.ONES64])
    c.op("dve", lambda e: e.memset(g.ones64b[:], 1.0 / 64.0), writes=[g.ONES64])
    return g


W_NAMES = ["w_in", "b_f", "lam_q1", "lam_k1", "lam_q2", "lam_k2", "g_sub", "w_o", "g_ffn1", "w1_gate", "w1_up",
           "w1_down", "g_mix", "g_ffn2", "w2_gate", "w2_up", "w2_down", "g_final"]
W_SHAPES = {
    "w_in": [DEPTH, D, INW], "b_f": [DEPTH, 6], "lam_q1": [DEPTH, 32], "lam_k1": [DEPTH, 32], "lam_q2": [DEPTH, 32],
    "lam_k2": [DEPTH, 32], "g_sub": [DEPTH, 64], "w_o": [DEPTH, D, D], "g_ffn1": [DEPTH, D],
    "w1_gate": [DEPTH, D, DFF], "w1_up": [DEPTH, D, DFF], "w1_down": [DEPTH, DFF, D], "g_mix": [DEPTH, D],
    "g_ffn2": [DEPTH, D], "w2_gate": [DEPTH, D, DFF], "w2_up": [DEPTH, D, DFF], "w2_down": [DEPTH, DFF, D],
    "g_final": [D],
}


def const_shapes(S):
    return {"c_ident": ([128, 128], BF16), "c_identf": ([128, 128], F32), "c_tril": ([128, 128], BF16),
            "c_wt": ([128, 2688], BF16), "c_cosA": ([128, S], F32), "c_sinA": ([128, S], F32),
            "c_cosC": ([128, S], F32), "c_sinC": ([128, S], F32)}


def make_consts(S):
    bf = ml_dtypes.bfloat16
    cst = {}
    cst["c_ident"] = np.eye(128, dtype=np.float32).astype(bf)
    cst["c_identf"] = np.eye(128, dtype=np.float32)
    kl = np.arange(128)[:, None]
    ql = np.arange(128)[None, :]
    cst["c_tril"] = np.where(ql >= kl, 0.0, -30000.0).astype(np.float32).astype(bf)
    xx = np.arange(2688)[None, :]
    dl = xx - kl
    wmask = ((dl >= 0) & (dl <= 128)).astype(np.float32) + ((dl >= 0) & (dl <= 512) & (dl % 4 == 0)) \
        + ((dl >= 0) & (dl <= 2048) & (dl % 16 == 0))
    cst["c_wt"] = wmask.astype(np.float32).astype(bf)
    pos = np.arange(S, dtype=np.float32)
    for nm, half in (("A", 16), ("C", 32)):
        inv = (np.float32(10000.0) ** (-(np.arange(half, dtype=np.float32) / np.float32(half)))).astype(np.float32)
        ang = (pos[None, :] * inv[:, None]).astype(np.float32)
        rows = np.arange(128) % half
        a = ang[rows].astype(np.float64)
        cst["c_cos" + nm] = np.cos(a).astype(np.float32)
        cst["c_sin" + nm] = np.sin(a).astype(np.float32)
    return cst


def build_nc(S, layers, final, phases=("ffn1", "proj", "attn", "ffn2"), dbg=False):
    nc = bass.Bass("TRN2", target_bir_lowering=False)
    c = Ctx(nc)
    dt = lambda name, shape, dtype=F32: nc.dram_tensor(name, shape, dtype, kind="ExternalInput").ap()
    x_in = dt("x", [S, D])
    w = {nm: dt(nm, W_SHAPES[nm]) for nm in W_NAMES}
    consts = {nm: dt(nm, shp, dty) for nm, (shp, dty) in const_shapes(S).items()}
    out = nc.dram_tensor("out", [S, D], F32, kind="ExternalOutput").ap()
    dbg_ap = nc.dram_tensor("dbg", [1024, S], BF16, kind="ExternalOutput").ap() if dbg else None
    sc = alloc_scratch(nc, S)
    g = setup_globals(c, nc, consts)
    c.barrier()
    first = True

    def srcs():
        return x_in if first else out

    for l in layers:
        lam_init = 0.8 - 0.6 * math.exp(-0.3 * l)
        if "ffn1" in phases:
            phase_ffn(c, g, w["w1_gate"][l], w["w1_up"][l], w["w1_down"][l], w["g_ffn1"][l], srcs(), None, out, None, S,
                      f"L{l}a_")
            first = False
        if "proj" in phases:
            phase_proj(c, g, sc, w["w_in"][l], w["g_mix"][l], w["b_f"][l], consts, srcs(), S, f"L{l}p_")
        if "attn" in phases:
            phase_attn(c, g, sc, w["w_o"][l], w["lam_q1"][l], w["lam_k1"][l], w["lam_q2"][l], w["lam_k2"][l],
                       w["g_sub"][l], lam_init, consts, srcs(), out, S, f"L{l}m_", dbg=dbg_ap)
            first = False
        if "ffn2" in phases:
            phase_ffn(c, g, w["w2_gate"][l], w["w2_up"][l], w["w2_down"][l], w["g_ffn2"][l], srcs(), None, out, None, S,
                      f"L{l}c_")
            first = False
    if final:
        phase_final(c, g, w["g_final"], srcs(), None, out, None, S, "fin_")
    c.barrier()
    nc._ctx_ninst = c.ninst
    return nc


def kernel(**inputs):
    x = np.ascontiguousarray(inputs["x"], dtype=np.float32)
    B = x.shape[0]
    nc = build_nc(SEQ, list(range(DEPTH)), True)
    shared = {k: np.ascontiguousarray(inputs[k], dtype=np.float32) for k in W_NAMES}
    shared.update(make_consts(SEQ))
    in_maps = [dict(shared, x=x[b]) for b in range(B)]
    res = run_bass_kernel_spmd(nc, in_maps, core_ids=list(range(B)))
    return np.stack([r["out"] for r in res.results], axis=0)
```

```python
import math
from contextlib import ExitStack

import numpy as np
import ml_dtypes

import concourse.bass as bass
import concourse.mybir as mybir
from concourse.bass_utils import run_bass_kernel_spmd

F32 = mybir.dt.float32
BF16 = mybir.dt.bfloat16
AF = mybir.ActivationFunctionType
ALU = mybir.AluOpType

D = 1024
DFF = 2816
NFC = DFF // 128
DEPTH = 4
SEQ = 4096
NCORES = 8
INW = 3078
NORM_EPS = 1e-6
SUBLN_EPS = 1e-5
O_AQ, O_AK, O_AV, O_FQ, O_FK, O_FV, O_FG, O_CQ, O_CK, O_CV = 0, 384, 768, 1152, 1536, 1920, 2304, 2310, 2566, 2822


ATT_NSB = 4
ATT_PAIR = False
ATT_PRE = 2
ATT_LAG = 2
FFN_PREFETCH = True
DIL_DEFER = 6
DIL_POOL_EVERY = 10 ** 9


class Buf:
    def __init__(self, name, excl=False):
        self.name = name
        self.excl = excl
        self.w = {}
        self.r = {}


def _merge(d, tok):
    if tok is None:
        return
    s, v = tok
    if d.get(s, 0) < v:
        d[s] = v


class Ctx:
    def __init__(self, nc):
        self.nc = nc
        self.eng = {"pe": nc.tensor, "act": nc.scalar, "dve": nc.vector, "pool": nc.gpsimd, "sp": nc.sync}
        self.sems = {}
        self.cnt = {}
        self.seen = {}
        self.ninst = 0

    def sem(self, name):
        if name not in self.sems:
            self.sems[name] = self.nc.alloc_semaphore(name)
            self.cnt[name] = 0
        return self.sems[name]

    def wait(self, e, tok):
        if tok is None:
            return
        s, v = tok
        if v <= 0:
            return
        key = (e, s)
        if self.seen.get(key, 0) >= v:
            return
        self.eng[e].wait_ge(self.sems[s], v)
        self.seen[key] = v
        self.ninst += 1

    def _deps(self, e, reads, writes):
        reads = [b for b in reads if b is not None]
        writes = [b for b in writes if b is not None]
        for b in reads:
            for s, v in b.w.items():
                self.wait(e, (s, v))
            if b.excl:
                for s, v in b.r.items():
                    self.wait(e, (s, v))
        for b in writes:
            for s, v in b.w.items():
                self.wait(e, (s, v))
            for s, v in b.r.items():
                self.wait(e, (s, v))

    def _commit(self, tok, reads, writes):
        reads = [b for b in reads if b is not None]
        writes = [b for b in writes if b is not None]
        for b in reads:
            if b.excl:
                b.w = {tok[0]: tok[1]}
                b.r = {}
            else:
                _merge(b.r, tok)
        for b in writes:
            b.w = {tok[0]: tok[1]}
            b.r = {}

    def op(self, e, fn, reads=(), writes=()):
        self._deps(e, reads, writes)
        ins = fn(self.eng[e])
        s = "c_" + e
        self.sem(s)
        self.cnt[s] += 1
        ins.then_inc(self.sems[s], 1)
        self.ninst += 1
        tok = (s, self.cnt[s])
        self._commit(tok, reads, writes)
        return tok

    def group(self, e, fns, reads=(), writes=()):
        self._deps(e, reads, writes)
        ins = None
        for fn in fns:
            ins = fn(self.eng[e])
            self.ninst += 1
        s = "c_" + e
        self.sem(s)
        self.cnt[s] += 1
        ins.then_inc(self.sems[s], 1)
        tok = (s, self.cnt[s])
        self._commit(tok, reads, writes)
        return tok

    def dma(self, e, semname, out, in_, reads=(), writes=(), **kw):
        self.sem(semname)
        self._deps(e, reads, writes)
        self.wait(e, (semname, self.cnt[semname]))
        ins = self.eng[e].dma_start(out=out, in_=in_, **kw)
        self.cnt[semname] += 16
        ins.then_inc(self.sems[semname], 16)
        self.ninst += 1
        tok = (semname, self.cnt[semname])
        self._commit(tok, reads, writes)
        return tok

    def barrier(self, engines=("pe", "act", "dve", "pool", "sp"), exclude_prefix=None):
        for e in engines:
            for s, v in self.cnt.items():
                if exclude_prefix is not None and s.startswith(exclude_prefix):
                    continue
                self.wait(e, (s, v))


class G:
    pass


def rmsnorm_rstd(c, g, xt_ap, XT, junk, JUNK, ss, SS, rs, RS):
    c.op("act", lambda e: e.activation(out=junk, in_=xt_ap, func=AF.Square, accum_out=ss),
         reads=[XT], writes=[JUNK, SS])
    c.op("act", lambda e: e.activation(out=rs, in_=ss, func=AF.Sqrt, scale=1.0 / D, bias=g.eps_norm[:, 0:1]),
         reads=[SS], writes=[RS])
    c.op("dve", lambda e: e.reciprocal(out=rs, in_=rs), reads=[], writes=[RS])


def norm_tile(c, g, i, tt, x_src, XSRC, gbc, GBC, t, stages="ABC"):
    n = i * 4 + tt
    s = n % 2
    r0 = n * 128
    if "A" in stages:
        c.dma("sp", f"ld_xt{s}", t.xt[s][:], x_src[r0:r0 + 128, :], reads=[XSRC], writes=[t.XT[s]])
    if "B" in stages:
        c.op("act", lambda e: e.activation(out=t.junk[:], in_=t.xt[s][:], func=AF.Square, accum_out=t.ss[s][:]),
             reads=[t.XT[s]], writes=[t.JUNK, t.SS[s]])
        c.op("act", lambda e: e.activation(out=t.rs[s][:], in_=t.ss[s][:], func=AF.Sqrt, scale=1.0 / D,
                                           bias=g.eps_norm[:, 0:1]), reads=[t.SS[s]], writes=[t.RS[s]])
    if "C" in stages:
        c.op("dve", lambda e: e.reciprocal(out=t.rs[s][:], in_=t.rs[s][:]), reads=[], writes=[t.RS[s]])
        c.op("dve", lambda e: e.scalar_tensor_tensor(out=t.xn[tt][:], in0=t.xt[s][:], scalar=t.rs[s][:, 0:1],
                                                     in1=gbc[:], op0=ALU.mult, op1=ALU.mult),
             reads=[t.XT[s], t.RS[s], GBC], writes=[t.XN[tt]])


def transpose_tile(c, g, tt, t, xnT, XNT):
    if tt % 2 == 0:
        pst, PSTB = g.psT[:], g.PST
    else:
        pst, PSTB = g.ps[6][:].bitcast(BF16), g.PS[6]
    c.group("pe", [
        (lambda e, kc=kc: e.transpose(out=pst[:, kc * 128:(kc + 1) * 128],
                                      in_=t.xn[tt][:, kc * 128:(kc + 1) * 128], identity=g.ident[:]))
        for kc in range(8)], reads=[t.XN[tt], g.IDENT], writes=[PSTB])
    src = pst.rearrange("p (k t) -> p k t", k=8)
    dst = xnT[:, :, tt * 128:(tt + 1) * 128]
    if tt % 2 == 0:
        c.op("act", lambda e: e.copy(out=dst, in_=src), reads=[PSTB], writes=[XNT])
    else:
        c.op("dve", lambda e: e.tensor_copy(out=dst, in_=src), reads=[PSTB], writes=[XNT])


def norm_to_xnT(c, g, i, x_src, XSRC, gbc, GBC, t, xnT, XNT, S):
    for tt in range(4):
        norm_tile(c, g, i, tt, x_src, XSRC, gbc, GBC, t)
        transpose_tile(c, g, tt, t, xnT, XNT)


class T:
    pass


def alloc_norm_tiles(nc, st, pfx):
    t = T()
    A = lambda name, shape, dt: st.enter_context(nc.sbuf_tensor(pfx + name, shape, dt))
    t.xt = [A(f"xt{i}", [128, D], F32) for i in range(2)]
    t.XT = [Buf(f"xt{i}") for i in range(2)]
    t.xn = [A(f"xn{i}", [128, D], BF16) for i in range(4)]
    t.XN = [Buf(f"xn{i}") for i in range(4)]
    t.junk = A("junk", [128, D], BF16)
    t.JUNK = Buf("junk")
    t.ss = [A(f"ss{i}", [128, 1], F32) for i in range(2)]
    t.SS = [Buf(f"ss{i}") for i in range(2)]
    t.rs = [A(f"rs{i}", [128, 1], F32) for i in range(2)]
    t.RS = [Buf(f"rs{i}") for i in range(2)]
    return t


FFN_BOUNDS = [0, 128, 384, 768, 1280, 1920, 2368, DFF]


def alloc_ffn_weights(nc, st, pfx):
    A = lambda name, shape, dt: st.enter_context(nc.sbuf_tensor(pfx + name, shape, dt))
    sh = T()
    sh.wg = A("wg", [128, 8, DFF], BF16)
    sh.wu = A("wu", [128, 8, DFF], BF16)
    sh.wd = A("wd", [128, NFC, D], BF16)
    NG = len(FFN_BOUNDS) - 1
    sh.WG = [Buf(f"wg{j}") for j in range(NG)]
    sh.WU = [Buf(f"wu{j}") for j in range(NG)]
    sh.WD = [Buf(f"wd{j}") for j in range(2)]
    sh.loaded = False
    sh.k = 0
    return sh


def ffn_load_gu(c, sh, wg_d, wu_d, groups):
    wg_v = wg_d.rearrange("(kc p) f -> p kc f", p=128)
    wu_v = wu_d.rearrange("(kc p) f -> p kc f", p=128)
    for j in groups:
        lo, hi = FFN_BOUNDS[j], FFN_BOUNDS[j + 1]
        c.dma("pool", f"ld_w{sh.k % 4}", sh.wg[:, :, lo:hi], wg_v[:, :, lo:hi], writes=[sh.WG[j]])
        sh.k += 1
        c.dma("pool", f"ld_w{sh.k % 4}", sh.wu[:, :, lo:hi], wu_v[:, :, lo:hi], writes=[sh.WU[j]])
        sh.k += 1


def ffn_load_d(c, sh, wd_d):
    wd_v = wd_d.rearrange("(fc p) d -> p fc d", p=128)
    for jj in range(2):
        c.dma("pool", f"ld_w{sh.k % 4}", sh.wd[:, jj * 11:(jj + 1) * 11, :], wd_v[:, jj * 11:(jj + 1) * 11, :],
              writes=[sh.WD[jj]])
        sh.k += 1


def phase_ffn(c, g, wg_d, wu_d, wd_d, gvec_d, x_src, XSRC, x_dst, XDST, S, pfx, shared=None, prefetch=None):
    nc = c.nc
    NB = S // 512
    with ExitStack() as st:
        A = lambda name, shape, dt: st.enter_context(nc.sbuf_tensor(pfx + name, shape, dt))
        sh = shared if shared is not None else alloc_ffn_weights(nc, st, pfx)
        wg, wu, wd = sh.wg, sh.wu, sh.wd
        gbc = A("gbc", [128, D], F32)
        GBC = Buf("gbc")
        t = alloc_norm_tiles(nc, st, pfx)
        xnT = [A(f"xnT{i}", [128, 8, 512], BF16) for i in range(2)]
        XNT = [Buf(f"xnT{i}") for i in range(2)]
        hT = A("hT", [128, NFC, 512], BF16)
        HT = Buf("hT")
        sg = [A(f"sg{i}", [128, 512], F32) for i in range(2)]
        SG = [Buf(f"sg{i}") for i in range(2)]
        xr = [A(f"xr{i}", [128, 512], F32) for i in range(3)]
        XR = [Buf(f"xr{i}") for i in range(3)]

        bounds = FFN_BOUNDS
        NG = len(bounds) - 1
        WG, WU, WD = sh.WG, sh.WU, sh.WD
        c.dma("sp", "ld_g", gbc[:], gvec_d.partition_broadcast(128), writes=[GBC])
        if not sh.loaded:
            ffn_load_gu(c, sh, wg_d, wu_d, range(0, 4))
            ffn_load_d(c, sh, wd_d)
            ffn_load_gu(c, sh, wg_d, wu_d, range(4, NG))
        sh.loaded = False

        def grp(col):
            for j in range(NG):
                if bounds[j] <= col < bounds[j + 1]:
                    return j

        norm_to_xnT(c, g, 0, x_src, XSRC, gbc, GBC, t, xnT[0], XNT[0], S)
        for i in range(NB):
            b = i % 2
            for fc in range(NFC):
                pg, PG = g.ps[fc % 2], g.PS[fc % 2]
                pu, PU = g.ps[2 + fc % 2], g.PS[2 + fc % 2]
                j = grp(fc * 128)
                j2 = grp(fc * 128 + 127)
                wbufs_g = [WG[j]] + ([WG[j2]] if j2 != j else [])
                wbufs_u = [WU[j]] + ([WU[j2]] if j2 != j else [])
                c.group("pe", [
                    (lambda e, kc=kc: e.matmul(pg[:], wg[:, kc, fc * 128:(fc + 1) * 128], xnT[b][:, kc, :],
                                               start=(kc == 0), stop=(kc == 7)))
                    for kc in range(8)], reads=[XNT[b]] + wbufs_g, writes=[PG])
                c.group("pe", [
                    (lambda e, kc=kc: e.matmul(pu[:], wu[:, kc, fc * 128:(fc + 1) * 128], xnT[b][:, kc, :],
                                               start=(kc == 0), stop=(kc == 7)))
                    for kc in range(8)], reads=[XNT[b]] + wbufs_u, writes=[PU])
                c.op("act", lambda e: e.activation(out=sg[fc % 2][:], in_=pg[:], func=AF.Silu),
                     reads=[PG], writes=[SG[fc % 2]])
                c.op("dve", lambda e: e.tensor_tensor(out=hT[:, fc, :], in0=sg[fc % 2][:], in1=pu[:], op=ALU.mult),
                     reads=[SG[fc % 2], PU], writes=[HT])
                if i + 1 < NB:
                    if fc in (0, 5, 10, 15):
                        norm_tile(c, g, i + 1, fc // 5, x_src, XSRC, gbc, GBC, t, stages="A")
                    if fc in (2, 7, 12, 17):
                        norm_tile(c, g, i + 1, (fc - 2) // 5, x_src, XSRC, gbc, GBC, t, stages="B")
                    if fc in (4, 9, 14, 19):
                        norm_tile(c, g, i + 1, (fc - 4) // 5, x_src, XSRC, gbc, GBC, t, stages="C")
            if i + 1 < NB:
                for tt in range(4):
                    transpose_tile(c, g, tt, t, xnT[1 - b], XNT[1 - b])
            elif prefetch is not None:
                ffn_load_gu(c, sh, prefetch[0], prefetch[1], range(0, NG))
            n = 0
            for tt in range(4):
                for dh in range(2):
                    py, PY = g.ps[4 + n % 2], g.PS[4 + n % 2]
                    r0 = (i * 4 + tt) * 128
                    xs = (i * 8 + n) % 3
                    c.dma("sp", f"ld_xr{xs}", xr[xs][:], x_src[r0:r0 + 128, dh * 512:(dh + 1) * 512],
                          reads=[XSRC], writes=[XR[xs]])
                    c.group("pe", [
                        (lambda e, fc=fc: e.matmul(py[:], hT[:, fc, tt * 128:(tt + 1) * 128],
                                                   wd[:, fc, dh * 512:(dh + 1) * 512],
                                                   start=(fc == 0), stop=(fc == NFC - 1)))
                        for fc in range(NFC)], reads=[HT, WD[0], WD[1]], writes=[PY])
                    c.op("dve", lambda e: e.scalar_tensor_tensor(out=xr[xs][:], in0=py[:], scalar=0.5, in1=xr[xs][:],
                                                                 op0=ALU.mult, op1=ALU.add),
                         reads=[PY], writes=[XR[xs]])
                    c.dma("sp", f"st_xr{xs}", x_dst[r0:r0 + 128, dh * 512:(dh + 1) * 512], xr[xs][:],
                          reads=[XR[xs]], writes=[XDST])
                    n += 1
        if prefetch is not None:
            ffn_load_d(c, sh, prefetch[2])
            sh.loaded = True
            c.barrier(exclude_prefix="ld_w")
        else:
            c.barrier()


def phase_final(c, g, gvec_d, x_src, XSRC, x_dst, XDST, S, pfx):
    nc = c.nc
    with ExitStack() as st:
        A = lambda name, shape, dt: st.enter_context(nc.sbuf_tensor(pfx + name, shape, dt))
        gbc = A("gbc", [128, D], F32)
        GBC = Buf("gbc")
        t = alloc_norm_tiles(nc, st, pfx)
        yo = [A(f"yo{i}", [128, D], F32) for i in range(2)]
        YO = [Buf(f"yo{i}") for i in range(2)]
        c.dma("sp", "ld_g", gbc[:], gvec_d.partition_broadcast(128), writes=[GBC])
        for n in range(S // 128):
            s = n % 2
            r0 = n * 128
            c.dma("sp", f"ld_xt{s}", t.xt[s][:], x_src[r0:r0 + 128, :], reads=[XSRC], writes=[t.XT[s]])
            rmsnorm_rstd(c, g, t.xt[s][:], t.XT[s], t.junk[:], t.JUNK, t.ss[s][:], t.SS[s], t.rs[s][:], t.RS[s])
            c.op("dve", lambda e: e.scalar_tensor_tensor(out=yo[s][:], in0=t.xt[s][:], scalar=t.rs[s][:, 0:1],
                                                         in1=gbc[:], op0=ALU.mult, op1=ALU.mult),
                 reads=[t.XT[s], t.RS[s], GBC], writes=[YO[s]])
            c.dma("sp", f"st_yo{s}", x_dst[r0:r0 + 128, :], yo[s][:], reads=[YO[s]], writes=[XDST])
        c.barrier()


def alloc_scratch(nc, S):
    sc = T()
    d = lambda name, shape, dtype: nc.dram_tensor(name, shape, dtype).ap()
    sc.qTd = d("s_qTd", [384, S], BF16)
    sc.kTd = d("s_kTd", [384, S], BF16)
    sc.qTf = d("s_qTf", [6, 65, S], BF16)
    sc.kTf = d("s_kTf", [6, 64, S], BF16)
    sc.qTc = d("s_qTc", [256, S], BF16)
    sc.kTc = d("s_kTc", [256, S], BF16)
    sc.v = d("s_v", [S, 1024], BF16)
    sc.ckm = d("s_ckm", [128, (S // 128) * 6], F32)
    sc.cend = d("s_cend", [6, S // 512], F32)
    return sc


WQ_AQ, WQ_AK, WQ_FQ, WQ_FK, WQ_CQ, WQ_CK = 0, 384, 768, 1152, 1536, 1792
WQ_RAQ, WQ_RAK, WQ_RCQ, WQ_RCK = 2048, 2432, 2816, 3072
WQ_FG = 3328
WQ_COLS = 3334


def phase_proj(c, g, sc, w_in_d, gvec_d, bf_d, consts, x_src, S, pfx):
    nc = c.nc
    NB = S // 512
    NT = S // 128
    with ExitStack() as st:
        A = lambda name, shape, dt: st.enter_context(nc.sbuf_tensor(pfx + name, shape, dt))
        wq = A("wq", [128, 8, WQ_COLS], BF16)
        wv = A("wv", [128, 8, 1024], BF16)
        gbc = A("gbc", [128, D], F32)
        GBC = Buf("gbc")
        t = alloc_norm_tiles(nc, st, pfx)
        xnT = [A(f"xnT{i}", [128, 8, 512], BF16) for i in range(2)]
        XNT = [Buf(f"xnT{i}") for i in range(2)]
        rt = [[A(f"rt{i}_{j}", [128, 512], F32) for j in range(4)] for i in range(2)]
        RT = [[Buf(f"rt{i}_{j}") for j in range(4)] for i in range(2)]
        tm1 = [A(f"tm1_{i}", [128, 512], F32) for i in range(2)]
        TM1 = [Buf(f"tm1_{i}") for i in range(2)]
        tm2 = [A(f"tm2_{i}", [128, 512], F32) for i in range(2)]
        TM2 = [Buf(f"tm2_{i}") for i in range(2)]
        ob = [A(f"ob{i}", [128, 512], BF16) for i in range(3)]
        OB = [Buf(f"ob{i}") for i in range(3)]
        vb = [A(f"vb{i}", [128, 1024], BF16) for i in range(2)]
        VB = [Buf(f"vb{i}") for i in range(2)]
        nbf = A("nbf", [6, 1], F32)
        NBF = Buf("nbf")
        uu = A("uu", [6, 512], F32)
        UU = Buf("uu")
        lf = A("lf", [6, 512], F32)
        LF = Buf("lf")
        cb = [A(f"cb{i}", [6, 512], F32) for i in range(2)]
        CB = [Buf(f"cb{i}") for i in range(2)]
        augb = A("augb", [6, 512], BF16)
        AUGB = Buf("augb")
        ones6 = A("ones6", [6, 512], F32)
        ONES6 = Buf("ones6")
        ckm = A("ckm", [128, NT, 6], F32)
        CKM = Buf("ckm")

        WQP = [Buf(f"wqp{j}") for j in range(4)]
        WQR = {0: Buf("wqr0"), 2: Buf("wqr2")}
        WV = [Buf(f"wv{j}") for j in range(3)]
        wv_ = w_in_d.rearrange("(kc p) f -> p kc f", p=128)
        c.dma("sp", "ld_g", gbc[:], gvec_d.partition_broadcast(128), writes=[GBC])
        c.dma("sp", "ld_c1", nbf[:], bf_d.rearrange("(p o) -> p o", o=1), writes=[NBF])
        c.op("dve", lambda e: e.tensor_scalar(out=nbf[:], in0=nbf[:], scalar1=-1.0, scalar2=None, op0=ALU.mult),
             reads=[], writes=[NBF])
        c.op("dve", lambda e: e.memset(ones6[:], 1.0), writes=[ONES6])
        k = 0
        for (pj, dst, srcc, n) in [(3, WQ_FG, O_FG, 6), (0, WQ_AQ, O_AQ, 768), (1, WQ_FQ, O_FQ, 768),
                                   (2, WQ_CQ, O_CQ, 512)]:
            c.dma("pool", f"ld_w{k % 4}", wq[:, :, dst:dst + n], wv_[:, :, srcc:srcc + n], writes=[WQP[pj]])
            k += 1
        for j, (dst, srcc, n) in enumerate([(0, O_AV, 384), (384, O_FV, 384), (768, O_CV, 256)]):
            c.dma("pool", f"ld_w{k % 4}", wv[:, :, dst:dst + n], wv_[:, :, srcc:srcc + n], writes=[WV[j]])
            k += 1
        for (pj, dst, srcc, n, half) in [(0, WQ_RAQ, WQ_AQ, 768, 16), (2, WQ_RCQ, WQ_CQ, 512, 32)]:
            for kc in range(8):
                sv = wq[:, kc, srcc:srcc + n].rearrange("p (u e) -> p u e", e=2 * half)
                dv = wq[:, kc, dst:dst + n].rearrange("p (u e) -> p u e", e=2 * half)
                if kc % 2 == 0:
                    c.op("dve", lambda e: e.tensor_scalar(out=dv[:, :, 0:half], in0=sv[:, :, half:2 * half],
                                                          scalar1=-1.0, scalar2=None, op0=ALU.mult),
                         reads=[WQP[pj]], writes=[WQR[pj]])
                    c.op("dve", lambda e: e.tensor_copy(out=dv[:, :, half:2 * half], in_=sv[:, :, 0:half]),
                         reads=[WQP[pj]], writes=[WQR[pj]])
                else:
                    c.op("act", lambda e: e.mul(out=dv[:, :, 0:half], in_=sv[:, :, half:2 * half], mul=-1.0),
                         reads=[WQP[pj]], writes=[WQR[pj]])
                    c.op("act", lambda e: e.copy(out=dv[:, :, half:2 * half], in_=sv[:, :, 0:half]),
                         reads=[WQP[pj]], writes=[WQR[pj]])

        chunks = []
        for ci in range(16):
            col = ci * 128
            if ci < 6:
                chunks.append(("diff", col, WQ_RAQ + col, 0, 0))
            elif ci < 9:
                chunks.append(("fq", col, None, None, 1))
            elif ci < 12:
                chunks.append(("fk", col, None, None, 1))
            else:
                chunks.append(("dil", col, WQ_RCQ + (col - WQ_CQ), 2, 2))

        nob = 0

        def ld_rope(i):
            for j, nm in enumerate(["c_cosA", "c_sinA", "c_cosC", "c_sinC"]):
                c.dma("sp", f"ld_rt{j}", rt[i % 2][j][:], consts[nm][:, i * 512:(i + 1) * 512], writes=[RT[i % 2][j]])

        ld_rope(0)
        norm_to_xnT(c, g, 0, x_src, None, gbc, GBC, t, xnT[0], XNT[0], S)
        for i in range(NB):
            b = i % 2
            t0 = i * 512
            pf, PF = g.ps[6], g.PS[6]
            c.group("pe", [
                (lambda e, kc=kc: e.matmul(pf[0:6, :], wq[:, kc, WQ_FG:WQ_FG + 6], xnT[b][:, kc, :],
                                           start=(kc == 0), stop=(kc == 7)))
                for kc in range(8)], reads=[XNT[b], WQP[3]], writes=[PF])
            c.op("act", lambda e: e.activation(out=uu[:], in_=pf[0:6, :], func=AF.Exp, scale=-1.0, bias=nbf[:, 0:1]),
                 reads=[PF, NBF], writes=[UU])
            c.op("act", lambda e: e.activation(out=lf[:], in_=uu[:], func=AF.Ln, scale=1.0, bias=g.one_c[0:6, 0:1]),
                 reads=[UU], writes=[LF])
            init = 0.0 if i == 0 else cb[1 - b][:, 511:512]
            c.op("dve", lambda e: e.tensor_tensor_scan(out=cb[b][:], data0=ones6[:], data1=lf[:], initial=init,
                                                       op0=ALU.mult, op1=ALU.add),
                 reads=[LF, ONES6] + ([CB[1 - b]] if i > 0 else []), writes=[CB[b]])
            c.op("dve", lambda e: e.tensor_scalar(out=augb[:], in0=cb[b][:], scalar1=cb[b][:, 511:512], scalar2=-1.0,
                                                  op0=ALU.subtract, op1=ALU.mult),
                 reads=[CB[b]], writes=[AUGB])
            c.dma("sp", "st_aug", sc.qTf[:, 64, t0:t0 + 512], augb[:], reads=[AUGB])
            c.dma("sp", "st_cend", sc.cend[:, i:i + 1], cb[b][:, 511:512], reads=[CB[b]], allow_slow_non_contiguous=True)

            def fg_transposes():
                c.group("pe", [
                    (lambda e, tt=tt: e.transpose(out=pf[:, tt * 6:(tt + 1) * 6],
                                                  in_=cb[b][0:6, tt * 128:(tt + 1) * 128],
                                                  identity=g.identf[0:6, 0:6]))
                    for tt in range(4)], reads=[CB[b], g.IDENTF], writes=[PF])
                c.op("dve", lambda e: e.tensor_copy(out=ckm[:, i * 4:(i + 1) * 4, :],
                                                    in_=pf[:, 0:24].rearrange("p (t h) -> p t h", h=6)),
                     reads=[PF], writes=[CKM])
            for ci, (kind, col, rcol, rti, pj) in enumerate(chunks):
                if ci == 1 and i + 1 < NB:
                    ld_rope(i + 1)
                if ci == 4:
                    fg_transposes()
                if i + 1 < NB:
                    if ci % 4 == 0:
                        norm_tile(c, g, i + 1, ci // 4, x_src, None, gbc, GBC, t, stages="A")
                    if ci % 4 == 1:
                        norm_tile(c, g, i + 1, ci // 4, x_src, None, gbc, GBC, t, stages="B")
                    if ci % 4 == 3:
                        norm_tile(c, g, i + 1, ci // 4, x_src, None, gbc, GBC, t, stages="C")
                pa, PA = g.ps[ci % 2], g.PS[ci % 2]
                c.group("pe", [
                    (lambda e, kc=kc: e.matmul(pa[:], wq[:, kc, col:col + 128], xnT[b][:, kc, :],
                                               start=(kc == 0), stop=(kc == 7)))
                    for kc in range(8)], reads=[XNT[b], WQP[pj]], writes=[PA])
                o = nob % 3
                nob += 1
                if rcol is not None:
                    pb_, PB_ = g.ps[2 + ci % 2], g.PS[2 + ci % 2]
                    c.group("pe", [
                        (lambda e, kc=kc: e.matmul(pb_[:], wq[:, kc, rcol:rcol + 128], xnT[b][:, kc, :],
                                                   start=(kc == 0), stop=(kc == 7)))
                        for kc in range(8)], reads=[XNT[b], WQR[pj]], writes=[PB_])
                    s2 = ci % 2
                    c.op("dve", lambda e: e.tensor_tensor(out=tm1[s2][:], in0=pa[:], in1=rt[b][rti][:], op=ALU.mult),
                         reads=[PA, RT[b][rti]], writes=[TM1[s2]])
                    c.op("dve", lambda e: e.tensor_tensor(out=tm2[s2][:], in0=pb_[:], in1=rt[b][rti + 1][:], op=ALU.mult),
                         reads=[PB_, RT[b][rti + 1]], writes=[TM2[s2]])
                    c.op("pool", lambda e: e.tensor_tensor(out=ob[o][:], in0=tm1[s2][:], in1=tm2[s2][:], op=ALU.add),
                         reads=[TM1[s2], TM2[s2]], writes=[OB[o]])
                elif kind == "fq":
                    c.op("act", lambda e: e.activation(out=ob[o][:], in_=pa[:], func=AF.Copy, scale=0.125),
                         reads=[PA], writes=[OB[o]])
                else:
                    c.op("act", lambda e: e.copy(out=ob[o][:], in_=pa[:]), reads=[PA], writes=[OB[o]])
                if kind == "diff":
                    dstt = sc.qTd if ci < 3 else sc.kTd
                    r = (ci % 3) * 128
                    c.dma("sp", f"st_ob{o}", dstt[r:r + 128, t0:t0 + 512], ob[o][:], reads=[OB[o]])
                elif kind == "dil":
                    dstt = sc.qTc if ci < 14 else sc.kTc
                    r = (ci % 2) * 128
                    c.dma("sp", f"st_ob{o}", dstt[r:r + 128, t0:t0 + 512], ob[o][:], reads=[OB[o]])
                else:
                    dstt = sc.qTf if kind == "fq" else sc.kTf
                    h0 = 2 * ((ci - 6) % 3)
                    c.dma("sp", f"st_ob{o}", dstt[h0, 0:64, t0:t0 + 512], ob[o][0:64, :], reads=[OB[o]])
                    c.dma("sp", f"st_ob{o}b", dstt[h0 + 1, 0:64, t0:t0 + 512], ob[o][64:128, :], reads=[OB[o]])
            for tt in range(4):
                n = i * 4 + tt
                s = n % 2
                for hf in range(2):
                    pv, PV = g.ps[4 + hf], g.PS[4 + hf]
                    c.group("pe", [
                        (lambda e, kc=kc: e.matmul(pv[:], xnT[b][:, kc, tt * 128:(tt + 1) * 128],
                                                   wv[:, kc, hf * 512:(hf + 1) * 512],
                                                   start=(kc == 0), stop=(kc == 7)))
                        for kc in range(8)], reads=[XNT[b]] + WV, writes=[PV])
                    if hf == 0:
                        c.op("act", lambda e: e.copy(out=vb[s][:, 0:512], in_=pv[:]), reads=[PV], writes=[VB[s]])
                    else:
                        c.op("dve", lambda e: e.tensor_copy(out=vb[s][:, 512:1024], in_=pv[:]), reads=[PV], writes=[VB[s]])
                c.dma("sp", f"st_vb{s}", sc.v[n * 128:(n + 1) * 128, :], vb[s][:], reads=[VB[s]])
            if i + 1 < NB:
                for tt in range(4):
                    transpose_tile(c, g, tt, t, xnT[1 - b], XNT[1 - b])
        c.dma("sp", "st_ckm", sc.ckm, ckm[:].rearrange("p t h -> p (t h)"), reads=[CKM])
        c.barrier()


def phase_attn(c, g, sc, wo_d, lamq1_d, lamk1_d, lamq2_d, lamk2_d, gsub_d, lam_init, consts, x_src, x_dst, S, pfx,
               dbg=None):
    nc = c.nc
    NB = S // 512
    NT = S // 128
    with ExitStack() as st:
        A = lambda name, shape, dt: st.enter_context(nc.sbuf_tensor(pfx + name, shape, dt))
        oT = A("oT", [128, 8, S], BF16)
        OT = Buf("oT")
        wo = A("wo", [128, 8, D], BF16)
        WO = Buf("wo")
        qs = [A(f"qs{i}", [128, S], BF16) for i in range(2)]
        QS = [Buf(f"qs{i}") for i in range(2)]
        qz = [A(f"qz{i}", [128, S], BF16) for i in range(4)]
        QZ = [Buf(f"qz{i}") for i in range(4)]
        ks = [A(f"ks{i}", [128, S], BF16) for i in range(2)]
        KS = [Buf(f"ks{i}") for i in range(2)]
        va = [A(f"va{i}", [128, NT, 128], BF16) for i in range(2)]
        VA = [Buf(f"va{i}") for i in range(2)]
        NPT = ATT_LAG + 3
        pt = [A(f"pt{i}", [128, 1024 if ATT_PAIR else 512], BF16) for i in range(NPT)]
        PT = [Buf(f"pt{i}") for i in range(NPT)]
        tril = A("tril", [128, 128], BF16)
        TRIL = Buf("tril")
        wt = A("wt", [128, 2688], BF16)
        WT = Buf("wt")
        ckm = A("ckm", [128, NT, 6], F32)
        CKM = Buf("ckm")
        cend = A("cend", [128, 6 * NB], F32)
        CEND = Buf("cend")
        bq = [A(f"bq{i}", [128, NT], F32) for i in range(2)]
        BQ = [Buf(f"bq{i}") for i in range(2)]
        rec = [A(f"rec{i}", [64, 512], F32) for i in range(2)]
        REC = [Buf(f"rec{i}") for i in range(2)]
        ta = A("ta", [64, 512], F32)
        TA = Buf("ta")
        tb = A("tb", [64, 512], F32)
        TB = Buf("tb")
        td2 = [A(f"td{i}", [64, 512], F32) for i in range(2)]
        TD2 = [Buf(f"td{i}") for i in range(2)]
        tsq2 = [A(f"tsq{i}", [64, 512], BF16) for i in range(2)]
        TSQ2 = [Buf(f"tsq{i}") for i in range(2)]
        tr2 = [A(f"tr{i}", [64, 512], F32) for i in range(2)]
        TR2 = [Buf(f"tr{i}") for i in range(2)]
        lv = [A(f"lv{i}", [64, 32], F32) for i in range(4)]
        LV = [Buf(f"lv{i}") for i in range(4)]
        lj = A("lj", [64, 32], F32)
        LJ = Buf("lj")
        sm = A("sm", [64, 8], F32)
        SM = Buf("sm")
        xr = [A(f"xr{i}", [128, 512], F32) for i in range(3)]
        XR = [Buf(f"xr{i}") for i in range(3)]

        c.dma("sp", "ld_c0", tril[:], consts["c_tril"], writes=[TRIL])
        c.dma("sp", "ld_c1", wt[:], consts["c_wt"], writes=[WT])
        c.dma("sp", "ld_c2", ckm[:].rearrange("p t h -> p (t h)"), sc.ckm, writes=[CKM])
        c.dma("sp", "ld_c3", cend[:], sc.cend.rearrange("h q -> (h q)").partition_broadcast(128), writes=[CEND])
        wo_v = wo_d.rearrange("(cc p) d -> p cc d", p=128)
        c.dma("pool", "ld_w0", wo[:], wo_v, writes=[WO])
        for j, dd in enumerate([lamq1_d, lamk1_d, lamq2_d, lamk2_d]):
            c.dma("sp", f"ld_lv{j}", lv[j][:], dd.partition_broadcast(64), writes=[LV[j]])
        c.dma("sp", "ld_c4", sm[:, 5:6], gsub_d.rearrange("(p o) -> p o", o=1), writes=[SM])
        for j in range(2):
            c.op("dve", lambda e: e.scalar_tensor_tensor(out=lj[:], in0=lv[2 * j][:], scalar=1.0, in1=lv[2 * j + 1][:],
                                                         op0=ALU.mult, op1=ALU.mult, accum_out=sm[:, j:j + 1]),
                 reads=[LV[2 * j], LV[2 * j + 1]], writes=[LJ, SM])
        c.op("act", lambda e: e.activation(out=sm[:, 2:4], in_=sm[:, 0:2], func=AF.Exp), reads=[], writes=[SM])
        c.op("dve", lambda e: e.tensor_tensor(out=sm[:, 4:5], in0=sm[:, 2:3], in1=sm[:, 3:4], op=ALU.subtract),
             reads=[], writes=[SM])
        c.op("dve", lambda e: e.tensor_scalar(out=sm[:, 4:5], in0=sm[:, 4:5], scalar1=float(lam_init), scalar2=-1.0,
                                              op0=ALU.add, op1=ALU.mult), reads=[], writes=[SM])
        c.op("dve", lambda e: e.tensor_scalar(out=sm[:, 5:6], in0=sm[:, 5:6], scalar1=float(1.0 - lam_init),
                                              scalar2=None, op0=ALU.mult), reads=[], writes=[SM])
        for i in range(2):
            c.op("pool", lambda e: e.memset(va[i][:, :, 64:128], 1.0), writes=[VA[i]])
        for i in range(4):
            c.op("pool" if i % 2 == 0 else "dve", lambda e: e.memset(qz[i][:], 0.0), writes=[QZ[i]])
        pm = g.psT[:].bitcast(F32)
        PM = g.PST

        vsrc = sc.v.rearrange("(kb p) c -> p kb c", p=128)
        state = {"sb": 0, "pt": 0, "ob": 0, "rec": 0, "bq": 0}
        SCALE_D = 32 ** -0.5

        heads = []
        for h in range(6):
            cd, hl = h // 2, h % 2
            ksl = cd % 2

            def loads(h=h, cd=cd, hl=hl, ksl=ksl):
                if hl == 0:
                    c.dma("sp", f"ld_ks{ksl}", ks[ksl][:], sc.kTd[cd * 128:(cd + 1) * 128, :], writes=[KS[ksl]])
                for pz in (2 * hl, 2 * hl + 1):
                    r = cd * 128 + pz * 32
                    c.dma("sp", f"ld_qz{pz}", qz[pz][pz * 32:(pz + 1) * 32, :], sc.qTd[r:r + 32, :], writes=[QZ[pz]])
                c.dma("sp", f"ld_va{h % 2}", va[h % 2][:, :, 0:64], vsrc[:, :, h * 64:(h + 1) * 64], writes=[VA[h % 2]])
            heads.append(dict(kind="diff", hh=h, vs=h % 2, loads=loads, K=128, scale=SCALE_D, mask="causal",
                              units=[(qz[2 * hl], QZ[2 * hl], ks[ksl], KS[ksl]),
                                     (qz[2 * hl + 1], QZ[2 * hl + 1], ks[ksl], KS[ksl])]))
        for h in range(6):
            sl = (6 + h) % 2
            ksl = (3 + h) % 2

            def loads(h=h, sl=sl, ksl=ksl):
                c.dma("sp", f"ld_qs{sl}", qs[sl][0:65, :], sc.qTf[h], writes=[QS[sl]])
                c.dma("sp", f"ld_ks{ksl}", ks[ksl][0:64, :], sc.kTf[h], writes=[KS[ksl]])
                c.op("pool", lambda e: e.memset(ks[ksl][64:65, :], 1.0), writes=[KS[ksl]])
                c.dma("sp", f"ld_va{sl}", va[sl][:, :, 0:64], vsrc[:, :, (6 + h) * 64:(7 + h) * 64], writes=[VA[sl]])
            heads.append(dict(kind="fox", hh=6 + h, fh=h, vs=sl, loads=loads, K=65, scale=1.0, mask="causal",
                              units=[(qs[sl], QS[sl], ks[ksl], KS[ksl])]))
        for h in range(4):
            cc, hl = h // 2, h % 2
            ksl = (9 + cc) % 2
            zi = 0 if hl == 0 else 3
            sl = (12 + h) % 2

            def loads(h=h, cc=cc, hl=hl, ksl=ksl, zi=zi, sl=sl):
                if hl == 0:
                    c.dma("sp", f"ld_ks{ksl}", ks[ksl][:], sc.kTc[cc * 128:(cc + 1) * 128, :], writes=[KS[ksl]])
                r = cc * 128 + hl * 64
                c.dma("sp", f"ld_qz{zi}", qz[zi][hl * 64:(hl + 1) * 64, :], sc.qTc[r:r + 64, :], writes=[QZ[zi]])
                c.dma("sp", f"ld_va{sl}", va[sl][:, :, 0:64], vsrc[:, :, (12 + h) * 64:(13 + h) * 64], writes=[VA[sl]])
            heads.append(dict(kind="dil", hh=12 + h, vs=sl, loads=loads, K=128, scale=0.125, mask="dil",
                              units=[(qz[zi], QZ[zi], ks[ksl], KS[ksl])]))

        items = []
        for hi, hd in enumerate(heads):
            for qb in range(NB):
                for ui, un in enumerate(hd["units"]):
                    kb_lo = max(0, 4 * qb - 16) if hd["kind"] == "dil" else 0
                    kbs = list(range(kb_lo, 4 * qb + 4))
                    grp_ = []
                    kk = 0
                    while kk < len(kbs):
                        if hd["kind"] != "fox" and ATT_PAIR and kbs[kk] + 1 < 4 * qb and kk + 1 < len(kbs):
                            grp_.append([kbs[kk], kbs[kk + 1]])
                            kk += 2
                        else:
                            grp_.append([kbs[kk]])
                            kk += 1
                    for n_, kbl in enumerate(grp_):
                        items.append(dict(hi=hi, hd=hd, qb=qb, ui=ui, un=un, kbl=kbl, kb=kbl[0], first=(n_ == 0),
                                          last=(n_ == len(grp_) - 1),
                                          head_start=(qb == 0 and ui == 0 and n_ == 0)))

        deferred = []

        def emit_S(it):
            hd = it["hd"]
            qt, QTB, kt, KTB = it["un"]
            qb, kb = it["qb"], it["kb"]
            q0 = qb * 512
            if it["first"] and hd["kind"] == "fox":
                bi = state["bq"] % 2
                state["bq"] += 1
                nk = 4 * qb + 4
                fh = hd["fh"]
                c.op("dve", lambda e: e.tensor_scalar(out=bq[bi][:, 0:nk], in0=ckm[:, 0:nk, fh],
                                                      scalar1=cend[:, fh * NB + qb:fh * NB + qb + 1], scalar2=None,
                                                      op0=ALU.subtract), reads=[CKM, CEND], writes=[BQ[bi]])
                hd["bi"] = bi
            if hd["kind"] == "fox":
                it["bi"] = hd["bi"]
            K = hd["K"]
            if len(it["kbl"]) == 2:
                if state["sb"] % 2 == 1:
                    state["sb"] += 1
                ti = (state["sb"] % 4) // 2
                state["sb"] += 2
                ptile = g.pp[ti]
                c.group("pe", [
                    (lambda e, m=m, kbm=kbm: e.matmul(ptile[:, m * 512:(m + 1) * 512],
                                                      kt[0:K, kbm * 128:(kbm + 1) * 128],
                                                      qt[0:K, q0:q0 + 512], start=True, stop=True))
                    for m, kbm in enumerate(it["kbl"])], reads=[QTB, KTB], writes=[g.PS[2 * ti], g.PS[2 * ti + 1]])
                it["S"] = (-1, 0, ptile, [g.PS[2 * ti], g.PS[2 * ti + 1]])
                return
            j = kb - 4 * qb
            col0 = max(0, j) * 128
            si = state["sb"] % 4
            state["sb"] += 1
            sbk, SBK = g.ps[si], g.PS[si]
            if hd["mask"] == "causal" and j >= 0:
                c.group("pe", [
                    lambda e: e.matmul(sbk[:, col0:512], kt[0:K, kb * 128:(kb + 1) * 128],
                                       qt[0:K, q0 + col0:q0 + 512], start=True, stop=False),
                    lambda e: e.matmul(sbk[:, col0:col0 + 128], g.ident[:], tril[:], start=False, stop=True),
                ], reads=[QTB, KTB, TRIL, g.IDENT], writes=[SBK])
            else:
                c.op("pe", lambda e: e.matmul(sbk[:, col0:512], kt[0:K, kb * 128:(kb + 1) * 128],
                                              qt[0:K, q0 + col0:q0 + 512], start=True, stop=True),
                     reads=[QTB, KTB], writes=[SBK])
            it["S"] = (j, col0, sbk, [SBK])

        def plain_epilogue(obk, OBK, hh, qb, use_act=False):
            q0 = qb * 512
            ri = state["rec"] % 2
            state["rec"] += 1
            pb = (hh % 2) * 64
            if use_act:
                c.op("act", lambda e: e.activation(out=rec[ri][:], in_=obk[64:128, :], func=AF.Ln),
                     reads=[OBK], writes=[REC[ri]])
                c.op("act", lambda e: e.activation(out=rec[ri][:], in_=rec[ri][:], func=AF.Exp, scale=-1.0),
                     reads=[], writes=[REC[ri]])
            else:
                c.op("dve", lambda e: e.reciprocal(out=rec[ri][:], in_=obk[64:128, :]), reads=[OBK],
                     writes=[REC[ri]])
            c.op("dve", lambda e: e.tensor_tensor(out=oT[pb:pb + 64, hh // 2, q0:q0 + 512], in0=obk[0:64, :],
                                                  in1=rec[ri][:], op=ALU.mult), reads=[OBK, REC[ri]], writes=[OT])

        def diff_stage1(oa, OA, o2, O2, par):
            td, TD, tsq, TSQ = td2[par], TD2[par], tsq2[par], TSQ2[par]
            c.op("dve", lambda e: e.reciprocal(out=rec[0][:], in_=oa[64:128, :]), reads=[OA], writes=[REC[0]])
            c.op("dve", lambda e: e.tensor_tensor(out=ta[:], in0=oa[0:64, :], in1=rec[0][:], op=ALU.mult),
                 reads=[OA, REC[0]], writes=[TA])
            c.op("dve", lambda e: e.reciprocal(out=rec[1][:], in_=o2[64:128, :]), reads=[O2], writes=[REC[1]])
            c.op("dve", lambda e: e.tensor_tensor(out=tb[:], in0=o2[0:64, :], in1=rec[1][:], op=ALU.mult),
                 reads=[O2, REC[1]], writes=[TB])
            c.op("dve", lambda e: e.scalar_tensor_tensor(out=td[:], in0=tb[:], scalar=sm[:, 4:5], in1=ta[:],
                                                         op0=ALU.mult, op1=ALU.add), reads=[TA, TB, SM], writes=[TD])
            c.op("pool", lambda e: e.tensor_tensor(out=tsq[:], in0=td[:], in1=td[:], op=ALU.mult),
                 reads=[TD], writes=[TSQ])

        def diff_stage2(h, qb, par):
            td, TD, tsq, TSQ, tr, TR = td2[par], TD2[par], tsq2[par], TSQ2[par], tr2[par], TR2[par]
            q0 = qb * 512
            pb = (h % 2) * 64
            c.op("pe", lambda e: e.matmul(pm[0:64, :], g.ones64b[0:64, 0:64], tsq[:], start=True, stop=True),
                 reads=[TSQ, g.ONES64], writes=[PM])
            c.op("act", lambda e: e.activation(out=tr[:], in_=pm[0:64, :], func=AF.Ln, scale=1.0,
                                               bias=g.eps_sub[0:64, 0:1]), reads=[PM], writes=[TR])
            c.op("act", lambda e: e.activation(out=tr[:], in_=tr[:], func=AF.Exp, scale=-0.5), reads=[], writes=[TR])
            c.op("dve", lambda e: e.scalar_tensor_tensor(out=oT[pb:pb + 64, h // 2, q0:q0 + 512], in0=td[:],
                                                         scalar=sm[:, 5:6], in1=tr[:], op0=ALU.mult, op1=ALU.mult),
                 reads=[TD, TR, SM], writes=[OT])

        def emit_PV(i):
            it = items[i]
            hd = it["hd"]
            qb = it["qb"]
            p, P, col0 = it["P"]
            if it["first"]:
                obi = 4 + state["ob"] % 3
                state["ob"] += 1
                hd["cur_o"] = (g.ps[obi], g.PS[obi])
            obk, OBK = hd["cur_o"]
            vs = hd["vs"]
            nk = len(it["kbl"])
            if nk == 2:
                c.group("pe", [
                    (lambda e, m=m, kbm=kbm: e.matmul(obk[:, 0:512], va[vs][:, kbm, :], p[:, m * 512:(m + 1) * 512],
                                                      start=(it["first"] and m == 0), stop=(it["last"] and m == 1)))
                    for m, kbm in enumerate(it["kbl"])], reads=[P, VA[vs]], writes=[OBK])
            else:
                kb = it["kb"]
                c.op("pe", lambda e: e.matmul(obk[:, col0:512], va[vs][:, kb, :], p[:, col0:512],
                                              start=it["first"], stop=it["last"]), reads=[P, VA[vs]], writes=[OBK])
            if it["last"]:
                if hd["kind"] == "diff":
                    if it["ui"] == 0:
                        pend_diff[(it["hi"], qb)] = (obk, OBK)
                    else:
                        oa, OA = pend_diff.pop((it["hi"], qb))
                        par = npair[0] % 2
                        npair[0] += 1
                        while len(deferred) > 1:
                            deferred.pop(0)[1]()
                        diff_stage1(oa, OA, obk, OBK, par)
                        deferred.append((i + DEFER, (lambda h=hd["hh"], qb=qb, par=par: diff_stage2(h, qb, par))))
                elif hd["kind"] == "dil":
                    deferred.append((i + DIL_DEFER, (lambda obk=obk, OBK=OBK, hh=hd["hh"], qb=qb:
                                                     plain_epilogue(obk, OBK, hh, qb, use_act=True))))
                else:
                    plain_epilogue(obk, OBK, hd["hh"], qb)

        heads[0]["loads"]()
        heads[1]["loads"]()
        LAG = ATT_LAG
        PRE = ATT_PRE
        DEFER = 22
        npair = [0]
        for i in range(min(PRE, len(items))):
            emit_S(items[i])
        pend_diff = {}
        for i, it in enumerate(items):
            hd = it["hd"]
            if i >= LAG and items[i - LAG]["head_start"]:
                nh = items[i - LAG]["hi"] + 1
                if nh >= 2 and nh < len(heads):
                    heads[nh]["loads"]()
            j, col0, sbk, SBKS = it["S"]
            qb, kb = it["qb"], it["kb"]
            q0 = qb * 512
            pi = state["pt"] % NPT
            state["pt"] += 1
            p, P = pt[pi], PT[pi]
            w1 = 1024 if len(it["kbl"]) == 2 else 512
            if hd["kind"] == "fox":
                bi = it["bi"]
                c.op("act", lambda e: e.activation(out=p[:, col0:512], in_=sbk[:, col0:512], func=AF.Exp,
                                                   scale=hd["scale"], bias=bq[bi][:, kb:kb + 1]),
                     reads=SBKS + [BQ[bi]], writes=[P])
            else:
                c.op("act", lambda e: e.activation(out=p[:, col0:w1], in_=sbk[:, col0:w1], func=AF.Exp,
                                                   scale=hd["scale"]), reads=SBKS, writes=[P])
            if hd["mask"] != "causal":
                for m, kbm in enumerate(it["kbl"]):
                    off = q0 - kbm * 128 + col0
                    c.op("pool" if i % DIL_POOL_EVERY == DIL_POOL_EVERY - 1 else "dve",
                         lambda e: e.tensor_tensor(out=p[:, m * 512 + col0:(m + 1) * 512],
                                                   in0=p[:, m * 512 + col0:(m + 1) * 512],
                                                   in1=wt[:, off:off + 512 - col0], op=ALU.mult),
                         reads=[WT], writes=[P])
            if i + PRE < len(items):
                emit_S(items[i + PRE])
            it["P"] = (p, P, col0)
            if i >= LAG:
                emit_PV(i - LAG)
            while deferred and deferred[0][0] <= i:
                deferred.pop(0)[1]()
        for i2 in range(max(0, len(items) - LAG), len(items)):
            emit_PV(i2)
        while deferred:
            deferred.pop(0)[1]()

        if dbg is not None:
            for cc in range(8):
                c.dma("sp", "st_dbg", dbg[cc * 128:(cc + 1) * 128, :], oT[:, cc, :], reads=[OT])

        groups = [(i, tt, dh) for i in range(NB) for tt in range(4) for dh in range(2)]
        NXR = 4
        xr = xr + [A("xr3", [128, 512], F32)]
        XR = XR + [Buf("xr3")]

        def ld_x(n):
            i, tt, dh = groups[n]
            r0 = (i * 4 + tt) * 128
            c.dma("sp", f"ld_xr{n % NXR}", xr[n % NXR][:], x_src[r0:r0 + 128, dh * 512:(dh + 1) * 512],
                  writes=[XR[n % NXR]])

        ld_x(0)
        ld_x(1)
        for n, (i, tt, dh) in enumerate(groups):
            if n + 2 < len(groups):
                ld_x(n + 2)
            py, PY = g.ps[n % 2], g.PS[n % 2]
            r0 = (i * 4 + tt) * 128
            xs = n % NXR
            c.group("pe", [
                (lambda e, cc=cc: e.matmul(py[:], oT[:, cc, r0:r0 + 128], wo[:, cc, dh * 512:(dh + 1) * 512],
                                           start=(cc == 0), stop=(cc == 7)))
                for cc in range(8)], reads=[OT, WO], writes=[PY])
            c.op("dve", lambda e: e.tensor_tensor(out=xr[xs][:], in0=py[:], in1=xr[xs][:], op=ALU.add),
                 reads=[PY], writes=[XR[xs]])
            c.dma("sp", f"st_xr{xs}", x_dst[r0:r0 + 128, dh * 512:(dh + 1) * 512], xr[xs][:], reads=[XR[xs]])
        c.barrier()


def setup_globals(c, nc, consts):
    g = G()
    g.pp = [nc.alloc_psum_tensor(f"pp{i}", [128, 1024], F32) for i in range(2)]
    g.ps = [g.pp[0][:, 0:512], g.pp[0][:, 512:1024], g.pp[1][:, 0:512], g.pp[1][:, 512:1024]]
    g.ps += [nc.alloc_psum_tensor(f"ps{i}", [128, 512], F32) for i in range(4, 7)]
    g.PS = [Buf(f"ps{i}", excl=True) for i in range(7)]
    g.psT = nc.alloc_psum_tensor("psT", [128, 1024], BF16)
    g.PST = Buf("psT", excl=True)
    g.ident = nc.alloc_sbuf_tensor("ident", [128, 128], BF16)
    g.IDENT = Buf("ident")
    g.identf = nc.alloc_sbuf_tensor("identf", [128, 128], F32)
    g.IDENTF = Buf("identf")
    g.ones64 = nc.alloc_sbuf_tensor("ones64", [64, 64], F32)
    g.ones64b = nc.alloc_sbuf_tensor("ones64b", [64, 64], BF16)
    g.ONES64 = Buf("ones64")
    g.eps_norm = nc.alloc_sbuf_tensor("eps_norm", [128, 1], F32)
    g.eps_sub = nc.alloc_sbuf_tensor("eps_sub", [128, 1], F32)
    g.one_c = nc.alloc_sbuf_tensor("one_c", [128, 1], F32)
    g.EPS = Buf("eps")
    c.dma("sp", "ld_c0", g.ident[:], consts["c_ident"], writes=[g.IDENT])
    c.dma("sp", "ld_c1", g.identf[:], consts["c_identf"], writes=[g.IDENTF])
    c.op("dve", lambda e: e.memset(g.eps_norm[:], NORM_EPS), writes=[g.EPS])
    c.op("dve", lambda e: e.memset(g.eps_sub[:], SUBLN_EPS), writes=[g.EPS])
    c.op("dve", lambda e: e.memset(g.one_c[:], 1.0), writes=[g.EPS])
    c.op("dve", lambda e: e.memset(g.ones64[:], 1.0 / 64.0), writes=[g.ONES64])
    c.op("dve", lambda e: e.memset(g.ones64b[:], 1.0 / 64.0), writes=[g.ONES64])
    return g


W_NAMES = ["w_in", "b_f", "lam_q1", "lam_k1", "lam_q2", "lam_k2", "g_sub", "w_o", "g_ffn1", "w1_gate", "w1_up",
           "w1_down", "g_mix", "g_ffn2", "w2_gate", "w2_up", "w2_down", "g_final"]
W_SHAPES = {
    "w_in": [DEPTH, D, INW], "b_f": [DEPTH, 6], "lam_q1": [DEPTH, 32], "lam_k1": [DEPTH, 32], "lam_q2": [DEPTH, 32],
    "lam_k2": [DEPTH, 32], "g_sub": [DEPTH, 64], "w_o": [DEPTH, D, D], "g_ffn1": [DEPTH, D],
    "w1_gate": [DEPTH, D, DFF], "w1_up": [DEPTH, D, DFF], "w1_down": [DEPTH, DFF, D], "g_mix": [DEPTH, D],
    "g_ffn2": [DEPTH, D], "w2_gate": [DEPTH, D, DFF], "w2_up": [DEPTH, D, DFF], "w2_down": [DEPTH, DFF, D],
    "g_final": [D],
}


def const_shapes(S):
    return {"c_ident": ([128, 128], BF16), "c_identf": ([128, 128], F32), "c_tril": ([128, 128], BF16),
            "c_wt": ([128, 2688], BF16), "c_cosA": ([128, S], F32), "c_sinA": ([128, S], F32),
            "c_cosC": ([128, S], F32), "c_sinC": ([128, S], F32)}


def make_consts(S):
    bf = ml_dtypes.bfloat16
    cst = {}
    cst["c_ident"] = np.eye(128, dtype=np.float32).astype(bf)
    cst["c_identf"] = np.eye(128, dtype=np.float32)
    kl = np.arange(128)[:, None]
    ql = np.arange(128)[None, :]
    cst["c_tril"] = np.where(ql >= kl, 0.0, -30000.0).astype(np.float32).astype(bf)
    xx = np.arange(2688)[None, :]
    dl = xx - kl
    wmask = ((dl >= 0) & (dl <= 128)).astype(np.float32) + ((dl >= 0) & (dl <= 512) & (dl % 4 == 0)) \
        + ((dl >= 0) & (dl <= 2048) & (dl % 16 == 0))
    cst["c_wt"] = wmask.astype(np.float32).astype(bf)
    pos = np.arange(S, dtype=np.float32)
    for nm, half in (("A", 16), ("C", 32)):
        inv = (np.float32(10000.0) ** (-(np.arange(half, dtype=np.float32) / np.float32(half)))).astype(np.float32)
        ang = (pos[None, :] * inv[:, None]).astype(np.float32)
        rows = np.arange(128) % half
        a = ang[rows].astype(np.float64)
        cst["c_cos" + nm] = np.cos(a).astype(np.float32)
        cst["c_sin" + nm] = np.sin(a).astype(np.float32)
    return cst


def build_nc(S, layers, final, phases=("ffn1", "proj", "attn", "ffn2"), dbg=False):
    nc = bass.Bass("TRN2", target_bir_lowering=False)
    c = Ctx(nc)
    dt = lambda name, shape, dtype=F32: nc.dram_tensor(name, shape, dtype, kind="ExternalInput").ap()
    x_in = dt("x", [S, D])
    w = {nm: dt(nm, W_SHAPES[nm]) for nm in W_NAMES}
    consts = {nm: dt(nm, shp, dty) for nm, (shp, dty) in const_shapes(S).items()}
    out = nc.dram_tensor("out", [S, D], F32, kind="ExternalOutput").ap()
    dbg_ap = nc.dram_tensor("dbg", [1024, S], BF16, kind="ExternalOutput").ap() if dbg else None
    sc = alloc_scratch(nc, S)
    g = setup_globals(c, nc, consts)
    c.barrier()
    first = True

    def srcs():
        return x_in if first else out

    pending = None
    for idx, l in enumerate(layers):
        lam_init = 0.8 - 0.6 * math.exp(-0.3 * l)
        if "ffn1" in phases:
            if pending is not None:
                pst, sh = pending
                phase_ffn(c, g, w["w1_gate"][l], w["w1_up"][l], w["w1_down"][l], w["g_ffn1"][l], srcs(), None, out,
                          None, S, f"L{l}a_", shared=sh)
                pst.close()
                pending = None
            else:
                phase_ffn(c, g, w["w1_gate"][l], w["w1_up"][l], w["w1_down"][l], w["g_ffn1"][l], srcs(), None, out,
                          None, S, f"L{l}a_")
            first = False
        if "proj" in phases:
            phase_proj(c, g, sc, w["w_in"][l], w["g_mix"][l], w["b_f"][l], consts, srcs(), S, f"L{l}p_")
        if "attn" in phases:
            phase_attn(c, g, sc, w["w_o"][l], w["lam_q1"][l], w["lam_k1"][l], w["lam_q2"][l], w["lam_k2"][l],
                       w["g_sub"][l], lam_init, consts, srcs(), out, S, f"L{l}m_", dbg=dbg_ap)
            first = False
        if "ffn2" in phases:
            nxt = layers[idx + 1] if idx + 1 < len(layers) else None
            if nxt is not None and "ffn1" in phases and FFN_PREFETCH:
                pst = ExitStack()
                sh = alloc_ffn_weights(nc, pst, f"L{l}s_")
                phase_ffn(c, g, w["w2_gate"][l], w["w2_up"][l], w["w2_down"][l], w["g_ffn2"][l], srcs(), None, out,
                          None, S, f"L{l}c_", shared=sh,
                          prefetch=(w["w1_gate"][nxt], w["w1_up"][nxt], w["w1_down"][nxt]))
                pending = (pst, sh)
            else:
                phase_ffn(c, g, w["w2_gate"][l], w["w2_up"][l], w["w2_down"][l], w["g_ffn2"][l], srcs(), None, out,
                          None, S, f"L{l}c_")
            first = False
    if final:
        phase_final(c, g, w["g_final"], srcs(), None, out, None, S, "fin_")
    c.barrier()
    nc._ctx_ninst = c.ninst
    return nc


def kernel(**inputs):
    x = np.ascontiguousarray(inputs["x"], dtype=np.float32)
    B = x.shape[0]
    nc = build_nc(SEQ, list(range(DEPTH)), True)
    shared = {k: np.ascontiguousarray(inputs[k], dtype=np.float32) for k in W_NAMES}
    shared.update(make_consts(SEQ))
    in_maps = [dict(shared, x=x[b]) for b in range(B)]
    res = run_bass_kernel_spmd(nc, in_maps, core_ids=list(range(B)))
    return np.stack([r["out"] for r in res.results], axis=0)
```

```python
import math
from contextlib import ExitStack

import numpy as np
import ml_dtypes

import concourse.bass as bass
import concourse.mybir as mybir
from concourse.bass_utils import run_bass_kernel_spmd

F32 = mybir.dt.float32
BF16 = mybir.dt.bfloat16
AF = mybir.ActivationFunctionType
ALU = mybir.AluOpType

D = 1024
DFF = 2816
NFC = DFF // 128
DEPTH = 4
SEQ = 4096
NCORES = 8
INW = 3078
NORM_EPS = 1e-6
SUBLN_EPS = 1e-5
O_AQ, O_AK, O_AV, O_FQ, O_FK, O_FV, O_FG, O_CQ, O_CK, O_CV = 0, 384, 768, 1152, 1536, 1920, 2304, 2310, 2566, 2822


ATT_NSB = 4
ATT_PAIR = False
ATT_PRE = 2
ATT_LAG = 2
FFN_PREFETCH = True
DIL_DEFER = 6
DIL_POOL_EVERY = 10 ** 9


class Buf:
    def __init__(self, name, excl=False):
        self.name = name
        self.excl = excl
        self.w = {}
        self.r = {}


def _merge(d, tok):
    if tok is None:
        return
    s, v = tok
    if d.get(s, 0) < v:
        d[s] = v


class Ctx:
    def __init__(self, nc):
        self.nc = nc
        self.eng = {"pe": nc.tensor, "act": nc.scalar, "dve": nc.vector, "pool": nc.gpsimd, "sp": nc.sync}
        self.sems = {}
        self.cnt = {}
        self.seen = {}
        self.ninst = 0

    def sem(self, name):
        if name not in self.sems:
            self.sems[name] = self.nc.alloc_semaphore(name)
            self.cnt[name] = 0
        return self.sems[name]

    def wait(self, e, tok):
        if tok is None:
            return
        s, v = tok
        if v <= 0:
            return
        key = (e, s)
        if self.seen.get(key, 0) >= v:
            return
        self.eng[e].wait_ge(self.sems[s], v)
        self.seen[key] = v
        self.ninst += 1

    def _deps(self, e, reads, writes):
        reads = [b for b in reads if b is not None]
        writes = [b for b in writes if b is not None]
        for b in reads:
            for s, v in b.w.items():
                self.wait(e, (s, v))
            if b.excl:
                for s, v in b.r.items():
                    self.wait(e, (s, v))
        for b in writes:
            for s, v in b.w.items():
                self.wait(e, (s, v))
            for s, v in b.r.items():
                self.wait(e, (s, v))

    def _commit(self, tok, reads, writes):
        reads = [b for b in reads if b is not None]
        writes = [b for b in writes if b is not None]
        for b in reads:
            if b.excl:
                b.w = {tok[0]: tok[1]}
                b.r = {}
            else:
                _merge(b.r, tok)
        for b in writes:
            b.w = {tok[0]: tok[1]}
            b.r = {}

    def op(self, e, fn, reads=(), writes=()):
        self._deps(e, reads, writes)
        ins = fn(self.eng[e])
        s = "c_" + e
        self.sem(s)
        self.cnt[s] += 1
        ins.then_inc(self.sems[s], 1)
        self.ninst += 1
        tok = (s, self.cnt[s])
        self._commit(tok, reads, writes)
        return tok

    def group(self, e, fns, reads=(), writes=()):
        self._deps(e, reads, writes)
        ins = None
        for fn in fns:
            ins = fn(self.eng[e])
            self.ninst += 1
        s = "c_" + e
        self.sem(s)
        self.cnt[s] += 1
        ins.then_inc(self.sems[s], 1)
        tok = (s, self.cnt[s])
        self._commit(tok, reads, writes)
        return tok

    def dma(self, e, semname, out, in_, reads=(), writes=(), **kw):
        self.sem(semname)
        self._deps(e, reads, writes)
        self.wait(e, (semname, self.cnt[semname]))
        ins = self.eng[e].dma_start(out=out, in_=in_, **kw)
        self.cnt[semname] += 16
        ins.then_inc(self.sems[semname], 16)
        self.ninst += 1
        tok = (semname, self.cnt[semname])
        self._commit(tok, reads, writes)
        return tok

    def barrier(self, engines=("pe", "act", "dve", "pool", "sp"), exclude_prefix=None):
        for e in engines:
            for s, v in self.cnt.items():
                if exclude_prefix is not None and s.startswith(exclude_prefix):
                    continue
                self.wait(e, (s, v))


class G:
    pass


def rmsnorm_rstd(c, g, xt_ap, XT, junk, JUNK, ss, SS, rs, RS):
    c.op("act", lambda e: e.activation(out=junk, in_=xt_ap, func=AF.Square, accum_out=ss),
         reads=[XT], writes=[JUNK, SS])
    c.op("act", lambda e: e.activation(out=rs, in_=ss, func=AF.Ln, scale=1.0 / D, bias=g.eps_norm[:, 0:1]),
         reads=[SS], writes=[RS])
    c.op("act", lambda e: e.activation(out=rs, in_=rs, func=AF.Exp, scale=-0.5), reads=[], writes=[RS])


def norm_tile(c, g, i, tt, x_src, XSRC, gbc, GBC, t, stages="ABC"):
    n = i * 4 + tt
    s = n % 2
    r0 = n * 128
    if "A" in stages:
        c.dma("sp", f"ld_xt{s}", t.xt[s][:], x_src[r0:r0 + 128, :], reads=[XSRC], writes=[t.XT[s]])
    if "B" in stages:
        c.op("act", lambda e: e.activation(out=t.junk[:], in_=t.xt[s][:], func=AF.Square, accum_out=t.ss[s][:]),
             reads=[t.XT[s]], writes=[t.JUNK, t.SS[s]])
        c.op("act", lambda e: e.activation(out=t.rs[s][:], in_=t.ss[s][:], func=AF.Ln, scale=1.0 / D,
                                           bias=g.eps_norm[:, 0:1]), reads=[t.SS[s]], writes=[t.RS[s]])
        c.op("act", lambda e: e.activation(out=t.rs[s][:], in_=t.rs[s][:], func=AF.Exp, scale=-0.5),
             reads=[], writes=[t.RS[s]])
    if "C" in stages:
        c.op("dve", lambda e: e.scalar_tensor_tensor(out=t.xn[tt][:], in0=t.xt[s][:], scalar=t.rs[s][:, 0:1],
                                                     in1=gbc[:], op0=ALU.mult, op1=ALU.mult),
             reads=[t.XT[s], t.RS[s], GBC], writes=[t.XN[tt]])


def transpose_tile(c, g, tt, t, xnT, XNT):
    if tt % 2 == 0:
        pst, PSTB = g.psT[:], g.PST
    else:
        pst, PSTB = g.ps[6][:].bitcast(BF16), g.PS[6]
    c.group("pe", [
        (lambda e, kc=kc: e.transpose(out=pst[:, kc * 128:(kc + 1) * 128],
                                      in_=t.xn[tt][:, kc * 128:(kc + 1) * 128], identity=g.ident[:]))
        for kc in range(8)], reads=[t.XN[tt], g.IDENT], writes=[PSTB])
    src = pst.rearrange("p (k t) -> p k t", k=8)
    dst = xnT[:, :, tt * 128:(tt + 1) * 128]
    if tt % 2 == 0:
        c.op("act", lambda e: e.copy(out=dst, in_=src), reads=[PSTB], writes=[XNT])
    else:
        c.op("dve", lambda e: e.tensor_copy(out=dst, in_=src), reads=[PSTB], writes=[XNT])


def norm_to_xnT(c, g, i, x_src, XSRC, gbc, GBC, t, xnT, XNT, S):
    for tt in range(4):
        norm_tile(c, g, i, tt, x_src, XSRC, gbc, GBC, t)
        transpose_tile(c, g, tt, t, xnT, XNT)


class T:
    pass


def alloc_norm_tiles(nc, st, pfx):
    t = T()
    A = lambda name, shape, dt: st.enter_context(nc.sbuf_tensor(pfx + name, shape, dt))
    t.xt = [A(f"xt{i}", [128, D], F32) for i in range(2)]
    t.XT = [Buf(f"xt{i}") for i in range(2)]
    t.xn = [A(f"xn{i}", [128, D], BF16) for i in range(4)]
    t.XN = [Buf(f"xn{i}") for i in range(4)]
    t.junk = A("junk", [128, D], BF16)
    t.JUNK = Buf("junk")
    t.ss = [A(f"ss{i}", [128, 1], F32) for i in range(2)]
    t.SS = [Buf(f"ss{i}") for i in range(2)]
    t.rs = [A(f"rs{i}", [128, 1], F32) for i in range(2)]
    t.RS = [Buf(f"rs{i}") for i in range(2)]
    return t


FFN_BOUNDS = [0, 128, 384, 768, 1280, 1920, 2368, DFF]


def alloc_ffn_weights(nc, st, pfx):
    A = lambda name, shape, dt: st.enter_context(nc.sbuf_tensor(pfx + name, shape, dt))
    sh = T()
    sh.wg = A("wg", [128, 8, DFF], BF16)
    sh.wu = A("wu", [128, 8, DFF], BF16)
    sh.wd = A("wd", [128, NFC, D], BF16)
    NG = len(FFN_BOUNDS) - 1
    sh.WG = [Buf(f"wg{j}") for j in range(NG)]
    sh.WU = [Buf(f"wu{j}") for j in range(NG)]
    sh.WD = [Buf(f"wd{j}") for j in range(2)]
    sh.loaded = False
    sh.k = 0
    return sh


def ffn_load_gu(c, sh, wg_d, wu_d, groups):
    wg_v = wg_d.rearrange("(kc p) f -> p kc f", p=128)
    wu_v = wu_d.rearrange("(kc p) f -> p kc f", p=128)
    for j in groups:
        lo, hi = FFN_BOUNDS[j], FFN_BOUNDS[j + 1]
        c.dma("pool", f"ld_w{sh.k % 4}", sh.wg[:, :, lo:hi], wg_v[:, :, lo:hi], writes=[sh.WG[j]])
        sh.k += 1
        c.dma("pool", f"ld_w{sh.k % 4}", sh.wu[:, :, lo:hi], wu_v[:, :, lo:hi], writes=[sh.WU[j]])
        sh.k += 1


def ffn_load_d(c, sh, wd_d):
    wd_v = wd_d.rearrange("(fc p) d -> p fc d", p=128)
    for jj in range(2):
        c.dma("pool", f"ld_w{sh.k % 4}", sh.wd[:, jj * 11:(jj + 1) * 11, :], wd_v[:, jj * 11:(jj + 1) * 11, :],
              writes=[sh.WD[jj]])
        sh.k += 1


def phase_ffn(c, g, wg_d, wu_d, wd_d, gvec_d, x_src, XSRC, x_dst, XDST, S, pfx, shared=None, prefetch=None):
    nc = c.nc
    NB = S // 512
    with ExitStack() as st:
        A = lambda name, shape, dt: st.enter_context(nc.sbuf_tensor(pfx + name, shape, dt))
        sh = shared if shared is not None else alloc_ffn_weights(nc, st, pfx)
        wg, wu, wd = sh.wg, sh.wu, sh.wd
        gbc = A("gbc", [128, D], F32)
        GBC = Buf("gbc")
        t = alloc_norm_tiles(nc, st, pfx)
        xnT = [A(f"xnT{i}", [128, 8, 512], BF16) for i in range(2)]
        XNT = [Buf(f"xnT{i}") for i in range(2)]
        hT = A("hT", [128, NFC, 512], BF16)
        HT = Buf("hT")
        sg = [A(f"sg{i}", [128, 512], F32) for i in range(2)]
        SG = [Buf(f"sg{i}") for i in range(2)]
        xr = [A(f"xr{i}", [128, 512], F32) for i in range(3)]
        XR = [Buf(f"xr{i}") for i in range(3)]

        bounds = FFN_BOUNDS
        NG = len(bounds) - 1
        WG, WU, WD = sh.WG, sh.WU, sh.WD
        c.dma("sp", "ld_g", gbc[:], gvec_d.partition_broadcast(128), writes=[GBC])
        if not sh.loaded:
            ffn_load_gu(c, sh, wg_d, wu_d, range(0, 4))
            ffn_load_d(c, sh, wd_d)
            ffn_load_gu(c, sh, wg_d, wu_d, range(4, NG))
        sh.loaded = False

        def grp(col):
            for j in range(NG):
                if bounds[j] <= col < bounds[j + 1]:
                    return j

        norm_to_xnT(c, g, 0, x_src, XSRC, gbc, GBC, t, xnT[0], XNT[0], S)
        for i in range(NB):
            b = i % 2
            for fc in range(NFC):
                pg, PG = g.ps[fc % 2], g.PS[fc % 2]
                pu, PU = g.ps[2 + fc % 2], g.PS[2 + fc % 2]
                j = grp(fc * 128)
                j2 = grp(fc * 128 + 127)
                wbufs_g = [WG[j]] + ([WG[j2]] if j2 != j else [])
                wbufs_u = [WU[j]] + ([WU[j2]] if j2 != j else [])
                c.group("pe", [
                    (lambda e, kc=kc: e.matmul(pg[:], wg[:, kc, fc * 128:(fc + 1) * 128], xnT[b][:, kc, :],
                                               start=(kc == 0), stop=(kc == 7)))
                    for kc in range(8)], reads=[XNT[b]] + wbufs_g, writes=[PG])
                c.group("pe", [
                    (lambda e, kc=kc: e.matmul(pu[:], wu[:, kc, fc * 128:(fc + 1) * 128], xnT[b][:, kc, :],
                                               start=(kc == 0), stop=(kc == 7)))
                    for kc in range(8)], reads=[XNT[b]] + wbufs_u, writes=[PU])
                c.op("act", lambda e: e.activation(out=sg[fc % 2][:], in_=pg[:], func=AF.Silu),
                     reads=[PG], writes=[SG[fc % 2]])
                c.op("dve", lambda e: e.tensor_tensor(out=hT[:, fc, :], in0=sg[fc % 2][:], in1=pu[:], op=ALU.mult),
                     reads=[SG[fc % 2], PU], writes=[HT])
                if i + 1 < NB:
                    if fc in (0, 5, 10, 15):
                        norm_tile(c, g, i + 1, fc // 5, x_src, XSRC, gbc, GBC, t, stages="A")
                    if fc in (2, 7, 12, 17):
                        norm_tile(c, g, i + 1, (fc - 2) // 5, x_src, XSRC, gbc, GBC, t, stages="B")
                    if fc in (4, 9, 14, 19):
                        norm_tile(c, g, i + 1, (fc - 4) // 5, x_src, XSRC, gbc, GBC, t, stages="C")
            if i + 1 < NB:
                for tt in range(4):
                    transpose_tile(c, g, tt, t, xnT[1 - b], XNT[1 - b])
            elif prefetch is not None:
                ffn_load_gu(c, sh, prefetch[0], prefetch[1], range(0, NG))
            n = 0
            for tt in range(4):
                for dh in range(2):
                    py, PY = g.ps[4 + n % 2], g.PS[4 + n % 2]
                    r0 = (i * 4 + tt) * 128
                    xs = (i * 8 + n) % 3
                    c.dma("sp", f"ld_xr{xs}", xr[xs][:], x_src[r0:r0 + 128, dh * 512:(dh + 1) * 512],
                          reads=[XSRC], writes=[XR[xs]])
                    c.group("pe", [
                        (lambda e, fc=fc: e.matmul(py[:], hT[:, fc, tt * 128:(tt + 1) * 128],
                                                   wd[:, fc, dh * 512:(dh + 1) * 512],
                                                   start=(fc == 0), stop=(fc == NFC - 1)))
                        for fc in range(NFC)], reads=[HT, WD[0], WD[1]], writes=[PY])
                    c.op("dve", lambda e: e.scalar_tensor_tensor(out=xr[xs][:], in0=py[:], scalar=0.5, in1=xr[xs][:],
                                                                 op0=ALU.mult, op1=ALU.add),
                         reads=[PY], writes=[XR[xs]])
                    c.dma("sp", f"st_xr{xs}", x_dst[r0:r0 + 128, dh * 512:(dh + 1) * 512], xr[xs][:],
                          reads=[XR[xs]], writes=[XDST])
                    n += 1
        if prefetch is not None:
            ffn_load_d(c, sh, prefetch[2])
            sh.loaded = True
            c.barrier(exclude_prefix="ld_w")
        else:
            c.barrier()


def phase_final(c, g, gvec_d, x_src, XSRC, x_dst, XDST, S, pfx):
    nc = c.nc
    with ExitStack() as st:
        A = lambda name, shape, dt: st.enter_context(nc.sbuf_tensor(pfx + name, shape, dt))
        gbc = A("gbc", [128, D], F32)
        GBC = Buf("gbc")
        t = alloc_norm_tiles(nc, st, pfx)
        yo = [A(f"yo{i}", [128, D], F32) for i in range(2)]
        YO = [Buf(f"yo{i}") for i in range(2)]
        c.dma("sp", "ld_g", gbc[:], gvec_d.partition_broadcast(128), writes=[GBC])
        for n in range(S // 128):
            s = n % 2
            r0 = n * 128
            c.dma("sp", f"ld_xt{s}", t.xt[s][:], x_src[r0:r0 + 128, :], reads=[XSRC], writes=[t.XT[s]])
            rmsnorm_rstd(c, g, t.xt[s][:], t.XT[s], t.junk[:], t.JUNK, t.ss[s][:], t.SS[s], t.rs[s][:], t.RS[s])
            c.op("dve", lambda e: e.scalar_tensor_tensor(out=yo[s][:], in0=t.xt[s][:], scalar=t.rs[s][:, 0:1],
                                                         in1=gbc[:], op0=ALU.mult, op1=ALU.mult),
                 reads=[t.XT[s], t.RS[s], GBC], writes=[YO[s]])
            c.dma("sp", f"st_yo{s}", x_dst[r0:r0 + 128, :], yo[s][:], reads=[YO[s]], writes=[XDST])
        c.barrier()


def alloc_scratch(nc, S):
    sc = T()
    d = lambda name, shape, dtype: nc.dram_tensor(name, shape, dtype).ap()
    sc.qTd = d("s_qTd", [384, S], BF16)
    sc.kTd = d("s_kTd", [384, S], BF16)
    sc.qTf = d("s_qTf", [6, 65, S], BF16)
    sc.kTf = d("s_kTf", [6, 64, S], BF16)
    sc.qTc = d("s_qTc", [256, S], BF16)
    sc.kTc = d("s_kTc", [256, S], BF16)
    sc.v = d("s_v", [S, 1024], BF16)
    sc.ckm = d("s_ckm", [128, (S // 128) * 6], F32)
    sc.cend = d("s_cend", [6, S // 512], F32)
    return sc


WQ_AQ, WQ_AK, WQ_FQ, WQ_FK, WQ_CQ, WQ_CK = 0, 384, 768, 1152, 1536, 1792
WQ_RAQ, WQ_RAK, WQ_RCQ, WQ_RCK = 2048, 2432, 2816, 3072
WQ_FG = 3328
WQ_COLS = 3334


def phase_proj(c, g, sc, w_in_d, gvec_d, bf_d, consts, x_src, S, pfx):
    nc = c.nc
    NB = S // 512
    NT = S // 128
    with ExitStack() as st:
        A = lambda name, shape, dt: st.enter_context(nc.sbuf_tensor(pfx + name, shape, dt))
        wq = A("wq", [128, 8, WQ_COLS], BF16)
        wv = A("wv", [128, 8, 1024], BF16)
        gbc = A("gbc", [128, D], F32)
        GBC = Buf("gbc")
        t = alloc_norm_tiles(nc, st, pfx)
        xnT = [A(f"xnT{i}", [128, 8, 512], BF16) for i in range(2)]
        XNT = [Buf(f"xnT{i}") for i in range(2)]
        rt = [[A(f"rt{i}_{j}", [128, 512], F32) for j in range(4)] for i in range(2)]
        RT = [[Buf(f"rt{i}_{j}") for j in range(4)] for i in range(2)]
        tm1 = [A(f"tm1_{i}", [128, 512], F32) for i in range(2)]
        TM1 = [Buf(f"tm1_{i}") for i in range(2)]
        tm2 = [A(f"tm2_{i}", [128, 512], F32) for i in range(2)]
        TM2 = [Buf(f"tm2_{i}") for i in range(2)]
        ob = [A(f"ob{i}", [128, 512], BF16) for i in range(3)]
        OB = [Buf(f"ob{i}") for i in range(3)]
        vb = [A(f"vb{i}", [128, 1024], BF16) for i in range(2)]
        VB = [Buf(f"vb{i}") for i in range(2)]
        nbf = A("nbf", [6, 1], F32)
        NBF = Buf("nbf")
        uu = A("uu", [6, 512], F32)
        UU = Buf("uu")
        lf = A("lf", [6, 512], F32)
        LF = Buf("lf")
        cb = [A(f"cb{i}", [6, 512], F32) for i in range(2)]
        CB = [Buf(f"cb{i}") for i in range(2)]
        augb = A("augb", [6, 512], BF16)
        AUGB = Buf("augb")
        ones6 = A("ones6", [6, 512], F32)
        ONES6 = Buf("ones6")
        ckm = A("ckm", [128, NT, 6], F32)
        CKM = Buf("ckm")

        WQP = [Buf(f"wqp{j}") for j in range(4)]
        WQR = {0: Buf("wqr0"), 2: Buf("wqr2")}
        WV = [Buf(f"wv{j}") for j in range(3)]
        wv_ = w_in_d.rearrange("(kc p) f -> p kc f", p=128)
        c.dma("sp", "ld_g", gbc[:], gvec_d.partition_broadcast(128), writes=[GBC])
        c.dma("sp", "ld_c1", nbf[:], bf_d.rearrange("(p o) -> p o", o=1), writes=[NBF])
        c.op("dve", lambda e: e.tensor_scalar(out=nbf[:], in0=nbf[:], scalar1=-1.0, scalar2=None, op0=ALU.mult),
             reads=[], writes=[NBF])
        c.op("dve", lambda e: e.memset(ones6[:], 1.0), writes=[ONES6])
        k = 0
        for (pj, dst, srcc, n) in [(3, WQ_FG, O_FG, 6), (0, WQ_AQ, O_AQ, 768), (1, WQ_FQ, O_FQ, 768),
                                   (2, WQ_CQ, O_CQ, 512)]:
            c.dma("pool", f"ld_w{k % 4}", wq[:, :, dst:dst + n], wv_[:, :, srcc:srcc + n], writes=[WQP[pj]])
            k += 1
        for j, (dst, srcc, n) in enumerate([(0, O_AV, 384), (384, O_FV, 384), (768, O_CV, 256)]):
            c.dma("pool", f"ld_w{k % 4}", wv[:, :, dst:dst + n], wv_[:, :, srcc:srcc + n], writes=[WV[j]])
            k += 1
        for (pj, dst, srcc, n, half) in [(0, WQ_RAQ, WQ_AQ, 768, 16), (2, WQ_RCQ, WQ_CQ, 512, 32)]:
            for kc in range(8):
                sv = wq[:, kc, srcc:srcc + n].rearrange("p (u e) -> p u e", e=2 * half)
                dv = wq[:, kc, dst:dst + n].rearrange("p (u e) -> p u e", e=2 * half)
                if kc % 2 == 0:
                    c.op("dve", lambda e: e.tensor_scalar(out=dv[:, :, 0:half], in0=sv[:, :, half:2 * half],
                                                          scalar1=-1.0, scalar2=None, op0=ALU.mult),
                         reads=[WQP[pj]], writes=[WQR[pj]])
                    c.op("dve", lambda e: e.tensor_copy(out=dv[:, :, half:2 * half], in_=sv[:, :, 0:half]),
                         reads=[WQP[pj]], writes=[WQR[pj]])
                else:
                    c.op("act", lambda e: e.mul(out=dv[:, :, 0:half], in_=sv[:, :, half:2 * half], mul=-1.0),
                         reads=[WQP[pj]], writes=[WQR[pj]])
                    c.op("act", lambda e: e.copy(out=dv[:, :, half:2 * half], in_=sv[:, :, 0:half]),
                         reads=[WQP[pj]], writes=[WQR[pj]])

        chunks = []
        for ci in range(16):
            col = ci * 128
            if ci < 6:
                chunks.append(("diff", col, WQ_RAQ + col, 0, 0))
            elif ci < 9:
                chunks.append(("fq", col, None, None, 1))
            elif ci < 12:
                chunks.append(("fk", col, None, None, 1))
            else:
                chunks.append(("dil", col, WQ_RCQ + (col - WQ_CQ), 2, 2))

        nob = 0

        def ld_rope(i):
            for j, nm in enumerate(["c_cosA", "c_sinA", "c_cosC", "c_sinC"]):
                c.dma("sp", f"ld_rt{j}", rt[i % 2][j][:], consts[nm][:, i * 512:(i + 1) * 512], writes=[RT[i % 2][j]])

        ld_rope(0)
        norm_to_xnT(c, g, 0, x_src, None, gbc, GBC, t, xnT[0], XNT[0], S)
        for i in range(NB):
            b = i % 2
            t0 = i * 512
            pf, PF = g.ps[6], g.PS[6]
            c.group("pe", [
                (lambda e, kc=kc: e.matmul(pf[0:6, :], wq[:, kc, WQ_FG:WQ_FG + 6], xnT[b][:, kc, :],
                                           start=(kc == 0), stop=(kc == 7)))
                for kc in range(8)], reads=[XNT[b], WQP[3]], writes=[PF])
            c.op("act", lambda e: e.activation(out=uu[:], in_=pf[0:6, :], func=AF.Exp, scale=-1.0, bias=nbf[:, 0:1]),
                 reads=[PF, NBF], writes=[UU])
            c.op("act", lambda e: e.activation(out=lf[:], in_=uu[:], func=AF.Ln, scale=1.0, bias=g.one_c[0:6, 0:1]),
                 reads=[UU], writes=[LF])
            init = 0.0 if i == 0 else cb[1 - b][:, 511:512]
            c.op("dve", lambda e: e.tensor_tensor_scan(out=cb[b][:], data0=ones6[:], data1=lf[:], initial=init,
                                                       op0=ALU.mult, op1=ALU.add),
                 reads=[LF, ONES6] + ([CB[1 - b]] if i > 0 else []), writes=[CB[b]])
            c.op("dve", lambda e: e.tensor_scalar(out=augb[:], in0=cb[b][:], scalar1=cb[b][:, 511:512], scalar2=-1.0,
                                                  op0=ALU.subtract, op1=ALU.mult),
                 reads=[CB[b]], writes=[AUGB])
            c.dma("sp", "st_aug", sc.qTf[:, 64, t0:t0 + 512], augb[:], reads=[AUGB])
            c.dma("sp", "st_cend", sc.cend[:, i:i + 1], cb[b][:, 511:512], reads=[CB[b]], allow_slow_non_contiguous=True)

            def fg_transposes():
                c.group("pe", [
                    (lambda e, tt=tt: e.transpose(out=pf[:, tt * 6:(tt + 1) * 6],
                                                  in_=cb[b][0:6, tt * 128:(tt + 1) * 128],
                                                  identity=g.identf[0:6, 0:6]))
                    for tt in range(4)], reads=[CB[b], g.IDENTF], writes=[PF])
                c.op("dve", lambda e: e.tensor_copy(out=ckm[:, i * 4:(i + 1) * 4, :],
                                                    in_=pf[:, 0:24].rearrange("p (t h) -> p t h", h=6)),
                     reads=[PF], writes=[CKM])
            for ci, (kind, col, rcol, rti, pj) in enumerate(chunks):
                if ci == 1 and i + 1 < NB:
                    ld_rope(i + 1)
                if ci == 4:
                    fg_transposes()
                if i + 1 < NB:
                    if ci % 4 == 0:
                        norm_tile(c, g, i + 1, ci // 4, x_src, None, gbc, GBC, t, stages="A")
                    if ci % 4 == 1:
                        norm_tile(c, g, i + 1, ci // 4, x_src, None, gbc, GBC, t, stages="B")
                    if ci % 4 == 3:
                        norm_tile(c, g, i + 1, ci // 4, x_src, None, gbc, GBC, t, stages="C")
                pa, PA = g.ps[ci % 2], g.PS[ci % 2]
                c.group("pe", [
                    (lambda e, kc=kc: e.matmul(pa[:], wq[:, kc, col:col + 128], xnT[b][:, kc, :],
                                               start=(kc == 0), stop=(kc == 7)))
                    for kc in range(8)], reads=[XNT[b], WQP[pj]], writes=[PA])
                o = nob % 3
                nob += 1
                if rcol is not None:
                    pb_, PB_ = g.ps[2 + ci % 2], g.PS[2 + ci % 2]
                    c.group("pe", [
                        (lambda e, kc=kc: e.matmul(pb_[:], wq[:, kc, rcol:rcol + 128], xnT[b][:, kc, :],
                                                   start=(kc == 0), stop=(kc == 7)))
                        for kc in range(8)], reads=[XNT[b], WQR[pj]], writes=[PB_])
                    s2 = ci % 2
                    c.op("dve", lambda e: e.tensor_tensor(out=tm1[s2][:], in0=pa[:], in1=rt[b][rti][:], op=ALU.mult),
                         reads=[PA, RT[b][rti]], writes=[TM1[s2]])
                    c.op("dve", lambda e: e.tensor_tensor(out=tm2[s2][:], in0=pb_[:], in1=rt[b][rti + 1][:], op=ALU.mult),
                         reads=[PB_, RT[b][rti + 1]], writes=[TM2[s2]])
                    c.op("pool", lambda e: e.tensor_tensor(out=ob[o][:], in0=tm1[s2][:], in1=tm2[s2][:], op=ALU.add),
                         reads=[TM1[s2], TM2[s2]], writes=[OB[o]])
                elif kind == "fq":
                    c.op("act", lambda e: e.activation(out=ob[o][:], in_=pa[:], func=AF.Copy, scale=0.125),
                         reads=[PA], writes=[OB[o]])
                else:
                    c.op("act", lambda e: e.copy(out=ob[o][:], in_=pa[:]), reads=[PA], writes=[OB[o]])
                if kind == "diff":
                    dstt = sc.qTd if ci < 3 else sc.kTd
                    r = (ci % 3) * 128
                    c.dma("sp", f"st_ob{o}", dstt[r:r + 128, t0:t0 + 512], ob[o][:], reads=[OB[o]])
                elif kind == "dil":
                    dstt = sc.qTc if ci < 14 else sc.kTc
                    r = (ci % 2) * 128
                    c.dma("sp", f"st_ob{o}", dstt[r:r + 128, t0:t0 + 512], ob[o][:], reads=[OB[o]])
                else:
                    dstt = sc.qTf if kind == "fq" else sc.kTf
                    h0 = 2 * ((ci - 6) % 3)
                    c.dma("sp", f"st_ob{o}", dstt[h0, 0:64, t0:t0 + 512], ob[o][0:64, :], reads=[OB[o]])
                    c.dma("sp", f"st_ob{o}b", dstt[h0 + 1, 0:64, t0:t0 + 512], ob[o][64:128, :], reads=[OB[o]])
            for tt in range(4):
                n = i * 4 + tt
                s = n % 2
                for hf in range(2):
                    pv, PV = g.ps[4 + hf], g.PS[4 + hf]
                    c.group("pe", [
                        (lambda e, kc=kc: e.matmul(pv[:], xnT[b][:, kc, tt * 128:(tt + 1) * 128],
                                                   wv[:, kc, hf * 512:(hf + 1) * 512],
                                                   start=(kc == 0), stop=(kc == 7)))
                        for kc in range(8)], reads=[XNT[b]] + WV, writes=[PV])
                    if hf == 0:
                        c.op("act", lambda e: e.copy(out=vb[s][:, 0:512], in_=pv[:]), reads=[PV], writes=[VB[s]])
                    else:
                        c.op("dve", lambda e: e.tensor_copy(out=vb[s][:, 512:1024], in_=pv[:]), reads=[PV], writes=[VB[s]])
                c.dma("sp", f"st_vb{s}", sc.v[n * 128:(n + 1) * 128, :], vb[s][:], reads=[VB[s]])
            if i + 1 < NB:
                for tt in range(4):
                    transpose_tile(c, g, tt, t, xnT[1 - b], XNT[1 - b])
        c.dma("sp", "st_ckm", sc.ckm, ckm[:].rearrange("p t h -> p (t h)"), reads=[CKM])
        c.barrier()


def phase_attn(c, g, sc, wo_d, lamq1_d, lamk1_d, lamq2_d, lamk2_d, gsub_d, lam_init, consts, x_src, x_dst, S, pfx,
               dbg=None):
    nc = c.nc
    NB = S // 512
    NT = S // 128
    with ExitStack() as st:
        A = lambda name, shape, dt: st.enter_context(nc.sbuf_tensor(pfx + name, shape, dt))
        oT = A("oT", [128, 8, S], BF16)
        OT = Buf("oT")
        wo = A("wo", [128, 8, D], BF16)
        WO = Buf("wo")
        qs = [A(f"qs{i}", [128, S], BF16) for i in range(2)]
        QS = [Buf(f"qs{i}") for i in range(2)]
        qz = [A(f"qz{i}", [128, S], BF16) for i in range(4)]
        QZ = [Buf(f"qz{i}") for i in range(4)]
        ks = [A(f"ks{i}", [128, S], BF16) for i in range(2)]
        KS = [Buf(f"ks{i}") for i in range(2)]
        va = [A(f"va{i}", [128, NT, 128], BF16) for i in range(2)]
        VA = [Buf(f"va{i}") for i in range(2)]
        NPT = ATT_LAG + 3
        pt = [A(f"pt{i}", [128, 1024 if ATT_PAIR else 512], BF16) for i in range(NPT)]
        PT = [Buf(f"pt{i}") for i in range(NPT)]
        tril = A("tril", [128, 128], BF16)
        TRIL = Buf("tril")
        wt = A("wt", [128, 2688], BF16)
        WT = Buf("wt")
        ckm = A("ckm", [128, NT, 6], F32)
        CKM = Buf("ckm")
        cend = A("cend", [128, 6 * NB], F32)
        CEND = Buf("cend")
        bq = [A(f"bq{i}", [128, NT], F32) for i in range(2)]
        BQ = [Buf(f"bq{i}") for i in range(2)]
        rec = [A(f"rec{i}", [64, 512], F32) for i in range(2)]
        REC = [Buf(f"rec{i}") for i in range(2)]
        ta = A("ta", [64, 512], F32)
        TA = Buf("ta")
        tb = A("tb", [64, 512], F32)
        TB = Buf("tb")
        td2 = [A(f"td{i}", [64, 512], F32) for i in range(2)]
        TD2 = [Buf(f"td{i}") for i in range(2)]
        tsq2 = [A(f"tsq{i}", [64, 512], BF16) for i in range(2)]
        TSQ2 = [Buf(f"tsq{i}") for i in range(2)]
        tr2 = [A(f"tr{i}", [64, 512], F32) for i in range(2)]
        TR2 = [Buf(f"tr{i}") for i in range(2)]
        lv = [A(f"lv{i}", [64, 32], F32) for i in range(4)]
        LV = [Buf(f"lv{i}") for i in range(4)]
        lj = A("lj", [64, 32], F32)
        LJ = Buf("lj")
        sm = A("sm", [64, 8], F32)
        SM = Buf("sm")
        xr = [A(f"xr{i}", [128, 512], F32) for i in range(3)]
        XR = [Buf(f"xr{i}") for i in range(3)]

        c.dma("sp", "ld_c0", tril[:], consts["c_tril"], writes=[TRIL])
        c.dma("sp", "ld_c1", wt[:], consts["c_wt"], writes=[WT])
        c.dma("sp", "ld_c2", ckm[:].rearrange("p t h -> p (t h)"), sc.ckm, writes=[CKM])
        c.dma("sp", "ld_c3", cend[:], sc.cend.rearrange("h q -> (h q)").partition_broadcast(128), writes=[CEND])
        wo_v = wo_d.rearrange("(cc p) d -> p cc d", p=128)
        c.dma("pool", "ld_w0", wo[:], wo_v, writes=[WO])
        for j, dd in enumerate([lamq1_d, lamk1_d, lamq2_d, lamk2_d]):
            c.dma("sp", f"ld_lv{j}", lv[j][:], dd.partition_broadcast(64), writes=[LV[j]])
        c.dma("sp", "ld_c4", sm[:, 5:6], gsub_d.rearrange("(p o) -> p o", o=1), writes=[SM])
        for j in range(2):
            c.op("dve", lambda e: e.scalar_tensor_tensor(out=lj[:], in0=lv[2 * j][:], scalar=1.0, in1=lv[2 * j + 1][:],
                                                         op0=ALU.mult, op1=ALU.mult, accum_out=sm[:, j:j + 1]),
                 reads=[LV[2 * j], LV[2 * j + 1]], writes=[LJ, SM])
        c.op("act", lambda e: e.activation(out=sm[:, 2:4], in_=sm[:, 0:2], func=AF.Exp), reads=[], writes=[SM])
        c.op("dve", lambda e: e.tensor_tensor(out=sm[:, 4:5], in0=sm[:, 2:3], in1=sm[:, 3:4], op=ALU.subtract),
             reads=[], writes=[SM])
        c.op("dve", lambda e: e.tensor_scalar(out=sm[:, 4:5], in0=sm[:, 4:5], scalar1=float(lam_init), scalar2=-1.0,
                                              op0=ALU.add, op1=ALU.mult), reads=[], writes=[SM])
        c.op("dve", lambda e: e.tensor_scalar(out=sm[:, 5:6], in0=sm[:, 5:6], scalar1=float(1.0 - lam_init),
                                              scalar2=None, op0=ALU.mult), reads=[], writes=[SM])
        for i in range(2):
            c.op("pool", lambda e: e.memset(va[i][:, :, 64:128], 1.0), writes=[VA[i]])
        for i in range(4):
            c.op("pool" if i % 2 == 0 else "dve", lambda e: e.memset(qz[i][:], 0.0), writes=[QZ[i]])
        pm = g.psT[:].bitcast(F32)
        PM = g.PST

        vsrc = sc.v.rearrange("(kb p) c -> p kb c", p=128)
        state = {"sb": 0, "pt": 0, "ob": 0, "rec": 0, "bq": 0}
        SCALE_D = 32 ** -0.5

        heads = []
        for h in range(6):
            cd, hl = h // 2, h % 2
            ksl = cd % 2

            def loads(h=h, cd=cd, hl=hl, ksl=ksl):
                if hl == 0:
                    c.dma("sp", f"ld_ks{ksl}", ks[ksl][:], sc.kTd[cd * 128:(cd + 1) * 128, :], writes=[KS[ksl]])
                for pz in (2 * hl, 2 * hl + 1):
                    r = cd * 128 + pz * 32
                    c.dma("sp", f"ld_qz{pz}", qz[pz][pz * 32:(pz + 1) * 32, :], sc.qTd[r:r + 32, :], writes=[QZ[pz]])
                c.dma("sp", f"ld_va{h % 2}", va[h % 2][:, :, 0:64], vsrc[:, :, h * 64:(h + 1) * 64], writes=[VA[h % 2]])
            heads.append(dict(kind="diff", hh=h, vs=h % 2, loads=loads, K=128, scale=SCALE_D, mask="causal",
                              units=[(qz[2 * hl], QZ[2 * hl], ks[ksl], KS[ksl]),
                                     (qz[2 * hl + 1], QZ[2 * hl + 1], ks[ksl], KS[ksl])]))
        for h in range(6):
            sl = (6 + h) % 2
            ksl = (3 + h) % 2

            def loads(h=h, sl=sl, ksl=ksl):
                c.dma("sp", f"ld_qs{sl}", qs[sl][0:65, :], sc.qTf[h], writes=[QS[sl]])
                c.dma("sp", f"ld_ks{ksl}", ks[ksl][0:64, :], sc.kTf[h], writes=[KS[ksl]])
                c.op("pool", lambda e: e.memset(ks[ksl][64:65, :], 1.0), writes=[KS[ksl]])
                c.dma("sp", f"ld_va{sl}", va[sl][:, :, 0:64], vsrc[:, :, (6 + h) * 64:(7 + h) * 64], writes=[VA[sl]])
            heads.append(dict(kind="fox", hh=6 + h, fh=h, vs=sl, loads=loads, K=65, scale=1.0, mask="causal",
                              units=[(qs[sl], QS[sl], ks[ksl], KS[ksl])]))
        for h in range(4):
            cc, hl = h // 2, h % 2
            ksl = (9 + cc) % 2
            zi = 0 if hl == 0 else 3
            sl = (12 + h) % 2

            def loads(h=h, cc=cc, hl=hl, ksl=ksl, zi=zi, sl=sl):
                if hl == 0:
                    c.dma("sp", f"ld_ks{ksl}", ks[ksl][:], sc.kTc[cc * 128:(cc + 1) * 128, :], writes=[KS[ksl]])
                r = cc * 128 + hl * 64
                c.dma("sp", f"ld_qz{zi}", qz[zi][hl * 64:(hl + 1) * 64, :], sc.qTc[r:r + 64, :], writes=[QZ[zi]])
                c.dma("sp", f"ld_va{sl}", va[sl][:, :, 0:64], vsrc[:, :, (12 + h) * 64:(13 + h) * 64], writes=[VA[sl]])
            heads.append(dict(kind="dil", hh=12 + h, vs=sl, loads=loads, K=128, scale=0.125, mask="dil",
                              units=[(qz[zi], QZ[zi], ks[ksl], KS[ksl])]))

        items = []
        for hi, hd in enumerate(heads):
            for qb in range(NB):
                for ui, un in enumerate(hd["units"]):
                    kb_lo = max(0, 4 * qb - 16) if hd["kind"] == "dil" else 0
                    kbs = list(range(kb_lo, 4 * qb + 4))
                    grp_ = []
                    kk = 0
                    while kk < len(kbs):
                        if hd["kind"] != "fox" and ATT_PAIR and kbs[kk] + 1 < 4 * qb and kk + 1 < len(kbs):
                            grp_.append([kbs[kk], kbs[kk + 1]])
                            kk += 2
                        else:
                            grp_.append([kbs[kk]])
                            kk += 1
                    for n_, kbl in enumerate(grp_):
                        items.append(dict(hi=hi, hd=hd, qb=qb, ui=ui, un=un, kbl=kbl, kb=kbl[0], first=(n_ == 0),
                                          last=(n_ == len(grp_) - 1),
                                          head_start=(qb == 0 and ui == 0 and n_ == 0)))

        deferred = []

        def emit_S(it):
            hd = it["hd"]
            qt, QTB, kt, KTB = it["un"]
            qb, kb = it["qb"], it["kb"]
            q0 = qb * 512
            if it["first"] and hd["kind"] == "fox":
                bi = state["bq"] % 2
                state["bq"] += 1
                nk = 4 * qb + 4
                fh = hd["fh"]
                c.op("dve", lambda e: e.tensor_scalar(out=bq[bi][:, 0:nk], in0=ckm[:, 0:nk, fh],
                                                      scalar1=cend[:, fh * NB + qb:fh * NB + qb + 1], scalar2=None,
                                                      op0=ALU.subtract), reads=[CKM, CEND], writes=[BQ[bi]])
                hd["bi"] = bi
            if hd["kind"] == "fox":
                it["bi"] = hd["bi"]
            K = hd["K"]
            if len(it["kbl"]) == 2:
                if state["sb"] % 2 == 1:
                    state["sb"] += 1
                ti = (state["sb"] % 4) // 2
                state["sb"] += 2
                ptile = g.pp[ti]
                c.group("pe", [
                    (lambda e, m=m, kbm=kbm: e.matmul(ptile[:, m * 512:(m + 1) * 512],
                                                      kt[0:K, kbm * 128:(kbm + 1) * 128],
                                                      qt[0:K, q0:q0 + 512], start=True, stop=True))
                    for m, kbm in enumerate(it["kbl"])], reads=[QTB, KTB], writes=[g.PS[2 * ti], g.PS[2 * ti + 1]])
                it["S"] = (-1, 0, ptile, [g.PS[2 * ti], g.PS[2 * ti + 1]])
                return
            j = kb - 4 * qb
            col0 = max(0, j) * 128
            si = state["sb"] % 4
            state["sb"] += 1
            sbk, SBK = g.ps[si], g.PS[si]
            if hd["mask"] == "causal" and j >= 0:
                c.group("pe", [
                    lambda e: e.matmul(sbk[:, col0:512], kt[0:K, kb * 128:(kb + 1) * 128],
                                       qt[0:K, q0 + col0:q0 + 512], start=True, stop=False),
                    lambda e: e.matmul(sbk[:, col0:col0 + 128], g.ident[:], tril[:], start=False, stop=True),
                ], reads=[QTB, KTB, TRIL, g.IDENT], writes=[SBK])
            else:
                c.op("pe", lambda e: e.matmul(sbk[:, col0:512], kt[0:K, kb * 128:(kb + 1) * 128],
                                              qt[0:K, q0 + col0:q0 + 512], start=True, stop=True),
                     reads=[QTB, KTB], writes=[SBK])
            it["S"] = (j, col0, sbk, [SBK])

        def plain_epilogue(obk, OBK, hh, qb, use_act=False):
            q0 = qb * 512
            ri = state["rec"] % 2
            state["rec"] += 1
            pb = (hh % 2) * 64
            if use_act:
                c.op("act", lambda e: e.activation(out=rec[ri][:], in_=obk[64:128, :], func=AF.Ln),
                     reads=[OBK], writes=[REC[ri]])
                c.op("act", lambda e: e.activation(out=rec[ri][:], in_=rec[ri][:], func=AF.Exp, scale=-1.0),
                     reads=[], writes=[REC[ri]])
            else:
                c.op("dve", lambda e: e.reciprocal(out=rec[ri][:], in_=obk[64:128, :]), reads=[OBK],
                     writes=[REC[ri]])
            c.op("dve", lambda e: e.tensor_tensor(out=oT[pb:pb + 64, hh // 2, q0:q0 + 512], in0=obk[0:64, :],
                                                  in1=rec[ri][:], op=ALU.mult), reads=[OBK, REC[ri]], writes=[OT])

        def diff_stage1(oa, OA, o2, O2, par):
            td, TD, tsq, TSQ = td2[par], TD2[par], tsq2[par], TSQ2[par]
            c.op("dve", lambda e: e.reciprocal(out=rec[0][:], in_=oa[64:128, :]), reads=[OA], writes=[REC[0]])
            c.op("dve", lambda e: e.tensor_tensor(out=ta[:], in0=oa[0:64, :], in1=rec[0][:], op=ALU.mult),
                 reads=[OA, REC[0]], writes=[TA])
            c.op("dve", lambda e: e.reciprocal(out=rec[1][:], in_=o2[64:128, :]), reads=[O2], writes=[REC[1]])
            c.op("dve", lambda e: e.tensor_tensor(out=tb[:], in0=o2[0:64, :], in1=rec[1][:], op=ALU.mult),
                 reads=[O2, REC[1]], writes=[TB])
            c.op("dve", lambda e: e.scalar_tensor_tensor(out=td[:], in0=tb[:], scalar=sm[:, 4:5], in1=ta[:],
                                                         op0=ALU.mult, op1=ALU.add), reads=[TA, TB, SM], writes=[TD])
            c.op("pool", lambda e: e.tensor_tensor(out=tsq[:], in0=td[:], in1=td[:], op=ALU.mult),
                 reads=[TD], writes=[TSQ])

        def diff_stage2(h, qb, par):
            td, TD, tsq, TSQ, tr, TR = td2[par], TD2[par], tsq2[par], TSQ2[par], tr2[par], TR2[par]
            q0 = qb * 512
            pb = (h % 2) * 64
            c.op("pe", lambda e: e.matmul(pm[0:64, :], g.ones64b[0:64, 0:64], tsq[:], start=True, stop=True),
                 reads=[TSQ, g.ONES64], writes=[PM])
            c.op("act", lambda e: e.activation(out=tr[:], in_=pm[0:64, :], func=AF.Ln, scale=1.0,
                                               bias=g.eps_sub[0:64, 0:1]), reads=[PM], writes=[TR])
            c.op("act", lambda e: e.activation(out=tr[:], in_=tr[:], func=AF.Exp, scale=-0.5), reads=[], writes=[TR])
            c.op("dve", lambda e: e.scalar_tensor_tensor(out=oT[pb:pb + 64, h // 2, q0:q0 + 512], in0=td[:],
                                                         scalar=sm[:, 5:6], in1=tr[:], op0=ALU.mult, op1=ALU.mult),
                 reads=[TD, TR, SM], writes=[OT])

        def emit_PV(i):
            it = items[i]
            hd = it["hd"]
            qb = it["qb"]
            p, P, col0 = it["P"]
            if it["first"]:
                obi = 4 + state["ob"] % 3
                state["ob"] += 1
                hd["cur_o"] = (g.ps[obi], g.PS[obi])
            obk, OBK = hd["cur_o"]
            vs = hd["vs"]
            nk = len(it["kbl"])
            if nk == 2:
                c.group("pe", [
                    (lambda e, m=m, kbm=kbm: e.matmul(obk[:, 0:512], va[vs][:, kbm, :], p[:, m * 512:(m + 1) * 512],
                                                      start=(it["first"] and m == 0), stop=(it["last"] and m == 1)))
                    for m, kbm in enumerate(it["kbl"])], reads=[P, VA[vs]], writes=[OBK])
            else:
                kb = it["kb"]
                c.op("pe", lambda e: e.matmul(obk[:, col0:512], va[vs][:, kb, :], p[:, col0:512],
                                              start=it["first"], stop=it["last"]), reads=[P, VA[vs]], writes=[OBK])
            if it["last"]:
                if hd["kind"] == "diff":
                    if it["ui"] == 0:
                        pend_diff[(it["hi"], qb)] = (obk, OBK)
                    else:
                        oa, OA = pend_diff.pop((it["hi"], qb))
                        par = npair[0] % 2
                        npair[0] += 1
                        while len(deferred) > 1:
                            deferred.pop(0)[1]()
                        diff_stage1(oa, OA, obk, OBK, par)
                        deferred.append((i + DEFER, (lambda h=hd["hh"], qb=qb, par=par: diff_stage2(h, qb, par))))
                elif hd["kind"] == "dil":
                    deferred.append((i + DIL_DEFER, (lambda obk=obk, OBK=OBK, hh=hd["hh"], qb=qb:
                                                     plain_epilogue(obk, OBK, hh, qb, use_act=True))))
                else:
                    plain_epilogue(obk, OBK, hd["hh"], qb)

        heads[0]["loads"]()
        heads[1]["loads"]()
        LAG = ATT_LAG
        PRE = ATT_PRE
        DEFER = 22
        npair = [0]
        for i in range(min(PRE, len(items))):
            emit_S(items[i])
        pend_diff = {}
        for i, it in enumerate(items):
            hd = it["hd"]
            if i >= LAG and items[i - LAG]["head_start"]:
                nh = items[i - LAG]["hi"] + 1
                if nh >= 2 and nh < len(heads):
                    heads[nh]["loads"]()
            j, col0, sbk, SBKS = it["S"]
            qb, kb = it["qb"], it["kb"]
            q0 = qb * 512
            pi = state["pt"] % NPT
            state["pt"] += 1
            p, P = pt[pi], PT[pi]
            w1 = 1024 if len(it["kbl"]) == 2 else 512
            if hd["kind"] == "fox":
                bi = it["bi"]
                c.op("act", lambda e: e.activation(out=p[:, col0:512], in_=sbk[:, col0:512], func=AF.Exp,
                                                   scale=hd["scale"], bias=bq[bi][:, kb:kb + 1]),
                     reads=SBKS + [BQ[bi]], writes=[P])
            else:
                c.op("act", lambda e: e.activation(out=p[:, col0:w1], in_=sbk[:, col0:w1], func=AF.Exp,
                                                   scale=hd["scale"]), reads=SBKS, writes=[P])
            if hd["mask"] != "causal":
                for m, kbm in enumerate(it["kbl"]):
                    off = q0 - kbm * 128 + col0
                    c.op("pool" if i % DIL_POOL_EVERY == DIL_POOL_EVERY - 1 else "dve",
                         lambda e: e.tensor_tensor(out=p[:, m * 512 + col0:(m + 1) * 512],
                                                   in0=p[:, m * 512 + col0:(m + 1) * 512],
                                                   in1=wt[:, off:off + 512 - col0], op=ALU.mult),
                         reads=[WT], writes=[P])
            if i + PRE < len(items):
                emit_S(items[i + PRE])
            it["P"] = (p, P, col0)
            if i >= LAG:
                emit_PV(i - LAG)
            while deferred and deferred[0][0] <= i:
                deferred.pop(0)[1]()
        for i2 in range(max(0, len(items) - LAG), len(items)):
            emit_PV(i2)
        while deferred:
            deferred.pop(0)[1]()

        if dbg is not None:
            for cc in range(8):
                c.dma("sp", "st_dbg", dbg[cc * 128:(cc + 1) * 128, :], oT[:, cc, :], reads=[OT])

        groups = [(i, tt, dh) for i in range(NB) for tt in range(4) for dh in range(2)]
        NXR = 4
        xr = xr + [A("xr3", [128, 512], F32)]
        XR = XR + [Buf("xr3")]

        def ld_x(n):
            i, tt, dh = groups[n]
            r0 = (i * 4 + tt) * 128
            c.dma("sp", f"ld_xr{n % NXR}", xr[n % NXR][:], x_src[r0:r0 + 128, dh * 512:(dh + 1) * 512],
                  writes=[XR[n % NXR]])

        ld_x(0)
        ld_x(1)
        for n, (i, tt, dh) in enumerate(groups):
            if n + 2 < len(groups):
                ld_x(n + 2)
            py, PY = g.ps[n % 2], g.PS[n % 2]
            r0 = (i * 4 + tt) * 128
            xs = n % NXR
            c.group("pe", [
                (lambda e, cc=cc: e.matmul(py[:], oT[:, cc, r0:r0 + 128], wo[:, cc, dh * 512:(dh + 1) * 512],
                                           start=(cc == 0), stop=(cc == 7)))
                for cc in range(8)], reads=[OT, WO], writes=[PY])
            c.op("dve", lambda e: e.tensor_tensor(out=xr[xs][:], in0=py[:], in1=xr[xs][:], op=ALU.add),
                 reads=[PY], writes=[XR[xs]])
            c.dma("sp", f"st_xr{xs}", x_dst[r0:r0 + 128, dh * 512:(dh + 1) * 512], xr[xs][:], reads=[XR[xs]])
        c.barrier()


def setup_globals(c, nc, consts):
    g = G()
    g.pp = [nc.alloc_psum_tensor(f"pp{i}", [128, 1024], F32) for i in range(2)]
    g.ps = [g.pp[0][:, 0:512], g.pp[0][:, 512:1024], g.pp[1][:, 0:512], g.pp[1][:, 512:1024]]
    g.ps += [nc.alloc_psum_tensor(f"ps{i}", [128, 512], F32) for i in range(4, 7)]
    g.PS = [Buf(f"ps{i}", excl=True) for i in range(7)]
    g.psT = nc.alloc_psum_tensor("psT", [128, 1024], BF16)
    g.PST = Buf("psT", excl=True)
    g.ident = nc.alloc_sbuf_tensor("ident", [128, 128], BF16)
    g.IDENT = Buf("ident")
    g.identf = nc.alloc_sbuf_tensor("identf", [128, 128], F32)
    g.IDENTF = Buf("identf")
    g.ones64 = nc.alloc_sbuf_tensor("ones64", [64, 64], F32)
    g.ones64b = nc.alloc_sbuf_tensor("ones64b", [64, 64], BF16)
    g.ONES64 = Buf("ones64")
    g.eps_norm = nc.alloc_sbuf_tensor("eps_norm", [128, 1], F32)
    g.eps_sub = nc.alloc_sbuf_tensor("eps_sub", [128, 1], F32)
    g.one_c = nc.alloc_sbuf_tensor("one_c", [128, 1], F32)
    g.EPS = Buf("eps")
    c.dma("sp", "ld_c0", g.ident[:], consts["c_ident"], writes=[g.IDENT])
    c.dma("sp", "ld_c1", g.identf[:], consts["c_identf"], writes=[g.IDENTF])
    c.op("dve", lambda e: e.memset(g.eps_norm[:], NORM_EPS), writes=[g.EPS])
    c.op("dve", lambda e: e.memset(g.eps_sub[:], SUBLN_EPS), writes=[g.EPS])
    c.op("dve", lambda e: e.memset(g.one_c[:], 1.0), writes=[g.EPS])
    c.op("dve", lambda e: e.memset(g.ones64[:], 1.0 / 64.0), writes=[g.ONES64])
    c.op("dve", lambda e: e.memset(g.ones64b[:], 1.0 / 64.0), writes=[g.ONES64])
    return g


W_NAMES = ["w_in", "b_f", "lam_q1", "lam_k1", "lam_q2", "lam_k2", "g_sub", "w_o", "g_ffn1", "w1_gate", "w1_up",
           "w1_down", "g_mix", "g_ffn2", "w2_gate", "w2_up", "w2_down", "g_final"]
W_SHAPES = {
    "w_in": [DEPTH, D, INW], "b_f": [DEPTH, 6], "lam_q1": [DEPTH, 32], "lam_k1": [DEPTH, 32], "lam_q2": [DEPTH, 32],
    "lam_k2": [DEPTH, 32], "g_sub": [DEPTH, 64], "w_o": [DEPTH, D, D], "g_ffn1": [DEPTH, D],
    "w1_gate": [DEPTH, D, DFF], "w1_up": [DEPTH, D, DFF], "w1_down": [DEPTH, DFF, D], "g_mix": [DEPTH, D],
    "g_ffn2": [DEPTH, D], "w2_gate": [DEPTH, D, DFF], "w2_up": [DEPTH, D, DFF], "w2_down": [DEPTH, DFF, D],
    "g_final": [D],
}


def const_shapes(S):
    return {"c_ident": ([128, 128], BF16), "c_identf": ([128, 128], F32), "c_tril": ([128, 128], BF16),
            "c_wt": ([128, 2688], BF16), "c_cosA": ([128, S], F32), "c_sinA": ([128, S], F32),
            "c_cosC": ([128, S], F32), "c_sinC": ([128, S], F32)}


def make_consts(S):
    bf = ml_dtypes.bfloat16
    cst = {}
    cst["c_ident"] = np.eye(128, dtype=np.float32).astype(bf)
    cst["c_identf"] = np.eye(128, dtype=np.float32)
    kl = np.arange(128)[:, None]
    ql = np.arange(128)[None, :]
    cst["c_tril"] = np.where(ql >= kl, 0.0, -30000.0).astype(np.float32).astype(bf)
    xx = np.arange(2688)[None, :]
    dl = xx - kl
    wmask = ((dl >= 0) & (dl <= 128)).astype(np.float32) + ((dl >= 0) & (dl <= 512) & (dl % 4 == 0)) \
        + ((dl >= 0) & (dl <= 2048) & (dl % 16 == 0))
    cst["c_wt"] = wmask.astype(np.float32).astype(bf)
    pos = np.arange(S, dtype=np.float32)
    for nm, half in (("A", 16), ("C", 32)):
        inv = (np.float32(10000.0) ** (-(np.arange(half, dtype=np.float32) / np.float32(half)))).astype(np.float32)
        ang = (pos[None, :] * inv[:, None]).astype(np.float32)
        rows = np.arange(128) % half
        a = ang[rows].astype(np.float64)
        cst["c_cos" + nm] = np.cos(a).astype(np.float32)
        cst["c_sin" + nm] = np.sin(a).astype(np.float32)
    return cst


def build_nc(S, layers, final, phases=("ffn1", "proj", "attn", "ffn2"), dbg=False):
    nc = bass.Bass("TRN2", target_bir_lowering=False)
    c = Ctx(nc)
    dt = lambda name, shape, dtype=F32: nc.dram_tensor(name, shape, dtype, kind="ExternalInput").ap()
    x_in = dt("x", [S, D])
    w = {nm: dt(nm, W_SHAPES[nm]) for nm in W_NAMES}
    consts = {nm: dt(nm, shp, dty) for nm, (shp, dty) in const_shapes(S).items()}
    out = nc.dram_tensor("out", [S, D], F32, kind="ExternalOutput").ap()
    dbg_ap = nc.dram_tensor("dbg", [1024, S], BF16, kind="ExternalOutput").ap() if dbg else None
    sc = alloc_scratch(nc, S)
    g = setup_globals(c, nc, consts)
    c.barrier()
    first = True

    def srcs():
        return x_in if first else out

    pending = None
    for idx, l in enumerate(layers):
        lam_init = 0.8 - 0.6 * math.exp(-0.3 * l)
        if "ffn1" in phases:
            if pending is not None:
                pst, sh = pending
                phase_ffn(c, g, w["w1_gate"][l], w["w1_up"][l], w["w1_down"][l], w["g_ffn1"][l], srcs(), None, out,
                          None, S, f"L{l}a_", shared=sh)
                pst.close()
                pending = None
            else:
                phase_ffn(c, g, w["w1_gate"][l], w["w1_up"][l], w["w1_down"][l], w["g_ffn1"][l], srcs(), None, out,
                          None, S, f"L{l}a_")
            first = False
        if "proj" in phases:
            phase_proj(c, g, sc, w["w_in"][l], w["g_mix"][l], w["b_f"][l], consts, srcs(), S, f"L{l}p_")
        if "attn" in phases:
            phase_attn(c, g, sc, w["w_o"][l], w["lam_q1"][l], w["lam_k1"][l], w["lam_q2"][l], w["lam_k2"][l],
                       w["g_sub"][l], lam_init, consts, srcs(), out, S, f"L{l}m_", dbg=dbg_ap)
            first = False
        if "ffn2" in phases:
            nxt = layers[idx + 1] if idx + 1 < len(layers) else None
            if nxt is not None and "ffn1" in phases and FFN_PREFETCH:
                pst = ExitStack()
                sh = alloc_ffn_weights(nc, pst, f"L{l}s_")
                phase_ffn(c, g, w["w2_gate"][l], w["w2_up"][l], w["w2_down"][l], w["g_ffn2"][l], srcs(), None, out,
                          None, S, f"L{l}c_", shared=sh,
                          prefetch=(w["w1_gate"][nxt], w["w1_up"][nxt], w["w1_down"][nxt]))
                pending = (pst, sh)
            else:
                phase_ffn(c, g, w["w2_gate"][l], w["w2_up"][l], w["w2_down"][l], w["g_ffn2"][l], srcs(), None, out,
                          None, S, f"L{l}c_")
            first = False
    if final:
        phase_final(c, g, w["g_final"], srcs(), None, out, None, S, "fin_")
    c.barrier()
    nc._ctx_ninst = c.ninst
    return nc


def kernel(**inputs):
    x = np.ascontiguousarray(inputs["x"], dtype=np.float32)
    B = x.shape[0]
    nc = build_nc(SEQ, list(range(DEPTH)), True)
    shared = {k: np.ascontiguousarray(inputs[k], dtype=np.float32) for k in W_NAMES}
    shared.update(make_consts(SEQ))
    in_maps = [dict(shared, x=x[b]) for b in range(B)]
    res = run_bass_kernel_spmd(nc, in_maps, core_ids=list(range(B)))
    return np.stack([r["out"] for r in res.results], axis=0)
```
